# Optimizing a Trainium2 kernel written in Bass

```python
import jax, jax.numpy as jnp
from jax import lax
import numpy as np

D_MODEL = 2048
BATCH = 2
SEQ = 8192
DEPTH = 1

MIX_WIDTH = D_MODEL
HEAD_DIM = 128
POOL_WINDOWS = (2, 4, 8, 16)
N_POOL_GROUPS = len(POOL_WINDOWS)
POOL_WIDTH = MIX_WIDTH // 4
POOL_GROUP_DIM = POOL_WIDTH // N_POOL_GROUPS
ATTN_WIDTH = MIX_WIDTH - POOL_WIDTH
N_ATTN_HEADS = ATTN_WIDTH // HEAD_DIM
DILATION_PAIRS = ((128, 1), (512, 4), (2048, 16))
N_DIL_GROUPS = len(DILATION_PAIRS)
HEADS_PER_GROUP = N_ATTN_HEADS // N_DIL_GROUPS
BLOCK = 128
D_FF = ((8 * D_MODEL // 3 + 255) // 256) * 256
D_PLE = 256
NUM_BUCKETS = 32
MAX_EXACT = NUM_BUCKETS // 2
MAX_DISTANCE = 2048
EPS = 1e-6
NEG_INF = -1e30
IN_PROJ_WIDTH = POOL_WIDTH + 3 * ATTN_WIDTH

kernel_name = "hymba_pool_dilated_attn_swiglu_ple"


def rms_norm(x, g):
    xf = x.astype(jnp.float32)
    y = xf * lax.rsqrt(jnp.mean(xf * xf, axis=-1, keepdims=True) + EPS)
    return (y * g.astype(jnp.float32)).astype(x.dtype)


def t5_causal_bucket(distance):
    n = jnp.maximum(distance, 0)
    nf = jnp.maximum(n, 1).astype(jnp.float32)
    large = MAX_EXACT + (jnp.log(nf / MAX_EXACT) / np.float32(np.log(MAX_DISTANCE / MAX_EXACT))
                         * (NUM_BUCKETS - MAX_EXACT)).astype(jnp.int32)
    large = jnp.minimum(large, NUM_BUCKETS - 1)
    return jnp.where(n < MAX_EXACT, n, large)


def pool_mixer(u, pool_w, pool_scale):
    B, S, _ = u.shape
    uf = u.astype(jnp.float32)
    cs = jnp.pad(jnp.cumsum(uf, axis=1), ((0, 0), (1, 0), (0, 0)))
    pos = jnp.arange(1, S + 1, dtype=jnp.int32)
    outs = []
    for gi, w in enumerate(POOL_WINDOWS):
        sl = slice(gi * POOL_GROUP_DIM, (gi + 1) * POOL_GROUP_DIM)
        c = cs[..., sl]
        prev = jnp.pad(c, ((0, 0), (w, 0), (0, 0)))[:, 1:S + 1]
        cnt = jnp.minimum(pos, w).astype(jnp.float32)[None, :, None]
        outs.append((c[:, 1:] - prev) / cnt - uf[..., sl])
    pooled = jnp.stack(outs, axis=2)
    mixed = jnp.einsum('bsgc,gcd->bsgd', pooled, pool_w.astype(jnp.float32))
    return (mixed.reshape(B, S, POOL_WIDTH) * pool_scale.astype(jnp.float32)).astype(u.dtype)


def dilated_window_attention(q, k, v, bias_table, window, dilation):
    B, S, H, Dh = q.shape
    d = dilation
    steps = window // d
    L = S // d
    nb = -(-L // BLOCK)
    Lp = nb * BLOCK

    def to_sub(t):
        t = t.reshape(B, L, d, H, Dh).astype(jnp.float32)
        return jnp.pad(t, ((0, 0), (0, Lp - L), (0, 0), (0, 0), (0, 0)))

    def band_keys(t):
        tp = jnp.pad(to_sub(t), ((0, 0), (BLOCK, 0), (0, 0), (0, 0), (0, 0)))
        prev = tp[:, :Lp].reshape(B, nb, BLOCK, d, H, Dh)
        cur = tp[:, BLOCK:].reshape(B, nb, BLOCK, d, H, Dh)
        return jnp.concatenate([prev, cur], axis=2)

    qs = to_sub(q).reshape(B, nb, BLOCK, d, H, Dh)
    kb = band_keys(k)
    vb = band_keys(v)
    scores = jnp.einsum('bnqrhe,bnkrhe->bnrhqk', qs, kb) * np.float32(Dh ** -0.5)

    qi = jnp.arange(BLOCK, dtype=jnp.int32)[:, None]
    ki = jnp.arange(2 * BLOCK, dtype=jnp.int32)[None, :]
    offset = qi + BLOCK - ki
    valid_off = (offset >= 0) & (offset <= steps)
    key_ok = (jnp.arange(nb)[:, None] > 0) | (ki >= BLOCK)
    mask = valid_off[None] & key_ok[:, None, :]
    bias = jnp.transpose(bias_table.astype(jnp.float32)[t5_causal_bucket(offset * d)], (2, 0, 1))

    logits = jnp.where(mask[None, :, None, None], scores + bias[None, None, None], NEG_INF)
    m = jnp.max(logits, axis=-1, keepdims=True)
    pexp = jnp.exp(logits - m)
    den = jnp.sum(pexp, axis=-1, keepdims=True)
    out = jnp.einsum('bnrhqk,bnkrhe->bnrhqe', pexp, vb) / den
    lse = (m + jnp.log(den))[..., 0]

    out = jnp.transpose(out, (0, 1, 4, 2, 3, 5)).reshape(B, Lp, d, H, Dh)[:, :L]
    lse = jnp.transpose(lse, (0, 1, 4, 2, 3)).reshape(B, Lp, d, H)[:, :L]
    return out.reshape(B, S, H, Dh), lse.reshape(B, S, H)


def dilated_mixture_attention(q, k, v, rel_bias):
    B, S, _ = q.shape
    q = q.reshape(B, S, N_ATTN_HEADS, HEAD_DIM)
    k = k.reshape(B, S, N_ATTN_HEADS, HEAD_DIM)
    v = v.reshape(B, S, N_ATTN_HEADS, HEAD_DIM)
    outs, lses = [], []
    for gi, (window, dil) in enumerate(DILATION_PAIRS):
        hs = slice(gi * HEADS_PER_GROUP, (gi + 1) * HEADS_PER_GROUP)
        o, l = dilated_window_attention(q[:, :, hs], k[:, :, hs], v[:, :, hs], rel_bias[:, hs], window, dil)
        outs.append(o)
        lses.append(l)
    outs = jnp.stack(outs, axis=2)
    alpha = jax.nn.softmax(jnp.stack(lses, axis=2), axis=2)
    return (outs * alpha[..., None]).reshape(B, S, ATTN_WIDTH).astype(q.dtype)


def setup_inputs(seed: int = 0) -> dict:
    key = jax.random.key(seed)
    ks = jax.random.split(key, 16)
    f32 = jnp.float32
    nrm = lambda k, shape, fan_in: jax.random.normal(k, shape, f32) * (fan_in ** -0.5)
    gain = lambda k, shape: 1.0 + 0.05 * jax.random.normal(k, shape, f32)
    return {
        "x": jax.random.normal(ks[0], (BATCH, SEQ, D_MODEL), f32),
        "p": jax.random.normal(ks[1], (DEPTH, BATCH, SEQ, D_PLE), f32),
        "rel_bias": 0.5 * jax.random.normal(ks[2], (NUM_BUCKETS, N_ATTN_HEADS), f32),
        "norm_mix_g": gain(ks[3], (DEPTH, D_MODEL)),
        "w_in": nrm(ks[4], (DEPTH, D_MODEL, IN_PROJ_WIDTH), D_MODEL),
        "pool_w": nrm(ks[5], (DEPTH, N_POOL_GROUPS, POOL_GROUP_DIM, POOL_GROUP_DIM), POOL_GROUP_DIM),
        "pool_scale": gain(ks[6], (DEPTH, POOL_WIDTH)),
        "w_out": nrm(ks[7], (DEPTH, MIX_WIDTH, D_MODEL), MIX_WIDTH),
        "norm_ffn_g": gain(ks[8], (DEPTH, D_MODEL)),
        "w_gate": nrm(ks[9], (DEPTH, D_MODEL, D_FF), D_MODEL),
        "w_up": nrm(ks[10], (DEPTH, D_MODEL, D_FF), D_MODEL),
        "w_down": nrm(ks[11], (DEPTH, D_FF, D_MODEL), D_FF),
        "norm_ple_g": gain(ks[12], (DEPTH, D_MODEL)),
        "w_ple_gate": nrm(ks[13], (DEPTH, D_MODEL, D_MODEL), D_MODEL),
        "w_ple_proj": nrm(ks[14], (DEPTH, D_PLE, D_MODEL), D_PLE),
        "final_norm_g": gain(ks[15], (D_MODEL,)),
    }


def reference(x, p, rel_bias, norm_mix_g, w_in, pool_w, pool_scale, w_out,
              norm_ffn_g, w_gate, w_up, w_down, norm_ple_g, w_ple_gate, w_ple_proj,
              final_norm_g):
    h = x
    for i in range(DEPTH):
        xn = rms_norm(h, norm_mix_g[i])
        proj = xn @ w_in[i]
        u_pool = proj[..., :POOL_WIDTH]
        q = proj[..., POOL_WIDTH:POOL_WIDTH + ATTN_WIDTH]
        k = proj[..., POOL_WIDTH + ATTN_WIDTH:POOL_WIDTH + 2 * ATTN_WIDTH]
        v = proj[..., POOL_WIDTH + 2 * ATTN_WIDTH:]
        y_pool = pool_mixer(u_pool, pool_w[i], pool_scale[i])
        y_attn = dilated_mixture_attention(q, k, v, rel_bias)
        h = h + jnp.concatenate([y_pool, y_attn], axis=-1) @ w_out[i]
        hn = rms_norm(h, norm_ffn_g[i])
        h = h + (jax.nn.silu(hn @ w_gate[i]) * (hn @ w_up[i])) @ w_down[i]
        gate = jax.nn.sigmoid(rms_norm(h, norm_ple_g[i]) @ w_ple_gate[i])
        h = h + gate * (p[i] @ w_ple_proj[i])
    return rms_norm(h, final_norm_g)
```

```python
import numpy as np
from contextlib import ExitStack
import concourse.bass as bass
import concourse.mybir as mybir
from concourse.bass_utils import run_bass_kernel_spmd

F32 = mybir.dt.float32
BF16 = mybir.dt.bfloat16
AF = mybir.ActivationFunctionType
ALU = mybir.AluOpType

COMPUTE = ("pe", "act", "dve", "pool")
EPOCH = 12000
NDMASEM = 20

NT = 2048
D = 2048
KC = 16
DFF = 5632
EPS = 1e-6
QSCALE = float(128 ** -0.5)
NEG = -30000.0
GD = (1, 4, 16)
GH = (128, 512, 2048)


class Tok:
    __slots__ = ("eng", "id", "sem", "val", "dma")

    def __init__(self, eng, id_, dma=False):
        self.eng = eng
        self.id = id_
        self.sem = None
        self.val = 0
        self.dma = dma


class Buf:
    __slots__ = ("name", "w", "r")

    def __init__(self, name=""):
        self.name = name
        self.w = None
        self.r = {}


def bufs(n):
    return [Buf() for _ in range(n)]


class K:
    def __init__(self, nc, stack, needed=None):
        self.nc = nc
        self.stack = stack
        self.needed = needed
        self.used = set()
        self.opid = 0
        self.engs = {"pe": nc.tensor, "act": nc.scalar, "dve": nc.vector,
                     "pool": nc.gpsimd, "sp": nc.sync}
        self.cur = {}
        self.nsem = 0
        for e in COMPUTE:
            self.cur[e] = [self._newsem(e), 0]
        self.waited = {e: {p: 0 for p in COMPUTE} for e in self.engs}
        self.waited_dma = {e: {} for e in self.engs}
        self.dsem = {}
        for q in ("sp", "pool"):
            self.dsem[q] = [[self._newsem(f"d{q}{i}"), 0] for i in range(NDMASEM)]
        self.dptr = {q: 0 for q in self.dsem}
        self.last = {e: None for e in COMPUTE}
        self.dma_toks = {}

    def _newsem(self, name):
        self.nsem += 1
        return self.stack.enter_context(self.nc.semaphore(f"s{self.nsem}_{name}"))

    def _wait(self, eng, t):
        if t is None:
            return
        if t.dma:
            key = id(t.sem)
            if self.waited_dma[eng].get(key, 0) >= t.val:
                return
            self.engs[eng].wait_ge(t.sem, t.val)
            self.waited_dma[eng][key] = t.val
            return
        if t.eng == eng and eng == "pe":
            return
        if self.waited[eng][t.eng] >= t.id:
            return
        self.used.add(t.id)
        self.waited[eng][t.eng] = t.id
        if t.sem is None:
            raise RuntimeError("dependency on an unsignalled op (two-pass mismatch)")
        self.engs[eng].wait_ge(t.sem, t.val)

    @staticmethod
    def _deps(reads, writes, extra):
        deps = []
        for b in reads:
            if b.w is not None:
                deps.append(b.w)
        for b in writes:
            if b.w is not None:
                deps.append(b.w)
            deps.extend(b.r.values())
        deps.extend(extra)
        return deps

    @staticmethod
    def _mark(tok, reads, writes, key):
        for b in reads:
            b.r[key] = tok
        for b in writes:
            b.w = tok
            b.r = {}

    def op(self, eng, fn, reads=(), writes=(), extra=()):
        for t in self._deps(reads, writes, extra):
            self._wait(eng, t)
        inst = fn()
        self.opid += 1
        tok = Tok(eng, self.opid)
        if self.needed is None or tok.id in self.needed:
            c = self.cur[eng]
            if c[1] >= EPOCH:
                c[0] = self._newsem(eng)
                c[1] = 0
            c[1] += 1
            inst.then_inc(c[0], 1)
            tok.sem = c[0]
            tok.val = c[1]
        self.last[eng] = tok
        self._mark(tok, reads, writes, eng)
        return tok

    def dma(self, q, out, in_, reads=(), writes=(), extra=(), **kw):
        for t in self._deps(reads, writes, extra):
            self._wait(q, t)
        ring = self.dsem[q]
        slot = ring[self.dptr[q] % NDMASEM]
        self.dptr[q] += 1
        sem, cnt = slot
        if cnt > 0:
            key = id(sem)
            if self.waited_dma[q].get(key, 0) < cnt:
                self.engs[q].wait_ge(sem, cnt)
                self.waited_dma[q][key] = cnt
        inst = self.engs[q].dma_start(out=out, in_=in_, **kw)
        slot[1] = cnt + 16
        inst.then_inc(sem, 16)
        self.opid += 1
        tok = Tok(q, self.opid, dma=True)
        tok.sem = sem
        tok.val = cnt + 16
        self._mark(tok, reads, writes, ("dma", tok.id))
        self.dma_toks[id(sem)] = tok
        return tok

    def all_tokens(self):
        return [t for t in self.last.values() if t is not None] + list(self.dma_toks.values())

    def barrier(self):
        toks = self.all_tokens()
        for e in ("pe", "act", "dve", "pool", "sp"):
            for t in toks:
                self._wait(e, t)


class Ring:
    def __init__(self, items):
        self.items = items
        self.i = 0

    def next(self):
        it = self.items[self.i % len(self.items)]
        self.i += 1
        return it


class WStream:
    def __init__(self, k, bufs_, specs, q="pool"):
        self.k = k
        self.q = q
        self.bufs = bufs_
        self.specs = specs
        self.loaded = 0
        self.i = 0

    def _load(self, j):
        flat, B = self.bufs[j % len(self.bufs)]
        for pi, (off, src, a, b) in enumerate(self.specs[j]):
            view = flat[:, off:off + a * b].rearrange("p (a b) -> p a b", a=a)
            self.k.dma(self.q, view, src, writes=[B[pi]])

    def prefetch(self):
        n = len(self.bufs)
        while self.loaded < min(len(self.specs), self.i + n):
            self._load(self.loaded)
            self.loaded += 1

    def get(self):
        n = len(self.bufs)
        while self.loaded < min(len(self.specs), self.i + n):
            self._load(self.loaded)
            self.loaded += 1
        flat, B = self.bufs[self.i % n]
        self.i += 1
        return flat, B


def v3(flat, off, a, b):
    return flat[:, off:off + a * b].rearrange("p (a b) -> p a b", a=a)


def build(needed=None, debug=False, stages=3):
    nc = bass.Bass("TRN2", target_bir_lowering=False)

    def din(name, shape):
        return nc.dram_tensor(name, shape, F32, kind="ExternalInput").ap()

    xh = din("xh", [2 * NT, D])
    pc = din("pc", [NT, 256])
    w_in = din("w_in", [D, 5120])
    w_out = din("w_out", [D, D])
    w_gate = din("w_gate", [D, DFF])
    w_up = din("w_up", [D, DFF])
    w_down = din("w_down", [DFF, D])
    w_pg = din("w_pg", [D, D])
    w_pp = din("w_pp", [256, D])
    poolw_d = din("poolw_in", [128, 4, 128])
    gvec_d = din("gvec_in", [128, 68])
    biasT_d = din("biasT", [128, 4, 3, 256])
    maskc_d = din("maskc_in", [128, 256])
    hneg_d = din("hneg_in", [128, 1])
    rcnt_d = din("rcnt_in", [128, 4, 16])
    ident_d = din("identf_in", [128, 128])
    gfinb_d = din("gfinb_in", [128, D])
    out_d = nc.dram_tensor("out", [NT, D], F32, kind="ExternalOutput").ap()
    skind = "ExternalOutput" if debug else "Internal"
    xT_d = nc.dram_tensor("xT_d", [KC, 128, NT], F32, kind=skind).ap()
    xnT_d = nc.dram_tensor("xnT_d", [KC, 128, 2 * NT], BF16, kind=skind).ap()
    mixT_d = nc.dram_tensor("mixT_d", [KC, 128, NT], BF16, kind=skind).ap()

    def dview(t3, c0, c1):
        return t3[:, :, c0:c1].rearrange("k p n -> p k n")

    with ExitStack() as st:
        k = K(nc, st, needed)

        def sb(stack, name, shape, dt):
            return stack.enter_context(nc.sbuf_tensor(name, shape, dt))

        identf = sb(st, "identf", [128, 128], F32)
        identb = sb(st, "identb", [128, 128], BF16)
        ones = sb(st, "ones", [128, 128], BF16)
        epst = sb(st, "epst", [128, 1], F32)
        gvec = sb(st, "gvec", [128, 68], F32)
        CB = Buf("const")
        k.dma("sp", identf[:], ident_d, writes=[CB])
        k.dma("sp", gvec[:], gvec_d, writes=[CB])
        k.dma("pool", identb[:], ident_d, writes=[CB])
        k.op("dve", lambda: nc.vector.memset(ones[:], 1.0), writes=[CB])
        k.op("dve", lambda: nc.vector.memset(epst[:], EPS), writes=[CB])

        banks = []
        for i in range(8):
            t = st.enter_context(nc.psum_tensor(f"bank{i}", [128, 512], F32))
            banks.append((t, Buf(f"bank{i}")))
        psum = Ring(banks)
        ev = [0]

        def evac_eng():
            ev[0] += 1
            return "act" if ev[0] % 2 else "dve"

        def copy_op(eng, out, in_, reads, writes, scale=None):
            if eng == "act":
                if scale is None:
                    return k.op("act", lambda: nc.scalar.copy(out, in_), reads=reads, writes=writes)
                return k.op("act", lambda: nc.scalar.mul(out, in_, scale), reads=reads, writes=writes)
            if scale is None:
                return k.op("dve", lambda: nc.vector.tensor_copy(out, in_), reads=reads, writes=writes)
            return k.op("dve", lambda: nc.vector.tensor_scalar(out, in_, scale, None, ALU.mult),
                        reads=reads, writes=writes)

        def norm_fm(src, srcB, gcol, dst, dstB, ncols, sq, sqB, rsr, cstart=0):
            for s in range(ncols // 512):
                c0 = cstart + s * 512
                k.op("act", lambda: nc.scalar.activation(sq, src[:, :, c0:c0 + 512], AF.Square),
                     reads=srcB, writes=sqB)
                bank, bB = psum.next()
                for kc in range(KC):
                    k.op("pe", lambda kc=kc: nc.tensor.matmul(bank[:, :], ones[:], sq[:, kc, :],
                                                             start=(kc == 0), stop=(kc == KC - 1)),
                         reads=sqB + [CB], writes=[bB])
                rs, rsB = rsr.next()
                k.op("act", lambda: nc.scalar.activation(rs[:], bank[:, :], AF.Sqrt, bias=epst[:], scale=1.0 / D),
                     reads=[bB, CB], writes=[rsB])
                k.op("dve", lambda: nc.vector.reciprocal(rs[:], rs[:]), reads=[rsB], writes=[rsB])
                for kc in range(KC):
                    k.op("dve", lambda kc=kc: nc.vector.scalar_tensor_tensor(
                        out=dst[:, kc, c0:c0 + 512], in0=src[:, kc, c0:c0 + 512],
                        scalar=gvec[:, gcol + kc:gcol + kc + 1], in1=rs[:], op0=ALU.mult, op1=ALU.mult),
                        reads=[srcB[kc], rsB, CB], writes=[dstB[kc]])

        def norm_lite(src, srcB, gcol, dst, dstB, ncols, sqs, rsr, skip_sq=False, skip_hg=False):
            for kc in range(0 if skip_hg else KC):
                if kc % 2 == 0:
                    k.op("dve", lambda kc=kc: nc.vector.tensor_scalar(
                        dst[:, kc, 0:ncols], src[:, kc, 0:ncols], gvec[:, gcol + kc:gcol + kc + 1], None, ALU.mult),
                        reads=[srcB[kc], CB], writes=[dstB[kc]])
                else:
                    k.op("act", lambda kc=kc: nc.scalar.activation(
                        dst[:, kc, 0:ncols], src[:, kc, 0:ncols], AF.Copy, scale=gvec[:, gcol + kc:gcol + kc + 1]),
                        reads=[srcB[kc], CB], writes=[dstB[kc]])
            return sumsq_chain(src, srcB, ncols, sqs, rsr, skip_sq)

        def sumsq_chain(src, srcB, ncols, sqs, rsr, skip_sq=False):
            for s_ in range(0 if skip_sq else ncols // 512):
                sq, sqB = sqs[s_]
                k.op("act", lambda s_=s_, sq=sq: nc.scalar.activation(sq, src[:, :, s_ * 512:(s_ + 1) * 512], AF.Square),
                     reads=srcB, writes=sqB)

            def finish():
                out = []
                for s_ in range(ncols // 512):
                    sq, sqB = sqs[s_]
                    bank, bB = psum.next()
                    for kc in range(KC):
                        k.op("pe", lambda kc=kc, sq=sq, bank=bank: nc.tensor.matmul(
                            bank[:, :], ones[:], sq[:, kc, :], start=(kc == 0), stop=(kc == KC - 1)),
                            reads=sqB + [CB], writes=[bB])
                    rs, rsB = rsr.next()
                    k.op("act", lambda rs=rs, bank=bank: nc.scalar.activation(
                        rs[:], bank[:, :], AF.Sqrt, bias=epst[:], scale=1.0 / D), reads=[bB, CB], writes=[rsB])
                    k.op("dve", lambda rs=rs: nc.vector.reciprocal(rs[:], rs[:]), reads=[rsB], writes=[rsB])
                    out.append((rs, rsB))
                return out
            return finish

        with ExitStack() as s1:
            xt = [(sb(s1, f"xt{i}", [128, D], F32), Buf()) for i in range(4)]
            xTt = [(sb(s1, f"xTt{i}", [128, KC, 512], F32), [bufs(4) for _ in range(4)]) for i in range(3)]
            sqs = [(sb(s1, f"sq1_{i}", [128, KC, 512], BF16), bufs(4)) for i in range(2)]
            xnt = [(sb(s1, f"xnt{i}", [128, KC, 512], BF16), bufs(KC)) for i in range(2)]
            rsr = Ring([(sb(s1, f"rs1_{i}", [128, 512], F32), Buf()) for i in range(2)])
            def s1_transpose(i):
                xT, xTB = xTt[i % 3]
                sq, sqB = sqs[i % 2]
                for b in range(4):
                    gb = i * 4 + b
                    xb, xbB = xt[gb % 4]
                    k.dma("sp", xb[:], xh[gb * 128:(gb + 1) * 128, :], writes=[xbB])
                    for j in range(4):
                        bank, bB = psum.next()
                        for q in range(4):
                            kc = 4 * j + q
                            k.op("pe", lambda kc=kc, q=q: nc.tensor.transpose(
                                bank[:, q * 128:(q + 1) * 128], xb[:, kc * 128:(kc + 1) * 128], identf[:]),
                                reads=[xbB, CB], writes=[bB])
                        copy_op(evac_eng(), xT[:, 4 * j:4 * j + 4, b * 128:(b + 1) * 128],
                                bank[:, :].rearrange("p (q n) -> p q n", q=4),
                                reads=[bB], writes=[xTB[j][b]])
                if i >= 4:
                    k.dma("pool", dview(xT_d, (i - 4) * 512, (i - 3) * 512), xT[:],
                          reads=[xTB[j][b] for j in range(4) for b in range(4)])

            def s1_squares(i):
                xT, xTB = xTt[i % 3]
                sq, sqB = sqs[i % 2]
                for b in range(4):
                    k.op("act", lambda b=b: nc.scalar.activation(sq[:, :, b * 128:(b + 1) * 128],
                                                               xT[:, :, b * 128:(b + 1) * 128], AF.Square),
                         reads=[xTB[j][b] for j in range(4)], writes=[sqB[b]])

            def s1_norm(i):
                xT, xTB = xTt[i % 3]
                sq, sqB = sqs[i % 2]
                xn, xnB = xnt[i % 2]
                bank, bB = psum.next()
                for kc in range(KC):
                    k.op("pe", lambda kc=kc: nc.tensor.matmul(bank[:, :], ones[:], sq[:, kc, :],
                                                             start=(kc == 0), stop=(kc == KC - 1)),
                         reads=sqB + [CB], writes=[bB])
                rs, rsB = rsr.next()
                k.op("act", lambda: nc.scalar.activation(rs[:], bank[:, :], AF.Sqrt, bias=epst[:], scale=1.0 / D),
                     reads=[bB, CB], writes=[rsB])
                k.op("dve", lambda: nc.vector.reciprocal(rs[:], rs[:]), reads=[rsB], writes=[rsB])
                for kc in range(KC):
                    k.op("dve", lambda kc=kc: nc.vector.scalar_tensor_tensor(
                        out=xn[:, kc, :], in0=xT[:, kc, :], scalar=gvec[:, kc:kc + 1], in1=rs[:],
                        op0=ALU.mult, op1=ALU.mult), reads=xTB[kc // 4] + [rsB, CB], writes=[xnB[kc]])
                k.dma("pool", dview(xnT_d, i * 512, (i + 1) * 512), xn[:], reads=xnB)

            s1_transpose(0)
            s1_transpose(1)
            s1_squares(0)
            for i in range(8):
                if i + 2 < 8:
                    s1_transpose(i + 2)
                if i + 1 < 8:
                    s1_squares(i + 1)
                s1_norm(i)
        k.barrier()

        if stages >= 2:
          with ExitStack() as s2:
            xnr = [(sb(s2, f"xn2_{i}", [128, KC, 512], BF16), Buf()) for i in range(3)]
            wsl = sb(s2, "wsl", [128, KC, 1152], BF16)
            wslB = bufs(9)
            poolw = sb(s2, "poolw", [128, 4, 128], BF16)
            rcnt = sb(s2, "rcnt", [128, 4, 16], F32)
            maskc = sb(s2, "maskc", [128, 256], F32)
            hneg = sb(s2, "hneg", [128, 1], F32)
            C2 = Buf("c2")
            k.dma("pool", poolw[:], poolw_d, writes=[C2])
            k.dma("sp", rcnt[:], rcnt_d, writes=[C2])
            k.dma("sp", maskc[:], maskc_d, writes=[C2])
            k.dma("sp", hneg[:], hneg_d, writes=[C2])
            xn_i = [0]

            def load_xn(i):
                t, B = xnr[xn_i[0] % 3]
                xn_i[0] += 1
                k.dma("sp", t[:], dview(xnT_d, i * 512, (i + 1) * 512), writes=[B])
                return t, B

            passes = [(j, gs) for j in range(4) for gs in ((0, 1), (2,))]
            tile_seq = []
            for (j, gs) in passes:
                for i in range(0 if 2 in gs else 3, 8):
                    tile_seq.append([(0, dview(xnT_d, i * 512, (i + 1) * 512), KC, 512)])
            xns = WStream(k, [(t[:].rearrange("p a b -> p (a b)"), [B]) for (t, B) in xnr], tile_seq, q="sp")

            def load_w(j, g):
                base = 512 + 1152 * j
                for kind in range(3):
                    c = kind * 384 + g * 128
                    k.dma("pool", wsl[:, :, c:c + 128],
                          w_in[:, base + c: base + c + 128].rearrange("(k p) n -> p k n", p=128),
                          writes=[wslB[kind * 3 + g]])

            with ExitStack() as sp_:
                uT = sb(sp_, "uT", [128, 4, 2064], F32)
                uTB = [bufs(5) for _ in range(4)]
                T = [(sb(sp_, f"pT{i}", [128, 2064], F32), Buf()) for i in range(2)]
                pooled = sb(sp_, "pooled", [128, 4, 2048], BF16)
                pooledB = [bufs(4) for _ in range(4)]
                t16 = sb(sp_, "t16", [128, 16], F32)
                t16B = Buf()
                mo = [(sb(sp_, f"mo{i}", [128, 2048], BF16), Buf()) for i in range(4)]
                k.dma("pool", wsl[:, :, 0:512], w_in[:, 0:512].rearrange("(k p) n -> p k n", p=128), writes=wslB[0:4])

                def pool_tile_post(kt):
                    lo, hi = 16 + 512 * kt, 16 + 512 * (kt + 1)
                    for c in range(4):
                        w = 2 << c
                        rd = [uTB[c][kt + 1]] + ([uTB[c][kt]] if True else [])
                        cur, curB = uT[:, c, :], rd
                        for stp in range(c + 1):
                            sh = 1 << stp
                            st_ = lo - w + 2 * sh
                            dT, dB = T[stp % 2]
                            k.op("dve", lambda cur=cur, st_=st_, sh=sh, dT=dT: nc.vector.tensor_tensor(
                                out=dT[:, st_:hi], in0=cur[:, st_:hi], in1=cur[:, st_ - sh:hi - sh], op=ALU.add),
                                reads=curB, writes=[dB])
                            cur, curB = dT[:, :], [dB]
                        k.op("dve", lambda cur=cur: nc.vector.scalar_tensor_tensor(
                            out=pooled[:, c, lo - 16:hi - 16], in0=cur[:, lo:hi], scalar=1.0 / w, in1=uT[:, c, lo:hi],
                            op0=ALU.mult, op1=ALU.subtract), reads=curB + rd, writes=[pooledB[c][kt]])
                        if kt == 0:
                            k.op("dve", lambda cur=cur: nc.vector.tensor_tensor(
                                out=t16[:], in0=cur[:, 16:32], in1=rcnt[:, c, :], op=ALU.mult),
                                reads=curB + [C2], writes=[t16B])
                            k.op("dve", lambda: nc.vector.tensor_tensor(
                                out=pooled[:, c, 0:16], in0=t16[:], in1=uT[:, c, 16:32], op=ALU.subtract),
                                reads=[t16B] + rd, writes=[pooledB[c][kt]])
                def pool_tile_mix(kt):
                    lo, hi = 16 + 512 * kt, 16 + 512 * (kt + 1)
                    for c in range(4):
                        m, mB = mo[c]
                        bank, bB = psum.next()
                        k.op("pe", lambda: nc.tensor.matmul(bank[:, :], poolw[:, c, :],
                                                           pooled[:, c, lo - 16:hi - 16], start=True, stop=True),
                             reads=[pooledB[c][kt], C2], writes=[bB])
                        k.op("act", lambda: nc.scalar.activation(
                            m[:, lo - 16:hi - 16], bank[:, :], AF.Copy, scale=gvec[:, 64 + c:65 + c]),
                            reads=[bB, CB], writes=[mB])

                for i in range(3, 8):
                    xn, xnB = load_xn(i)
                    c0, n = (496, 16) if i == 3 else (0, 512)
                    off = 0 if i == 3 else 16 + (i - 4) * 512
                    for c in range(4):
                        bank, bB = psum.next()
                        for kc in range(KC):
                            k.op("pe", lambda kc=kc: nc.tensor.matmul(
                                bank[:, 0:n], wsl[:, kc, c * 128:(c + 1) * 128], xn[:, kc, c0:c0 + n],
                                start=(kc == 0), stop=(kc == KC - 1)), reads=[xnB] + wslB[0:4], writes=[bB])
                        copy_op(evac_eng(), uT[:, c, off:off + n], bank[:, 0:n], reads=[bB], writes=[uTB[c][i - 3]])
                    if i >= 5:
                        pool_tile_mix(i - 5)
                    if i >= 4:
                        pool_tile_post(i - 4)
                pool_tile_mix(3)
                for c in range(4):
                    k.dma("pool", mixT_d[c], mo[c][0][:], reads=[mo[c][1]])
                for g in range(3):
                    load_w(0, g)
                xns.prefetch()
            k.barrier()

            LQ = [NT // d for d in GD]
            LK = [(NT + h) // d for d, h in zip(GD, GH)]
            NBK = [l // 128 for l in LK]
            QT = sb(s2, "QT", [128, 3, NT], BF16)
            KT = [sb(s2, f"KT{g}", [128, GD[g] * LK[g]], BF16) for g in range(3)]
            VT = [sb(s2, f"VT{g}", [128, GD[g] * LK[g]], BF16) for g in range(3)]
            V = [sb(s2, f"V{g}", [128, GD[g] * NBK[g], 128], BF16) for g in range(3)]
            num = sb(s2, "num", [128, 3, NT], F32)
            densum = sb(s2, "densum", [128, NT], F32)
            ybf = [(sb(s2, f"ybf{i}", [128, NT], BF16), Buf()) for i in range(3)]
            sbr = Ring([(sb(s2, f"sbs{i}", [128, 512], F32), Buf()) for i in range(2)])
            ptr = Ring([(sb(s2, f"pts{i}", [128, 512], BF16), Buf()) for i in range(2)])
            bm = sb(s2, "bm", [128, 3, 2, 256], F32)
            bmB = bufs(3)
            QTB = [bufs(8) for _ in range(3)]
            KTB = [bufs(8) for _ in range(3)]
            VTB = [bufs(8) for _ in range(3)]
            VB = [Buf() for _ in range(3)]
            numB = bufs(3)
            denB = Buf()
            yi = [0]
            ps_chunk = Ring(banks[0:4])
            ps_S = Ring(banks[4:6])
            ps_N, ps_D = banks[6], banks[7]

            def build_bm(j, g):
                k.dma("sp", bm[:, g, 0, :], biasT_d[:, j, g, :], writes=[bmB[g]])
                k.op("dve", lambda: nc.vector.tensor_tensor(out=bm[:, g, 0, :], in0=bm[:, g, 0, :], in1=maskc[:], op=ALU.add),
                     reads=[C2], writes=[bmB[g]])
                k.op("dve", lambda: nc.vector.tensor_copy(bm[:, g, 1, 128:256], bm[:, g, 0, 128:256]),
                     reads=[bmB[g]], writes=[bmB[g]])
                k.op("dve", lambda: nc.vector.tensor_scalar(bm[:, g, 1, 0:128], bm[:, g, 0, 0:128], hneg[:, 0:1], None, ALU.add),
                     reads=[bmB[g], C2], writes=[bmB[g]])

            def pass_chunks(j, gs):
                tasks = []
                for i in range(0 if 2 in gs else 3, 8):
                    holder = {}
                    kinds = (1, 2) if i < 4 else (0, 1, 2)
                    combos = [(g, kind) for g in gs for kind in kinds]
                    for ki, (g, kind) in enumerate(combos):
                        def chunk(i=i, kind=kind, g=g, holder=holder, first=(ki == 0)):
                            d, H = GD[g], GH[g]
                            if first:
                                flat, xB = xns.get()
                                holder["xn"] = (v3(flat, 0, KC, 512), xB)
                            xn, xnB = holder["xn"]
                            if kind == 0:
                                c0, n, e0 = 0, 512, (i - 4) * 512
                                dst3 = QT[:, g, :].rearrange("p (r l) -> p r l", r=d)
                                dB = QTB[g][i]
                            else:
                                e0 = i * 512 - NT + H
                                c0, n = (0, 512) if e0 >= 0 else (-e0, 512 + e0)
                                e0 = max(e0, 0)
                                tt_ = KT[g] if kind == 1 else VT[g]
                                dst3 = tt_[:, :].rearrange("p (r l) -> p r l", r=d)
                                dB = (KTB if kind == 1 else VTB)[g][i]
                            wc = kind * 384 + g * 128
                            bank, bB = ps_chunk.next()
                            for kc in range(KC):
                                k.op("pe", lambda kc=kc: nc.tensor.matmul(
                                    bank[:, 0:n], wsl[:, kc, wc:wc + 128], xn[:, kc, c0:c0 + n],
                                    start=(kc == 0), stop=(kc == KC - 1)),
                                    reads=xnB + [wslB[kind * 3 + g]], writes=[bB])
                            src3 = bank[:, 0:n].rearrange("p (l r) -> p r l", r=d)
                            dsl = dst3[:, :, e0 // d:(e0 + n) // d]
                            if kind == 0:
                                copy_op("act", dsl, src3, [bB], [dB], scale=QSCALE)
                            elif kind == 1:
                                copy_op("dve", dsl, src3, [bB], [dB])
                            else:
                                copy_op(evac_eng(), dsl, src3, [bB], [dB])
                        tasks.append(chunk)
                return tasks

            def attn_tasks(j, g):
                tasks = []
                d, nqb, nbk, lq, lk = GD[g], LQ[g] // 128, NBK[g], LQ[g], LK[g]
                nb = d * nbk

                def vtrans(b0):
                    nn = min(4, nb - b0)
                    bank, bB = ps_chunk.next()
                    bv = bank[:, :].bitcast(BF16)
                    for q in range(nn):
                        b = b0 + q
                        k.op("pe", lambda b=b, q=q: nc.tensor.transpose(
                            bv[:, q * 128:(q + 1) * 128], VT[g][:, b * 128:(b + 1) * 128], identb[:]),
                            reads=VTB[g] + [CB], writes=[bB])
                    copy_op(evac_eng(), V[g][:, b0:b0 + nn, :].rearrange("p a b -> p (a b)"),
                            bv[:, 0:nn * 128], [bB], [VB[g]])

                for b0 in range(0, nb, 4):
                    tasks.append(lambda b0=b0: vtrans(b0))
                blocks = [(r, qb) for r in range(d) for qb in range(nqb)]
                pairs = [blocks[p:p + 2] for p in range(0, 16, 2)]
                state = {}

                def emit_S(pi):
                    sbk, sB = ps_S.next()
                    for xi, (r, qb) in enumerate(pairs[pi]):
                        qv = QT[:, g, r * lq + qb * 128: r * lq + (qb + 1) * 128]
                        for hh in range(2):
                            kv = KT[g][:, r * lk + (qb + hh) * 128: r * lk + (qb + hh + 1) * 128]
                            co = xi * 256 + hh * 128
                            k.op("pe", lambda kv=kv, qv=qv, co=co: nc.tensor.matmul(
                                sbk[:, co:co + 128], kv, qv, start=True, stop=True),
                                reads=QTB[g] + KTB[g], writes=[sB])
                    sbt, sbB = sbr.next()
                    for xi, (r, qb) in enumerate(pairs[pi]):
                        var = 1 if qb == 0 else 0
                        k.op("dve", lambda xi=xi, var=var: nc.vector.tensor_tensor(
                            out=sbt[:, xi * 256:(xi + 1) * 256], in0=sbk[:, xi * 256:(xi + 1) * 256],
                            in1=bm[:, g, var, :], op=ALU.add), reads=[sB, bmB[g]], writes=[sbB])
                    pt, ptB = ptr.next()
                    k.op("act", lambda: nc.scalar.activation(pt[:], sbt[:], AF.Exp), reads=[sbB], writes=[ptB])
                    state[pi] = (pt, ptB)

                def emit_PV(pi):
                    pt, ptB = state.pop(pi)
                    (nb_, nbB), (db_, dbB) = ps_N, ps_D
                    p0 = (pi % 2) * 2
                    for xi, (r, qb) in enumerate(pairs[pi]):
                        co = (p0 + xi) * 128
                        for hh in range(2):
                            vv = V[g][:, r * nbk + qb + hh, :]
                            pv = pt[:, xi * 256 + hh * 128: xi * 256 + (hh + 1) * 128]
                            k.op("pe", lambda vv=vv, pv=pv, co=co, hh=hh: nc.tensor.matmul(
                                nb_[:, co:co + 128], vv, pv, start=(hh == 0), stop=(hh == 1)),
                                reads=[ptB, VB[g]], writes=[nbB])
                        for hh in range(2):
                            pv = pt[:, xi * 256 + hh * 128: xi * 256 + (hh + 1) * 128]
                            k.op("pe", lambda pv=pv, co=co, hh=hh: nc.tensor.matmul(
                                db_[:, co:co + 128], ones[:], pv, start=(hh == 0), stop=(hh == 1)),
                                reads=[ptB, CB], writes=[dbB])
                    if pi % 2 == 0:
                        return
                    q0 = (pi // 2) * 4
                    if g == 0:
                        u = q0 // 4
                        nv = num[:, g, u * 512:(u + 1) * 512]
                        dv = densum[:, u * 512:(u + 1) * 512]
                        sn, sd = nb_[:, :], db_[:, :]
                    elif g == 1:
                        r = q0 // 4
                        nv = num[:, g, :].rearrange("p (b i r) -> p r b i", b=4, r=4)[:, r, :, :]
                        dv = densum[:, :].rearrange("p (b i r) -> p r b i", b=4, r=4)[:, r, :, :]
                        sn = nb_[:, :].rearrange("p (b i) -> p b i", b=4)
                        sd = db_[:, :].rearrange("p (b i) -> p b i", b=4)
                    else:
                        nv = num[:, g, :].rearrange("p (i r) -> p r i", r=16)[:, q0:q0 + 4, :]
                        dv = densum[:, :].rearrange("p (i r) -> p r i", r=16)[:, q0:q0 + 4, :]
                        sn = nb_[:, :].rearrange("p (b i) -> p b i", b=4)
                        sd = db_[:, :].rearrange("p (b i) -> p b i", b=4)
                    k.op("act", lambda: nc.scalar.copy(nv, sn), reads=[nbB], writes=[numB[g]])
                    if g == 0:
                        k.op("dve", lambda: nc.vector.tensor_copy(dv, sd), reads=[dbB], writes=[denB])
                    else:
                        k.op("dve", lambda: nc.vector.tensor_tensor(out=dv, in0=sd, in1=dv, op=ALU.add),
                             reads=[dbB], writes=[denB])

                tasks.append(lambda: emit_S(0))
                for pi in range(8):
                    if pi + 1 < 8:
                        tasks.append(lambda pi=pi: emit_S(pi + 1))
                    tasks.append(lambda pi=pi: emit_PV(pi))
                if j + 1 < 4:
                    tasks.append(lambda: build_bm(j + 1, g))
                if g == 2:
                    for qq in range(4):
                        tasks.append(lambda qq=qq: finalize(j, qq))
                return tasks

            ycur = {}

            def finalize(j, qq):
                c0, c1 = qq * 512, (qq + 1) * 512
                k.op("dve", lambda: nc.vector.reciprocal(densum[:, c0:c1], densum[:, c0:c1]), reads=[denB], writes=[denB])
                for g in range(3):
                    if qq == 0:
                        ycur[g] = ybf[yi[0] % 3]
                        yi[0] += 1
                    y, yB = ycur[g]
                    k.op("dve", lambda g=g, y=y: nc.vector.tensor_tensor(
                        out=y[:, c0:c1], in0=num[:, g, c0:c1], in1=densum[:, c0:c1], op=ALU.mult),
                        reads=[numB[g], denB], writes=[yB])
                    if qq == 3:
                        k.dma("pool", mixT_d[4 + 4 * g + j], y[:], reads=[yB])

            for g in range(3):
                build_bm(0, g)
            pending = []
            for (j, gs) in passes:
                chunks = pass_chunks(j, gs)
                done = 0
                for ci, ch in enumerate(chunks):
                    ch()
                    want = (len(pending) + done) * (ci + 1) // len(chunks)
                    while done < want and pending:
                        pending.pop(0)()
                        done += 1
                while pending:
                    pending.pop(0)()
                pending = []
                for g in gs:
                    if j + 1 < 4:
                        load_w(j + 1, g)
                    pending += attn_tasks(j, g)
            while pending:
                pending.pop(0)()
          k.barrier()

        if stages >= 3:
          with ExitStack() as s3:
            TT = 1024
            hT = sb(s3, "hT", [128, KC, TT], F32)
            hB = bufs(KC)
            hn = sb(s3, "hn", [128, KC, TT], BF16)
            hnB = bufs(KC)
            R = sb(s3, "R", [128, 22 * 1024], BF16)
            RB = bufs(22)
            wr = [(sb(s3, f"wr{i}", [128, 8192], BF16)[:, :], bufs(2)) for i in range(3)]
            rsr = Ring([(sb(s3, f"rs3_{i}", [128, 512], F32), Buf()) for i in range(2)])
            sgr = Ring([(sb(s3, f"sg{i}", [128, 512], F32), Buf()) for i in range(2)])
            tmr = Ring([(sb(s3, f"tm{i}", [128, 512], F32), Buf()) for i in range(3)])
            t1r = tmr
            pblk = Ring([(sb(s3, f"pb{i}", [128, 256], F32), Buf()) for i in range(2)])
            sq = R[:, 0:8192].rearrange("p (k n) -> p k n", k=KC)
            sqB = RB[0:8]
            sqs2 = [(sq, sqB), (R[:, 10240:18432].rearrange("p (k n) -> p k n", k=KC), RB[10:18])]
            gfb = R[:, 18432:22528].bitcast(F32)
            gfbB = RB[18:22]
            rscol = sb(s3, "rscol", [128, 16], F32)
            rscolB = Buf()
            act = R[:, :].rearrange("p (k n) -> p k n", k=22)
            pTb = R[:, 8192:10240].rearrange("p (k n) -> p k n", k=2)
            pTB = RB[8:10]
            yblk = [(R[:, 10240 + i * 4096: 10240 + (i + 1) * 4096].bitcast(F32), RB[10 + 4 * i: 14 + 4 * i]) for i in range(2)]

            def kview(w, r0, nk, c0, nc_):
                return w[r0:r0 + nk * 128, c0:c0 + nc_].rearrange("(k p) n -> p k n", p=128)

            halves = [(0, 22), (22, 22)]
            for tt in range(NT // TT):
                t0 = tt * TT
                specs = [[(0, kview(w_out, 0, KC, a * 512, 512), KC, 512)] for a in range(4)]
                for (f0, nf) in halves:
                    for t in range(nf // 2):
                        specs.append([(0, kview(w_gate, 0, KC, (f0 + 2 * t) * 128, 256), KC, 256),
                                      (4096, kview(w_up, 0, KC, (f0 + 2 * t) * 128, 256), KC, 256)])
                    for c in range(8):
                        specs.append([(0, kview(w_down, f0 * 128, nf, c * 256, 256), nf, 256)])
                for a in range(8):
                    specs.append([(0, kview(w_pg, 0, KC, a * 256, 256), KC, 256),
                                  (4096, kview(w_pp, 0, 2, a * 256, 256), 2, 256)])
                ws = WStream(k, wr, specs)

                if tt == 0:
                    k.dma("sp", hn[:], dview(mixT_d, t0, t0 + TT), writes=hnB)
                for kc4 in range(0, KC, 4):
                    k.dma("sp", hT[:, kc4:kc4 + 4, :], xT_d[kc4:kc4 + 4, :, t0:t0 + TT].rearrange("k p n -> p k n"),
                          writes=hB[kc4:kc4 + 4])

                def mm_group(wv, wB, ocol, src, srcB, nk):
                    bb = [psum.next(), psum.next()]
                    for kk in range(nk):
                        for s_ in range(2):
                            k.op("pe", lambda kk=kk, s_=s_: nc.tensor.matmul(
                                bb[s_][0][:, :], wv[:, kk, ocol:ocol + 128], src[:, kk, s_ * 512:(s_ + 1) * 512],
                                start=(kk == 0), stop=(kk == nk - 1)),
                                reads=wB + srcB, writes=[bb[s_][1]])
                    return bb

                def sq_fly(oc, s_):
                    sqv, sqvB = sqs2[s_]
                    hv = hT[:, oc, s_ * 512:(s_ + 1) * 512]
                    k.op("act", lambda: nc.scalar.activation(sqv[:, oc, :], hv, AF.Square),
                         reads=[hB[oc]], writes=[sqvB[oc // 2]])

                def add_into_h(oc, bb, sq_on=False, hg_col=None):
                    for s_ in range(2):
                        hv = hT[:, oc, s_ * 512:(s_ + 1) * 512]
                        k.op("dve", lambda hv=hv, s_=s_: nc.vector.tensor_tensor(out=hv, in0=bb[s_][0][:, :], in1=hv, op=ALU.add),
                             reads=[bb[s_][1]], writes=[hB[oc]])
                        if sq_on:
                            sq_fly(oc, s_)
                        if hg_col is not None:
                            k.op("act", lambda hv=hv, s_=s_: nc.scalar.activation(
                                hn[:, oc, s_ * 512:(s_ + 1) * 512], hv, AF.Copy, scale=gvec[:, hg_col + oc:hg_col + oc + 1]),
                                reads=[hB[oc], CB], writes=[hnB[oc]])

                for a in range(4):
                    wf, wB = ws.get()
                    wv = v3(wf, 0, KC, 512)
                    for o in range(4):
                        bb = mm_group(wv, wB, o * 128, hn, hnB, KC)
                        add_into_h(4 * a + o, bb, sq_on=True)
                finF = norm_lite(hT, hB, 16, hn, hnB, TT, sqs2, rsr, skip_sq=True)
                rsF = None
                for (f0, nf) in halves:
                    for t in range(nf // 2):
                        wf, wB = ws.get()
                        wg, wu = v3(wf, 0, KC, 256), v3(wf, 4096, KC, 256)
                        for o in range(2):
                            fl = 2 * t + o
                            gb_ = mm_group(wg, wB, o * 128, hn, hnB, KC)
                            ub_ = mm_group(wu, wB, o * 128, hn, hnB, KC)
                            if rsF is None:
                                rsF = finF()
                            for s_ in range(2):
                                rs, rsB = rsF[s_]
                                t1, t1B = tmr.next()
                                sg, sgB = sgr.next()
                                t2, t2B = tmr.next()
                                k.op("dve", lambda s_=s_, t1=t1, rs=rs: nc.vector.tensor_tensor(
                                    out=t1[:], in0=gb_[s_][0][:, :], in1=rs[:], op=ALU.mult),
                                    reads=[gb_[s_][1], rsB], writes=[t1B])
                                k.op("act", lambda t1=t1, sg=sg: nc.scalar.activation(sg[:], t1[:], AF.Silu),
                                     reads=[t1B], writes=[sgB])
                                k.op("dve", lambda s_=s_, t2=t2, rs=rs: nc.vector.tensor_tensor(
                                    out=t2[:], in0=ub_[s_][0][:, :], in1=rs[:], op=ALU.mult),
                                    reads=[ub_[s_][1], rsB], writes=[t2B])
                                k.op("dve", lambda s_=s_, sg=sg, t2=t2: nc.vector.tensor_tensor(
                                    out=act[:, fl, s_ * 512:(s_ + 1) * 512], in0=sg[:], in1=t2[:], op=ALU.mult),
                                    reads=[sgB, t2B], writes=[RB[fl]])
                    for c in range(8):
                        wf, wB = ws.get()
                        wd = v3(wf, 0, nf, 256)
                        for o in range(2):
                            bb = mm_group(wd, wB, o * 128, act, RB[0:nf], nf)
                            add_into_h(2 * c + o, bb, hg_col=(32 if f0 > 0 else None))
                k.dma("sp", gfb, gfinb_d, writes=gfbB)
                for b in range(TT // 128):
                    pb, pbB = pblk.next()
                    k.dma("sp", pb[:], pc[t0 + b * 128: t0 + (b + 1) * 128, :], writes=[pbB])
                    bank, bB = psum.next()
                    for q in range(2):
                        k.op("pe", lambda q=q: nc.tensor.transpose(bank[:, q * 128:(q + 1) * 128], pb[:, q * 128:(q + 1) * 128], identf[:]),
                             reads=[pbB, CB], writes=[bB])
                    copy_op(evac_eng(), pTb[:, :, b * 128:(b + 1) * 128], bank[:, 0:256].rearrange("p (q n) -> p q n", q=2),
                            [bB], pTB)
                finP = norm_lite(hT, hB, 32, hn, hnB, TT, sqs2, rsr, skip_hg=True)
                rsP = None
                for a in range(8):
                    wf, wB = ws.get()
                    wv, wpp = v3(wf, 0, KC, 256), v3(wf, 4096, 2, 256)
                    for o in range(2):
                        oc = 2 * a + o
                        gb_ = mm_group(wv, wB, o * 128, hn, hnB, KC)
                        pb_ = mm_group(wpp, wB, o * 128, pTb, pTB, 2)
                        if rsP is None:
                            rsP = finP()
                        for s_ in range(2):
                            sg, sgB = sgr.next()
                            tm, tmB = tmr.next()
                            hv = hT[:, oc, s_ * 512:(s_ + 1) * 512]
                            rs, rsB = rsP[s_]
                            t1, t1B = t1r.next()
                            k.op("dve", lambda s_=s_, t1=t1, rs=rs: nc.vector.tensor_tensor(
                                out=t1[:], in0=gb_[s_][0][:, :], in1=rs[:], op=ALU.mult),
                                reads=[gb_[s_][1], rsB], writes=[t1B])
                            k.op("act", lambda t1=t1, sg=sg: nc.scalar.activation(sg[:], t1[:], AF.Sigmoid),
                                 reads=[t1B], writes=[sgB])
                            k.op("dve", lambda s_=s_, sg=sg, tm=tm: nc.vector.tensor_tensor(
                                out=tm[:], in0=sg[:], in1=pb_[s_][0][:, :], op=ALU.mult),
                                reads=[sgB, pb_[s_][1]], writes=[tmB])
                            k.op("dve", lambda hv=hv, tm=tm: nc.vector.tensor_tensor(out=hv, in0=tm[:], in1=hv, op=ALU.add),
                                 reads=[tmB], writes=[hB[oc]])
                            sq_fly(oc, s_)
                if tt + 1 < NT // TT:
                    k.dma("sp", hn[:], dview(mixT_d, t0 + TT, t0 + 2 * TT), writes=hnB)
                tbank, tB = psum.next()
                for b in range(TT // 128):
                    sqv, sqvB = sqs2[b // 4]
                    c0 = (b % 4) * 128
                    for kc in range(KC):
                        k.op("pe", lambda kc=kc, b=b, c0=c0, sqv=sqv: nc.tensor.matmul(
                            tbank[:, 2 * b:2 * b + 2], sqv[:, kc, c0:c0 + 128], ones[:, 0:2],
                            start=(kc == 0), stop=(kc == KC - 1)), reads=sqvB + [CB], writes=[tB])
                k.op("act", lambda: nc.scalar.activation(rscol[:], tbank[:, 0:16], AF.Sqrt, bias=epst[:], scale=1.0 / D),
                     reads=[tB, CB], writes=[rscolB])
                k.op("dve", lambda: nc.vector.reciprocal(rscol[:], rscol[:]), reads=[rscolB], writes=[rscolB])
                for b in range(TT // 128):
                    yb, ybB = yblk[b % 2]
                    for jq in range(4):
                        bank, bB = psum.next()
                        for q in range(4):
                            kc = 4 * jq + q
                            k.op("pe", lambda kc=kc, q=q: nc.tensor.transpose(
                                bank[:, q * 128:(q + 1) * 128], hT[:, kc, b * 128:(b + 1) * 128], identf[:]),
                                reads=[hB[kc], CB], writes=[bB])
                        k.op("dve", lambda jq=jq, b=b, bank=bank, yb=yb: nc.vector.scalar_tensor_tensor(
                            out=yb[:, jq * 512:(jq + 1) * 512], in0=bank[:, :], scalar=rscol[:, 2 * b:2 * b + 1],
                            in1=gfb[:, jq * 512:(jq + 1) * 512], op0=ALU.mult, op1=ALU.mult),
                            reads=[bB, rscolB] + gfbB, writes=ybB)
                    k.dma("sp", out_d[t0 + b * 128: t0 + (b + 1) * 128, :], yb, reads=ybB)
          k.barrier()

        for t in k.all_tokens():
            k._wait("sp", t)
        used = k.used
    return nc, used


def _bucket(n):
    n = np.asarray(n, dtype=np.int32)
    nf = np.maximum(n, 1).astype(np.float32)
    large = 16 + (np.log(nf / np.float32(16)) / np.float32(np.log(2048 / 16)) * np.float32(16)).astype(np.int32)
    large = np.minimum(large, 31)
    return np.where(n < 16, n, large)


def _host_inputs(inp):
    f = lambda a: np.ascontiguousarray(np.asarray(a, dtype=np.float32))
    x = f(inp["x"])
    p = f(inp["p"])[0]
    rel_bias = f(inp["rel_bias"])
    w_in = f(inp["w_in"])[0]
    cols = list(range(512))
    for j in range(4):
        for kind in range(3):
            for g in range(3):
                h = 4 * g + j
                c = 512 + kind * 1536 + h * 128
                cols.extend(range(c, c + 128))
    w_in_p = np.ascontiguousarray(w_in[:, cols])
    gv = np.concatenate([
        f(inp["norm_mix_g"])[0].reshape(16, 128).T, f(inp["norm_ffn_g"])[0].reshape(16, 128).T,
        f(inp["norm_ple_g"])[0].reshape(16, 128).T, f(inp["final_norm_g"]).reshape(16, 128).T,
        f(inp["pool_scale"])[0].reshape(4, 128).T], axis=1)
    gv = np.ascontiguousarray(gv)
    poolw = np.ascontiguousarray(f(inp["pool_w"])[0].transpose(1, 0, 2))
    ki = np.arange(256)[:, None]
    q = np.arange(128)[None, :]
    off = q + 128 - ki
    valid = (off >= 0) & (off <= 128)
    biasT = np.zeros((128, 4, 3, 256), np.float32)
    for g in range(3):
        bk = _bucket(np.maximum(off, 0) * GD[g])
        for j in range(4):
            tb = rel_bias[bk, 4 * g + j]
            biasT[:, j, g, 0:128] = tb[0:128]
            biasT[:, j, g, 128:256] = tb[128:256]
    maskc = np.where(valid, 0.0, NEG).astype(np.float32)
    maskc = np.ascontiguousarray(np.concatenate([maskc[0:128], maskc[128:256]], axis=1))
    common = {
        "w_in": w_in_p, "w_out": f(inp["w_out"])[0], "w_gate": f(inp["w_gate"])[0], "w_up": f(inp["w_up"])[0],
        "w_down": f(inp["w_down"])[0], "w_pg": f(inp["w_ple_gate"])[0], "w_pp": f(inp["w_ple_proj"])[0],
        "poolw_in": poolw, "gvec_in": gv, "biasT": biasT, "maskc_in": maskc, "identf_in": np.eye(128, dtype=np.float32),
        "gfinb_in": np.ascontiguousarray(np.broadcast_to(f(inp["final_norm_g"])[None, :], (128, D))),
    }
    maps = []
    for c in range(8):
        b, qd = c // 4, c % 4
        s = qd * NT
        xh = np.zeros((2 * NT, D), np.float32)
        if qd > 0:
            xh[0:NT] = x[b, s - NT:s]
        xh[NT:] = x[b, s:s + NT]
        hneg = np.full((128, 1), 0.0 if qd > 0 else NEG, np.float32)
        rc = np.zeros((128, 4, 16), np.float32)
        for gi, w in enumerate((2, 4, 8, 16)):
            pos = s + np.arange(16) + 1
            rc[:, gi, :] = (1.0 / np.minimum(pos, w)).astype(np.float32)[None, :]
        m = dict(common)
        m.update({"xh": xh, "pc": np.ascontiguousarray(p[b, s:s + NT]), "hneg_in": hneg, "rcnt_in": rc})
        maps.append(m)
    return maps


_NC_CACHE = {}


def _get_nc():
    if "nc" not in _NC_CACHE:
        _, used = build(None)
        nc, _ = build(used)
        _NC_CACHE["nc"] = nc
    return _NC_CACHE["nc"]


def kernel(**inputs):
    maps = _host_inputs(inputs)
    nc = _get_nc()
    res = run_bass_kernel_spmd(nc, maps, core_ids=list(range(8)))
    out = np.zeros((2, 4 * NT, D), np.float32)
    for c in range(8):
        b, qd = c // 4, c % 4
        out[b, qd * NT:(qd + 1) * NT] = np.asarray(res.results[c]["out"], dtype=np.float32)
    return out
```

```python
import numpy as np
from contextlib import ExitStack
import concourse.bass as bass
import concourse.mybir as mybir
from concourse.bass_utils import run_bass_kernel_spmd

F32 = mybir.dt.float32
BF16 = mybir.dt.bfloat16
AF = mybir.ActivationFunctionType
ALU = mybir.AluOpType

COMPUTE = ("pe", "act", "dve", "pool")
EPOCH = 12000
NDMASEM = 20

NT = 2048
D = 2048
KC = 16
DFF = 5632
EPS = 1e-6
QSCALE = float(128 ** -0.5)
NEG = -30000.0
GD = (1, 4, 16)
GH = (128, 512, 2048)


class Tok:
    __slots__ = ("eng", "id", "sem", "val", "dma")

    def __init__(self, eng, id_, dma=False):
        self.eng = eng
        self.id = id_
        self.sem = None
        self.val = 0
        self.dma = dma


class Buf:
    __slots__ = ("name", "w", "r")

    def __init__(self, name=""):
        self.name = name
        self.w = None
        self.r = {}


def bufs(n):
    return [Buf() for _ in range(n)]


class K:
    def __init__(self, nc, stack, needed=None):
        self.nc = nc
        self.stack = stack
        self.needed = needed
        self.used = set()
        self.opid = 0
        self.engs = {"pe": nc.tensor, "act": nc.scalar, "dve": nc.vector,
                     "pool": nc.gpsimd, "sp": nc.sync}
        self.cur = {}
        self.nsem = 0
        for e in COMPUTE:
            self.cur[e] = [self._newsem(e), 0]
        self.waited = {e: {p: 0 for p in COMPUTE} for e in self.engs}
        self.waited_dma = {e: {} for e in self.engs}
        self.dsem = {}
        for q in ("sp", "pool"):
            self.dsem[q] = [[self._newsem(f"d{q}{i}"), 0] for i in range(NDMASEM)]
        self.dptr = {q: 0 for q in self.dsem}
        self.last = {e: None for e in COMPUTE}
        self.dma_toks = {}

    def _newsem(self, name):
        self.nsem += 1
        return self.stack.enter_context(self.nc.semaphore(f"s{self.nsem}_{name}"))

    def _wait(self, eng, t):
        if t is None:
            return
        if t.dma:
            key = id(t.sem)
            if self.waited_dma[eng].get(key, 0) >= t.val:
                return
            self.engs[eng].wait_ge(t.sem, t.val)
            self.waited_dma[eng][key] = t.val
            return
        if t.eng == eng and eng == "pe":
            return
        if self.waited[eng][t.eng] >= t.id:
            return
        self.used.add(t.id)
        self.waited[eng][t.eng] = t.id
        if t.sem is None:
            raise RuntimeError("dependency on an unsignalled op (two-pass mismatch)")
        self.engs[eng].wait_ge(t.sem, t.val)

    @staticmethod
    def _deps(reads, writes, extra):
        deps = []
        for b in reads:
            if b.w is not None:
                deps.append(b.w)
        for b in writes:
            if b.w is not None:
                deps.append(b.w)
            deps.extend(b.r.values())
        deps.extend(extra)
        return deps

    @staticmethod
    def _mark(tok, reads, writes, key):
        for b in reads:
            b.r[key] = tok
        for b in writes:
            b.w = tok
            b.r = {}

    def op(self, eng, fn, reads=(), writes=(), extra=()):
        for t in self._deps(reads, writes, extra):
            self._wait(eng, t)
        inst = fn()
        self.opid += 1
        tok = Tok(eng, self.opid)
        if self.needed is None or tok.id in self.needed:
            c = self.cur[eng]
            if c[1] >= EPOCH:
                c[0] = self._newsem(eng)
                c[1] = 0
            c[1] += 1
            inst.then_inc(c[0], 1)
            tok.sem = c[0]
            tok.val = c[1]
        self.last[eng] = tok
        self._mark(tok, reads, writes, eng)
        return tok

    def dma(self, q, out, in_, reads=(), writes=(), extra=(), **kw):
        for t in self._deps(reads, writes, extra):
            self._wait(q, t)
        ring = self.dsem[q]
        slot = ring[self.dptr[q] % NDMASEM]
        self.dptr[q] += 1
        sem, cnt = slot
        if cnt > 0:
            key = id(sem)
            if self.waited_dma[q].get(key, 0) < cnt:
                self.engs[q].wait_ge(sem, cnt)
                self.waited_dma[q][key] = cnt
        inst = self.engs[q].dma_start(out=out, in_=in_, **kw)
        slot[1] = cnt + 16
        inst.then_inc(sem, 16)
        self.opid += 1
        tok = Tok(q, self.opid, dma=True)
        tok.sem = sem
        tok.val = cnt + 16
        self._mark(tok, reads, writes, ("dma", tok.id))
        self.dma_toks[id(sem)] = tok
        return tok

    def all_tokens(self):
        return [t for t in self.last.values() if t is not None] + list(self.dma_toks.values())

    def barrier(self):
        toks = self.all_tokens()
        for e in ("pe", "act", "dve", "pool", "sp"):
            for t in toks:
                self._wait(e, t)


class Ring:
    def __init__(self, items):
        self.items = items
        self.i = 0

    def next(self):
        it = self.items[self.i % len(self.items)]
        self.i += 1
        return it


class WStream:
    def __init__(self, k, bufs_, specs, q="pool"):
        self.k = k
        self.q = q
        self.bufs = bufs_
        self.specs = specs
        self.loaded = 0
        self.i = 0

    def _load(self, j):
        flat, B = self.bufs[j % len(self.bufs)]
        for pi, (off, src, a, b) in enumerate(self.specs[j]):
            view = flat[:, off:off + a * b].rearrange("p (a b) -> p a b", a=a)
            self.k.dma(self.q, view, src, writes=[B[pi]])

    def prefetch(self):
        n = len(self.bufs)
        while self.loaded < min(len(self.specs), self.i + n):
            self._load(self.loaded)
            self.loaded += 1

    def get(self):
        n = len(self.bufs)
        while self.loaded < min(len(self.specs), self.i + n):
            self._load(self.loaded)
            self.loaded += 1
        flat, B = self.bufs[self.i % n]
        self.i += 1
        return flat, B


def v3(flat, off, a, b):
    return flat[:, off:off + a * b].rearrange("p (a b) -> p a b", a=a)


def build(needed=None, debug=False, stages=3):
    nc = bass.Bass("TRN2", target_bir_lowering=False)

    def din(name, shape):
        return nc.dram_tensor(name, shape, F32, kind="ExternalInput").ap()

    xh = din("xh", [2 * NT, D])
    pc = din("pc", [NT, 256])
    w_in = din("w_in", [D, 5120])
    w_out = din("w_out", [D, D])
    w_gate = din("w_gate", [D, DFF])
    w_up = din("w_up", [D, DFF])
    w_down = din("w_down", [DFF, D])
    w_pg = din("w_pg", [D, D])
    w_pp = din("w_pp", [256, D])
    poolw_d = din("poolw_in", [128, 4, 128])
    gvec_d = din("gvec_in", [128, 68])
    biasT_d = din("biasT", [128, 4, 3, 256])
    maskc_d = din("maskc_in", [128, 256])
    hneg_d = din("hneg_in", [128, 1])
    rcnt_d = din("rcnt_in", [128, 4, 16])
    ident_d = din("identf_in", [128, 128])
    gfinb_d = din("gfinb_in", [128, D])
    out_d = nc.dram_tensor("out", [NT, D], F32, kind="ExternalOutput").ap()
    skind = "ExternalOutput" if debug else "Internal"
    xT_d = nc.dram_tensor("xT_d", [KC, 128, NT], F32, kind=skind).ap()
    xnT_d = nc.dram_tensor("xnT_d", [KC, 128, 2 * NT], BF16, kind=skind).ap()
    mixT_d = nc.dram_tensor("mixT_d", [KC, 128, NT], BF16, kind=skind).ap()

    def dview(t3, c0, c1):
        return t3[:, :, c0:c1].rearrange("k p n -> p k n")

    with ExitStack() as st:
        k = K(nc, st, needed)

        def sb(stack, name, shape, dt):
            return stack.enter_context(nc.sbuf_tensor(name, shape, dt))

        identf = sb(st, "identf", [128, 128], F32)
        identb = sb(st, "identb", [128, 128], BF16)
        ones = sb(st, "ones", [128, 128], BF16)
        epst = sb(st, "epst", [128, 1], F32)
        gvec = sb(st, "gvec", [128, 68], F32)
        CB = Buf("const")
        k.dma("sp", identf[:], ident_d, writes=[CB])
        k.dma("sp", gvec[:], gvec_d, writes=[CB])
        k.dma("pool", identb[:], ident_d, writes=[CB])
        k.op("dve", lambda: nc.vector.memset(ones[:], 1.0), writes=[CB])
        k.op("dve", lambda: nc.vector.memset(epst[:], EPS), writes=[CB])

        banks = []
        for i in range(8):
            t = st.enter_context(nc.psum_tensor(f"bank{i}", [128, 512], F32))
            banks.append((t, Buf(f"bank{i}")))
        psum = Ring(banks)
        ev = [0]

        def evac_eng():
            ev[0] += 1
            return "act" if ev[0] % 2 else "dve"

        def copy_op(eng, out, in_, reads, writes, scale=None):
            if eng == "act":
                if scale is None:
                    return k.op("act", lambda: nc.scalar.copy(out, in_), reads=reads, writes=writes)
                return k.op("act", lambda: nc.scalar.mul(out, in_, scale), reads=reads, writes=writes)
            if scale is None:
                return k.op("dve", lambda: nc.vector.tensor_copy(out, in_), reads=reads, writes=writes)
            return k.op("dve", lambda: nc.vector.tensor_scalar(out, in_, scale, None, ALU.mult),
                        reads=reads, writes=writes)

        def norm_fm(src, srcB, gcol, dst, dstB, ncols, sq, sqB, rsr, cstart=0):
            for s in range(ncols // 512):
                c0 = cstart + s * 512
                k.op("act", lambda: nc.scalar.activation(sq, src[:, :, c0:c0 + 512], AF.Square),
                     reads=srcB, writes=sqB)
                bank, bB = psum.next()
                for kc in range(KC):
                    k.op("pe", lambda kc=kc: nc.tensor.matmul(bank[:, :], ones[:], sq[:, kc, :],
                                                             start=(kc == 0), stop=(kc == KC - 1)),
                         reads=sqB + [CB], writes=[bB])
                rs, rsB = rsr.next()
                k.op("act", lambda: nc.scalar.activation(rs[:], bank[:, :], AF.Sqrt, bias=epst[:], scale=1.0 / D),
                     reads=[bB, CB], writes=[rsB])
                k.op("dve", lambda: nc.vector.reciprocal(rs[:], rs[:]), reads=[rsB], writes=[rsB])
                for kc in range(KC):
                    k.op("dve", lambda kc=kc: nc.vector.scalar_tensor_tensor(
                        out=dst[:, kc, c0:c0 + 512], in0=src[:, kc, c0:c0 + 512],
                        scalar=gvec[:, gcol + kc:gcol + kc + 1], in1=rs[:], op0=ALU.mult, op1=ALU.mult),
                        reads=[srcB[kc], rsB, CB], writes=[dstB[kc]])

        def norm_lite(src, srcB, gcol, dst, dstB, ncols, sqs, rsr, skip_sq=False, skip_hg=False):
            for kc in range(0 if skip_hg else KC):
                if kc % 2 == 0:
                    k.op("dve", lambda kc=kc: nc.vector.tensor_scalar(
                        dst[:, kc, 0:ncols], src[:, kc, 0:ncols], gvec[:, gcol + kc:gcol + kc + 1], None, ALU.mult),
                        reads=[srcB[kc], CB], writes=[dstB[kc]])
                else:
                    k.op("act", lambda kc=kc: nc.scalar.activation(
                        dst[:, kc, 0:ncols], src[:, kc, 0:ncols], AF.Copy, scale=gvec[:, gcol + kc:gcol + kc + 1]),
                        reads=[srcB[kc], CB], writes=[dstB[kc]])
            return sumsq_chain(src, srcB, ncols, sqs, rsr, skip_sq)

        def sumsq_chain(src, srcB, ncols, sqs, rsr, skip_sq=False):
            for s_ in range(0 if skip_sq else ncols // 512):
                sq, sqB = sqs[s_]
                k.op("act", lambda s_=s_, sq=sq: nc.scalar.activation(sq, src[:, :, s_ * 512:(s_ + 1) * 512], AF.Square),
                     reads=srcB, writes=sqB)

            def finish():
                out = []
                for s_ in range(ncols // 512):
                    sq, sqB = sqs[s_]
                    bank, bB = psum.next()
                    for kc in range(KC):
                        k.op("pe", lambda kc=kc, sq=sq, bank=bank: nc.tensor.matmul(
                            bank[:, :], ones[:], sq[:, kc, :], start=(kc == 0), stop=(kc == KC - 1)),
                            reads=sqB + [CB], writes=[bB])
                    rs, rsB = rsr.next()
                    k.op("act", lambda rs=rs, bank=bank: nc.scalar.activation(
                        rs[:], bank[:, :], AF.Sqrt, bias=epst[:], scale=1.0 / D), reads=[bB, CB], writes=[rsB])
                    k.op("dve", lambda rs=rs: nc.vector.reciprocal(rs[:], rs[:]), reads=[rsB], writes=[rsB])
                    out.append((rs, rsB))
                return out
            return finish

        with ExitStack() as s1:
            xt = [(sb(s1, f"xt{i}", [128, D], F32), Buf()) for i in range(4)]
            xTt = [(sb(s1, f"xTt{i}", [128, KC, 512], F32), [bufs(4) for _ in range(4)]) for i in range(3)]
            sqs = [(sb(s1, f"sq1_{i}", [128, KC, 512], BF16), bufs(4)) for i in range(2)]
            xnt = [(sb(s1, f"xnt{i}", [128, KC, 512], BF16), bufs(KC)) for i in range(2)]
            rsr = Ring([(sb(s1, f"rs1_{i}", [128, 512], F32), Buf()) for i in range(2)])
            def s1_transpose(i):
                xT, xTB = xTt[i % 3]
                sq, sqB = sqs[i % 2]
                for b in range(4):
                    gb = i * 4 + b
                    xb, xbB = xt[gb % 4]
                    k.dma("sp", xb[:], xh[gb * 128:(gb + 1) * 128, :], writes=[xbB])
                    for j in range(4):
                        bank, bB = psum.next()
                        for q in range(4):
                            kc = 4 * j + q
                            k.op("pe", lambda kc=kc, q=q: nc.tensor.transpose(
                                bank[:, q * 128:(q + 1) * 128], xb[:, kc * 128:(kc + 1) * 128], identf[:]),
                                reads=[xbB, CB], writes=[bB])
                        copy_op(evac_eng(), xT[:, 4 * j:4 * j + 4, b * 128:(b + 1) * 128],
                                bank[:, :].rearrange("p (q n) -> p q n", q=4),
                                reads=[bB], writes=[xTB[j][b]])
                if i >= 4:
                    k.dma("pool", dview(xT_d, (i - 4) * 512, (i - 3) * 512), xT[:],
                          reads=[xTB[j][b] for j in range(4) for b in range(4)])

            def s1_squares(i):
                xT, xTB = xTt[i % 3]
                sq, sqB = sqs[i % 2]
                for b in range(4):
                    k.op("act", lambda b=b: nc.scalar.activation(sq[:, :, b * 128:(b + 1) * 128],
                                                               xT[:, :, b * 128:(b + 1) * 128], AF.Square),
                         reads=[xTB[j][b] for j in range(4)], writes=[sqB[b]])

            def s1_norm(i):
                xT, xTB = xTt[i % 3]
                sq, sqB = sqs[i % 2]
                xn, xnB = xnt[i % 2]
                bank, bB = psum.next()
                for kc in range(KC):
                    k.op("pe", lambda kc=kc: nc.tensor.matmul(bank[:, :], ones[:], sq[:, kc, :],
                                                             start=(kc == 0), stop=(kc == KC - 1)),
                         reads=sqB + [CB], writes=[bB])
                rs, rsB = rsr.next()
                k.op("act", lambda: nc.scalar.activation(rs[:], bank[:, :], AF.Sqrt, bias=epst[:], scale=1.0 / D),
                     reads=[bB, CB], writes=[rsB])
                k.op("dve", lambda: nc.vector.reciprocal(rs[:], rs[:]), reads=[rsB], writes=[rsB])
                for kc in range(KC):
                    k.op("dve", lambda kc=kc: nc.vector.scalar_tensor_tensor(
                        out=xn[:, kc, :], in0=xT[:, kc, :], scalar=gvec[:, kc:kc + 1], in1=rs[:],
                        op0=ALU.mult, op1=ALU.mult), reads=xTB[kc // 4] + [rsB, CB], writes=[xnB[kc]])
                k.dma("pool", dview(xnT_d, i * 512, (i + 1) * 512), xn[:], reads=xnB)

            s1_transpose(0)
            s1_transpose(1)
            s1_squares(0)
            for i in range(8):
                if i + 2 < 8:
                    s1_transpose(i + 2)
                if i + 1 < 8:
                    s1_squares(i + 1)
                s1_norm(i)
        k.barrier()

        if stages >= 2:
          with ExitStack() as s2:
            xnr = [(sb(s2, f"xn2_{i}", [128, KC, 512], BF16), Buf()) for i in range(3)]
            wsl = sb(s2, "wsl", [128, KC, 1152], BF16)
            wslB = bufs(9)
            poolw = sb(s2, "poolw", [128, 4, 128], BF16)
            rcnt = sb(s2, "rcnt", [128, 4, 16], F32)
            maskc = sb(s2, "maskc", [128, 256], F32)
            hneg = sb(s2, "hneg", [128, 1], F32)
            C2 = Buf("c2")
            k.dma("pool", poolw[:], poolw_d, writes=[C2])
            k.dma("sp", rcnt[:], rcnt_d, writes=[C2])
            k.dma("sp", maskc[:], maskc_d, writes=[C2])
            k.dma("sp", hneg[:], hneg_d, writes=[C2])
            xn_i = [0]

            def load_xn(i):
                t, B = xnr[xn_i[0] % 3]
                xn_i[0] += 1
                k.dma("sp", t[:], dview(xnT_d, i * 512, (i + 1) * 512), writes=[B])
                return t, B

            passes = [(j, gs) for j in range(4) for gs in ((0, 1), (2,))]
            tile_seq = []
            for (j, gs) in passes:
                for i in range(0 if 2 in gs else 3, 8):
                    tile_seq.append([(0, dview(xnT_d, i * 512, (i + 1) * 512), KC, 512)])
            xns = WStream(k, [(t[:].rearrange("p a b -> p (a b)"), [B]) for (t, B) in xnr], tile_seq, q="sp")

            def load_w(j, g):
                base = 512 + 1152 * j
                for kind in range(3):
                    c = kind * 384 + g * 128
                    k.dma("pool", wsl[:, :, c:c + 128],
                          w_in[:, base + c: base + c + 128].rearrange("(k p) n -> p k n", p=128),
                          writes=[wslB[kind * 3 + g]])

            with ExitStack() as sp_:
                uT = sb(sp_, "uT", [128, 4, 2064], F32)
                uTB = [bufs(5) for _ in range(4)]
                T = [(sb(sp_, f"pT{i}", [128, 2064], F32), Buf()) for i in range(2)]
                pooled = sb(sp_, "pooled", [128, 4, 2048], BF16)
                pooledB = [bufs(4) for _ in range(4)]
                t16 = sb(sp_, "t16", [128, 16], F32)
                t16B = Buf()
                mo = [(sb(sp_, f"mo{i}", [128, 2048], BF16), Buf()) for i in range(4)]
                k.dma("pool", wsl[:, :, 0:512], w_in[:, 0:512].rearrange("(k p) n -> p k n", p=128), writes=wslB[0:4])

                def pool_tile_post(kt):
                    lo, hi = 16 + 512 * kt, 16 + 512 * (kt + 1)
                    for c in range(4):
                        w = 2 << c
                        rd = [uTB[c][kt + 1]] + ([uTB[c][kt]] if True else [])
                        cur, curB = uT[:, c, :], rd
                        for stp in range(c + 1):
                            sh = 1 << stp
                            st_ = lo - w + 2 * sh
                            dT, dB = T[stp % 2]
                            k.op("dve", lambda cur=cur, st_=st_, sh=sh, dT=dT: nc.vector.tensor_tensor(
                                out=dT[:, st_:hi], in0=cur[:, st_:hi], in1=cur[:, st_ - sh:hi - sh], op=ALU.add),
                                reads=curB, writes=[dB])
                            cur, curB = dT[:, :], [dB]
                        k.op("dve", lambda cur=cur: nc.vector.scalar_tensor_tensor(
                            out=pooled[:, c, lo - 16:hi - 16], in0=cur[:, lo:hi], scalar=1.0 / w, in1=uT[:, c, lo:hi],
                            op0=ALU.mult, op1=ALU.subtract), reads=curB + rd, writes=[pooledB[c][kt]])
                        if kt == 0:
                            k.op("dve", lambda cur=cur: nc.vector.tensor_tensor(
                                out=t16[:], in0=cur[:, 16:32], in1=rcnt[:, c, :], op=ALU.mult),
                                reads=curB + [C2], writes=[t16B])
                            k.op("dve", lambda: nc.vector.tensor_tensor(
                                out=pooled[:, c, 0:16], in0=t16[:], in1=uT[:, c, 16:32], op=ALU.subtract),
                                reads=[t16B] + rd, writes=[pooledB[c][kt]])
                def pool_tile_mix(kt):
                    lo, hi = 16 + 512 * kt, 16 + 512 * (kt + 1)
                    for c in range(4):
                        m, mB = mo[c]
                        bank, bB = psum.next()
                        k.op("pe", lambda: nc.tensor.matmul(bank[:, :], poolw[:, c, :],
                                                           pooled[:, c, lo - 16:hi - 16], start=True, stop=True),
                             reads=[pooledB[c][kt], C2], writes=[bB])
                        k.op("act", lambda: nc.scalar.activation(
                            m[:, lo - 16:hi - 16], bank[:, :], AF.Copy, scale=gvec[:, 64 + c:65 + c]),
                            reads=[bB, CB], writes=[mB])

                for i in range(3, 8):
                    xn, xnB = load_xn(i)
                    c0, n = (496, 16) if i == 3 else (0, 512)
                    off = 0 if i == 3 else 16 + (i - 4) * 512
                    for c in range(4):
                        bank, bB = psum.next()
                        for kc in range(KC):
                            k.op("pe", lambda kc=kc: nc.tensor.matmul(
                                bank[:, 0:n], wsl[:, kc, c * 128:(c + 1) * 128], xn[:, kc, c0:c0 + n],
                                start=(kc == 0), stop=(kc == KC - 1)), reads=[xnB] + wslB[0:4], writes=[bB])
                        copy_op(evac_eng(), uT[:, c, off:off + n], bank[:, 0:n], reads=[bB], writes=[uTB[c][i - 3]])
                    if i >= 5:
                        pool_tile_mix(i - 5)
                    if i >= 4:
                        pool_tile_post(i - 4)
                pool_tile_mix(3)
                for c in range(4):
                    k.dma("pool", mixT_d[c], mo[c][0][:], reads=[mo[c][1]])
                for g in range(3):
                    load_w(0, g)
                xns.prefetch()
            k.barrier()

            LQ = [NT // d for d in GD]
            LK = [(NT + h) // d for d, h in zip(GD, GH)]
            NBK = [l // 128 for l in LK]
            QT = sb(s2, "QT", [128, 3, NT], BF16)
            KT = [sb(s2, f"KT{g}", [128, GD[g] * LK[g]], BF16) for g in range(3)]
            VT = [sb(s2, f"VT{g}", [128, GD[g] * LK[g]], BF16) for g in range(3)]
            V = [sb(s2, f"V{g}", [128, GD[g] * NBK[g], 128], BF16) for g in range(3)]
            num = sb(s2, "num", [128, 3, NT], F32)
            densum = sb(s2, "densum", [128, NT], F32)
            ybf = [(sb(s2, f"ybf{i}", [128, NT], BF16), Buf()) for i in range(3)]
            sbr = Ring([(sb(s2, f"sbs{i}", [128, 512], F32), Buf()) for i in range(2)])
            ptr = Ring([(sb(s2, f"pts{i}", [128, 512], BF16), Buf()) for i in range(2)])
            bm = sb(s2, "bm", [128, 3, 2, 256], F32)
            bmB = bufs(3)
            QTB = [bufs(8) for _ in range(3)]
            KTB = [bufs(8) for _ in range(3)]
            VTB = [bufs(8) for _ in range(3)]
            VB = [Buf() for _ in range(3)]
            numB = bufs(3)
            denB = Buf()
            yi = [0]
            ps_chunk = Ring(banks[0:4])
            ps_S = Ring(banks[4:6])
            ps_N, ps_D = banks[6], banks[7]

            def build_bm(j, g):
                k.dma("sp", bm[:, g, 0, :], biasT_d[:, j, g, :], writes=[bmB[g]])
                k.op("dve", lambda: nc.vector.tensor_tensor(out=bm[:, g, 0, :], in0=bm[:, g, 0, :], in1=maskc[:], op=ALU.add),
                     reads=[C2], writes=[bmB[g]])
                k.op("dve", lambda: nc.vector.tensor_copy(bm[:, g, 1, 128:256], bm[:, g, 0, 128:256]),
                     reads=[bmB[g]], writes=[bmB[g]])
                k.op("dve", lambda: nc.vector.tensor_scalar(bm[:, g, 1, 0:128], bm[:, g, 0, 0:128], hneg[:, 0:1], None, ALU.add),
                     reads=[bmB[g], C2], writes=[bmB[g]])

            def pass_chunks(j, gs):
                tasks = []
                for i in range(0 if 2 in gs else 3, 8):
                    holder = {}
                    kinds = (1, 2) if i < 4 else (0, 1, 2)
                    combos = [(g, kind) for g in gs for kind in kinds]
                    for ki, (g, kind) in enumerate(combos):
                        def chunk(i=i, kind=kind, g=g, holder=holder, first=(ki == 0)):
                            d, H = GD[g], GH[g]
                            if first:
                                flat, xB = xns.get()
                                holder["xn"] = (v3(flat, 0, KC, 512), xB)
                            xn, xnB = holder["xn"]
                            if kind == 0:
                                c0, n, e0 = 0, 512, (i - 4) * 512
                                dst3 = QT[:, g, :].rearrange("p (r l) -> p r l", r=d)
                                dB = QTB[g][i]
                            else:
                                e0 = i * 512 - NT + H
                                c0, n = (0, 512) if e0 >= 0 else (-e0, 512 + e0)
                                e0 = max(e0, 0)
                                tt_ = KT[g] if kind == 1 else VT[g]
                                dst3 = tt_[:, :].rearrange("p (r l) -> p r l", r=d)
                                dB = (KTB if kind == 1 else VTB)[g][i]
                            wc = kind * 384 + g * 128
                            bank, bB = ps_chunk.next()
                            for kc in range(KC):
                                k.op("pe", lambda kc=kc: nc.tensor.matmul(
                                    bank[:, 0:n], wsl[:, kc, wc:wc + 128], xn[:, kc, c0:c0 + n],
                                    start=(kc == 0), stop=(kc == KC - 1)),
                                    reads=xnB + [wslB[kind * 3 + g]], writes=[bB])
                            src3 = bank[:, 0:n].rearrange("p (l r) -> p r l", r=d)
                            dsl = dst3[:, :, e0 // d:(e0 + n) // d]
                            if kind == 0:
                                copy_op("act", dsl, src3, [bB], [dB], scale=QSCALE)
                            elif kind == 1:
                                copy_op("dve", dsl, src3, [bB], [dB])
                            else:
                                copy_op(evac_eng(), dsl, src3, [bB], [dB])
                        tasks.append(chunk)
                return tasks

            def attn_tasks(j, g):
                tasks = []
                d, nqb, nbk, lq, lk = GD[g], LQ[g] // 128, NBK[g], LQ[g], LK[g]
                nb = d * nbk

                def vtrans(b0):
                    nn = min(4, nb - b0)
                    bank, bB = ps_chunk.next()
                    bv = bank[:, :].bitcast(BF16)
                    for q in range(nn):
                        b = b0 + q
                        k.op("pe", lambda b=b, q=q: nc.tensor.transpose(
                            bv[:, q * 128:(q + 1) * 128], VT[g][:, b * 128:(b + 1) * 128], identb[:]),
                            reads=VTB[g] + [CB], writes=[bB])
                    copy_op(evac_eng(), V[g][:, b0:b0 + nn, :].rearrange("p a b -> p (a b)"),
                            bv[:, 0:nn * 128], [bB], [VB[g]])

                for b0 in range(0, nb, 4):
                    tasks.append(lambda b0=b0: vtrans(b0))
                blocks = [(r, qb) for r in range(d) for qb in range(nqb)]
                pairs = [blocks[p:p + 2] for p in range(0, 16, 2)]
                state = {}

                def emit_S(pi):
                    sbk, sB = ps_S.next()
                    for xi, (r, qb) in enumerate(pairs[pi]):
                        qv = QT[:, g, r * lq + qb * 128: r * lq + (qb + 1) * 128]
                        for hh in range(2):
                            kv = KT[g][:, r * lk + (qb + hh) * 128: r * lk + (qb + hh + 1) * 128]
                            co = xi * 256 + hh * 128
                            k.op("pe", lambda kv=kv, qv=qv, co=co: nc.tensor.matmul(
                                sbk[:, co:co + 128], kv, qv, start=True, stop=True),
                                reads=QTB[g] + KTB[g], writes=[sB])
                    sbt, sbB = sbr.next()
                    for xi, (r, qb) in enumerate(pairs[pi]):
                        var = 1 if qb == 0 else 0
                        k.op("dve", lambda xi=xi, var=var: nc.vector.tensor_tensor(
                            out=sbt[:, xi * 256:(xi + 1) * 256], in0=sbk[:, xi * 256:(xi + 1) * 256],
                            in1=bm[:, g, var, :], op=ALU.add), reads=[sB, bmB[g]], writes=[sbB])
                    pt, ptB = ptr.next()
                    k.op("act", lambda: nc.scalar.activation(pt[:], sbt[:], AF.Exp), reads=[sbB], writes=[ptB])
                    state[pi] = (pt, ptB)

                def emit_PV(pi):
                    pt, ptB = state.pop(pi)
                    (nb_, nbB), (db_, dbB) = ps_N, ps_D
                    p0 = (pi % 2) * 2
                    for xi, (r, qb) in enumerate(pairs[pi]):
                        co = (p0 + xi) * 128
                        for hh in range(2):
                            vv = V[g][:, r * nbk + qb + hh, :]
                            pv = pt[:, xi * 256 + hh * 128: xi * 256 + (hh + 1) * 128]
                            k.op("pe", lambda vv=vv, pv=pv, co=co, hh=hh: nc.tensor.matmul(
                                nb_[:, co:co + 128], vv, pv, start=(hh == 0), stop=(hh == 1)),
                                reads=[ptB, VB[g]], writes=[nbB])
                        for hh in range(2):
                            pv = pt[:, xi * 256 + hh * 128: xi * 256 + (hh + 1) * 128]
                            k.op("pe", lambda pv=pv, co=co, hh=hh: nc.tensor.matmul(
                                db_[:, co:co + 128], ones[:], pv, start=(hh == 0), stop=(hh == 1)),
                                reads=[ptB, CB], writes=[dbB])
                    if pi % 2 == 0:
                        return
                    q0 = (pi // 2) * 4
                    if g == 0:
                        u = q0 // 4
                        nv = num[:, g, u * 512:(u + 1) * 512]
                        dv = densum[:, u * 512:(u + 1) * 512]
                        sn, sd = nb_[:, :], db_[:, :]
                    elif g == 1:
                        r = q0 // 4
                        nv = num[:, g, :].rearrange("p (b i r) -> p r b i", b=4, r=4)[:, r, :, :]
                        dv = densum[:, :].rearrange("p (b i r) -> p r b i", b=4, r=4)[:, r, :, :]
                        sn = nb_[:, :].rearrange("p (b i) -> p b i", b=4)
                        sd = db_[:, :].rearrange("p (b i) -> p b i", b=4)
                    else:
                        nv = num[:, g, :].rearrange("p (i r) -> p r i", r=16)[:, q0:q0 + 4, :]
                        dv = densum[:, :].rearrange("p (i r) -> p r i", r=16)[:, q0:q0 + 4, :]
                        sn = nb_[:, :].rearrange("p (b i) -> p b i", b=4)
                        sd = db_[:, :].rearrange("p (b i) -> p b i", b=4)
                    k.op("act", lambda: nc.scalar.copy(nv, sn), reads=[nbB], writes=[numB[g]])
                    if g == 0:
                        k.op("dve", lambda: nc.vector.tensor_copy(dv, sd), reads=[dbB], writes=[denB])
                    else:
                        k.op("dve", lambda: nc.vector.tensor_tensor(out=dv, in0=sd, in1=dv, op=ALU.add),
                             reads=[dbB], writes=[denB])

                tasks.append(lambda: emit_S(0))
                for pi in range(8):
                    if pi + 1 < 8:
                        tasks.append(lambda pi=pi: emit_S(pi + 1))
                    tasks.append(lambda pi=pi: emit_PV(pi))
                if j + 1 < 4:
                    tasks.append(lambda: build_bm(j + 1, g))
                if g == 2:
                    for qq in range(4):
                        tasks.append(lambda qq=qq: finalize(j, qq))
                return tasks

            ycur = {}

            def finalize(j, qq):
                c0, c1 = qq * 512, (qq + 1) * 512
                k.op("dve", lambda: nc.vector.reciprocal(densum[:, c0:c1], densum[:, c0:c1]), reads=[denB], writes=[denB])
                for g in range(3):
                    if qq == 0:
                        ycur[g] = ybf[yi[0] % 3]
                        yi[0] += 1
                    y, yB = ycur[g]
                    k.op("dve", lambda g=g, y=y: nc.vector.tensor_tensor(
                        out=y[:, c0:c1], in0=num[:, g, c0:c1], in1=densum[:, c0:c1], op=ALU.mult),
                        reads=[numB[g], denB], writes=[yB])
                    if qq == 3:
                        k.dma("pool", mixT_d[4 + 4 * g + j], y[:], reads=[yB])

            for g in range(3):
                build_bm(0, g)
            pending = []
            for (j, gs) in passes:
                chunks = pass_chunks(j, gs)
                done = 0
                for ci, ch in enumerate(chunks):
                    ch()
                    want = (len(pending) + done) * (ci + 1) // len(chunks)
                    while done < want and pending:
                        pending.pop(0)()
                        done += 1
                while pending:
                    pending.pop(0)()
                pending = []
                for g in gs:
                    if j + 1 < 4:
                        load_w(j + 1, g)
                    pending += attn_tasks(j, g)
            while pending:
                pending.pop(0)()
          k.barrier()

        if stages >= 3:
          with ExitStack() as s3:
            TT = 1024
            hT = sb(s3, "hT", [128, KC, TT], F32)
            hB = bufs(KC)
            hn = sb(s3, "hn", [128, KC, TT], BF16)
            hnB = bufs(KC)
            R = sb(s3, "R", [128, 22 * 1024], BF16)
            RB = bufs(22)
            wr = [(sb(s3, f"wr{i}", [128, 8192], BF16)[:, :], bufs(2)) for i in range(3)]
            rsr = Ring([(sb(s3, f"rs3_{i}", [128, 512], F32), Buf()) for i in range(2)])
            sgr = Ring([(sb(s3, f"sg{i}", [128, 512], F32), Buf()) for i in range(2)])
            tmr = Ring([(sb(s3, f"tm{i}", [128, 512], F32), Buf()) for i in range(3)])
            t1r = tmr
            pblk = Ring([(sb(s3, f"pb{i}", [128, 256], F32), Buf()) for i in range(2)])
            sq = R[:, 0:8192].rearrange("p (k n) -> p k n", k=KC)
            sqB = RB[0:8]
            sqs2 = [(sq, sqB), (R[:, 10240:18432].rearrange("p (k n) -> p k n", k=KC), RB[10:18])]
            gfb = R[:, 18432:22528].bitcast(F32)
            gfbB = RB[18:22]
            rscol = sb(s3, "rscol", [128, 16], F32)
            rscolB = Buf()
            act = R[:, :].rearrange("p (k n) -> p k n", k=22)
            pTb = R[:, 8192:10240].rearrange("p (k n) -> p k n", k=2)
            pTB = RB[8:10]
            yblk = [(R[:, 10240 + i * 4096: 10240 + (i + 1) * 4096].bitcast(F32), RB[10 + 4 * i: 14 + 4 * i]) for i in range(2)]

            def kview(w, r0, nk, c0, nc_):
                return w[r0:r0 + nk * 128, c0:c0 + nc_].rearrange("(k p) n -> p k n", p=128)

            halves = [(0, 22), (22, 22)]
            specs = []
            for tt in range(NT // TT):
                specs += [[(0, kview(w_out, 0, KC, a * 512, 512), KC, 512)] for a in range(4)]
                for (f0, nf) in halves:
                    for t in range(nf // 2):
                        specs.append([(0, kview(w_gate, 0, KC, (f0 + 2 * t) * 128, 256), KC, 256),
                                      (4096, kview(w_up, 0, KC, (f0 + 2 * t) * 128, 256), KC, 256)])
                    for c in range(8):
                        specs.append([(0, kview(w_down, f0 * 128, nf, c * 256, 256), nf, 256)])
                for a in range(8):
                    specs.append([(0, kview(w_pg, 0, KC, a * 256, 256), KC, 256),
                                  (4096, kview(w_pp, 0, 2, a * 256, 256), 2, 256)])
            ws = WStream(k, wr, specs)
            for tt in range(NT // TT):
                t0 = tt * TT

                if tt == 0:
                    k.dma("sp", hn[:], dview(mixT_d, t0, t0 + TT), writes=hnB)
                for kc4 in range(0, KC, 4):
                    k.dma("sp", hT[:, kc4:kc4 + 4, :], xT_d[kc4:kc4 + 4, :, t0:t0 + TT].rearrange("k p n -> p k n"),
                          writes=hB[kc4:kc4 + 4])

                def mm_group(wv, wB, ocol, src, srcB, nk):
                    bb = [psum.next(), psum.next()]
                    for kk in range(nk):
                        for s_ in range(2):
                            k.op("pe", lambda kk=kk, s_=s_: nc.tensor.matmul(
                                bb[s_][0][:, :], wv[:, kk, ocol:ocol + 128], src[:, kk, s_ * 512:(s_ + 1) * 512],
                                start=(kk == 0), stop=(kk == nk - 1)),
                                reads=wB + srcB, writes=[bb[s_][1]])
                    return bb

                def sq_fly(oc, s_):
                    sqv, sqvB = sqs2[s_]
                    hv = hT[:, oc, s_ * 512:(s_ + 1) * 512]
                    k.op("act", lambda: nc.scalar.activation(sqv[:, oc, :], hv, AF.Square),
                         reads=[hB[oc]], writes=[sqvB[oc // 2]])

                def add_into_h(oc, bb, sq_on=False, hg_col=None):
                    for s_ in range(2):
                        hv = hT[:, oc, s_ * 512:(s_ + 1) * 512]
                        k.op("dve", lambda hv=hv, s_=s_: nc.vector.tensor_tensor(out=hv, in0=bb[s_][0][:, :], in1=hv, op=ALU.add),
                             reads=[bb[s_][1]], writes=[hB[oc]])
                        if sq_on:
                            sq_fly(oc, s_)
                        if hg_col is not None:
                            k.op("act", lambda hv=hv, s_=s_: nc.scalar.activation(
                                hn[:, oc, s_ * 512:(s_ + 1) * 512], hv, AF.Copy, scale=gvec[:, hg_col + oc:hg_col + oc + 1]),
                                reads=[hB[oc], CB], writes=[hnB[oc]])

                for a in range(4):
                    wf, wB = ws.get()
                    wv = v3(wf, 0, KC, 512)
                    for o in range(4):
                        bb = mm_group(wv, wB, o * 128, hn, hnB, KC)
                        add_into_h(4 * a + o, bb, sq_on=True)
                finF = norm_lite(hT, hB, 16, hn, hnB, TT, sqs2, rsr, skip_sq=True)
                rsF = None
                for (f0, nf) in halves:
                    for t in range(nf // 2):
                        wf, wB = ws.get()
                        wg, wu = v3(wf, 0, KC, 256), v3(wf, 4096, KC, 256)
                        for o in range(2):
                            fl = 2 * t + o
                            gb_ = mm_group(wg, wB, o * 128, hn, hnB, KC)
                            ub_ = mm_group(wu, wB, o * 128, hn, hnB, KC)
                            if rsF is None:
                                rsF = finF()
                            for s_ in range(2):
                                rs, rsB = rsF[s_]
                                t1, t1B = tmr.next()
                                sg, sgB = sgr.next()
                                t2, t2B = tmr.next()
                                k.op("dve", lambda s_=s_, t1=t1, rs=rs: nc.vector.tensor_tensor(
                                    out=t1[:], in0=gb_[s_][0][:, :], in1=rs[:], op=ALU.mult),
                                    reads=[gb_[s_][1], rsB], writes=[t1B])
                                k.op("act", lambda t1=t1, sg=sg: nc.scalar.activation(sg[:], t1[:], AF.Silu),
                                     reads=[t1B], writes=[sgB])
                                k.op("dve", lambda s_=s_, t2=t2, rs=rs: nc.vector.tensor_tensor(
                                    out=t2[:], in0=ub_[s_][0][:, :], in1=rs[:], op=ALU.mult),
                                    reads=[ub_[s_][1], rsB], writes=[t2B])
                                k.op("dve", lambda s_=s_, sg=sg, t2=t2: nc.vector.tensor_tensor(
                                    out=act[:, fl, s_ * 512:(s_ + 1) * 512], in0=sg[:], in1=t2[:], op=ALU.mult),
                                    reads=[sgB, t2B], writes=[RB[fl]])
                    for c in range(8):
                        wf, wB = ws.get()
                        wd = v3(wf, 0, nf, 256)
                        for o in range(2):
                            bb = mm_group(wd, wB, o * 128, act, RB[0:nf], nf)
                            add_into_h(2 * c + o, bb, hg_col=(32 if f0 > 0 else None))
                k.dma("sp", gfb, gfinb_d, writes=gfbB)
                for b in range(TT // 128):
                    pb, pbB = pblk.next()
                    k.dma("sp", pb[:], pc[t0 + b * 128: t0 + (b + 1) * 128, :], writes=[pbB])
                    bank, bB = psum.next()
                    for q in range(2):
                        k.op("pe", lambda q=q: nc.tensor.transpose(bank[:, q * 128:(q + 1) * 128], pb[:, q * 128:(q + 1) * 128], identf[:]),
                             reads=[pbB, CB], writes=[bB])
                    copy_op(evac_eng(), pTb[:, :, b * 128:(b + 1) * 128], bank[:, 0:256].rearrange("p (q n) -> p q n", q=2),
                            [bB], pTB)
                finP = norm_lite(hT, hB, 32, hn, hnB, TT, sqs2, rsr, skip_hg=True)
                rsP = None
                for a in range(8):
                    wf, wB = ws.get()
                    wv, wpp = v3(wf, 0, KC, 256), v3(wf, 4096, 2, 256)
                    for o in range(2):
                        oc = 2 * a + o
                        gb_ = mm_group(wv, wB, o * 128, hn, hnB, KC)
                        pb_ = mm_group(wpp, wB, o * 128, pTb, pTB, 2)
                        if rsP is None:
                            rsP = finP()
                        for s_ in range(2):
                            sg, sgB = sgr.next()
                            tm, tmB = tmr.next()
                            hv = hT[:, oc, s_ * 512:(s_ + 1) * 512]
                            rs, rsB = rsP[s_]
                            t1, t1B = t1r.next()
                            k.op("dve", lambda s_=s_, t1=t1, rs=rs: nc.vector.tensor_tensor(
                                out=t1[:], in0=gb_[s_][0][:, :], in1=rs[:], op=ALU.mult),
                                reads=[gb_[s_][1], rsB], writes=[t1B])
                            k.op("act", lambda t1=t1, sg=sg: nc.scalar.activation(sg[:], t1[:], AF.Sigmoid),
                                 reads=[t1B], writes=[sgB])
                            k.op("dve", lambda s_=s_, sg=sg, tm=tm: nc.vector.tensor_tensor(
                                out=tm[:], in0=sg[:], in1=pb_[s_][0][:, :], op=ALU.mult),
                                reads=[sgB, pb_[s_][1]], writes=[tmB])
                            k.op("dve", lambda hv=hv, tm=tm: nc.vector.tensor_tensor(out=hv, in0=tm[:], in1=hv, op=ALU.add),
                                 reads=[tmB], writes=[hB[oc]])
                            sq_fly(oc, s_)
                if tt + 1 < NT // TT:
                    k.dma("sp", hn[:], dview(mixT_d, t0 + TT, t0 + 2 * TT), writes=hnB)
                tbank, tB = psum.next()
                for b in range(TT // 128):
                    sqv, sqvB = sqs2[b // 4]
                    c0 = (b % 4) * 128
                    for kc in range(KC):
                        k.op("pe", lambda kc=kc, b=b, c0=c0, sqv=sqv: nc.tensor.matmul(
                            tbank[:, 2 * b:2 * b + 2], sqv[:, kc, c0:c0 + 128], ones[:, 0:2],
                            start=(kc == 0), stop=(kc == KC - 1)), reads=sqvB + [CB], writes=[tB])
                k.op("act", lambda: nc.scalar.activation(rscol[:], tbank[:, 0:16], AF.Sqrt, bias=epst[:], scale=1.0 / D),
                     reads=[tB, CB], writes=[rscolB])
                k.op("dve", lambda: nc.vector.reciprocal(rscol[:], rscol[:]), reads=[rscolB], writes=[rscolB])
                for b in range(TT // 128):
                    yb, ybB = yblk[b % 2]
                    for jq in range(4):
                        bank, bB = psum.next()
                        for q in range(4):
                            kc = 4 * jq + q
                            k.op("pe", lambda kc=kc, q=q: nc.tensor.transpose(
                                bank[:, q * 128:(q + 1) * 128], hT[:, kc, b * 128:(b + 1) * 128], identf[:]),
                                reads=[hB[kc], CB], writes=[bB])
                        k.op("dve", lambda jq=jq, b=b, bank=bank, yb=yb: nc.vector.scalar_tensor_tensor(
                            out=yb[:, jq * 512:(jq + 1) * 512], in0=bank[:, :], scalar=rscol[:, 2 * b:2 * b + 1],
                            in1=gfb[:, jq * 512:(jq + 1) * 512], op0=ALU.mult, op1=ALU.mult),
                            reads=[bB, rscolB] + gfbB, writes=ybB)
                    k.dma("sp", out_d[t0 + b * 128: t0 + (b + 1) * 128, :], yb, reads=ybB)
          k.barrier()

        for t in k.all_tokens():
            k._wait("sp", t)
        used = k.used
    return nc, used


def _bucket(n):
    n = np.asarray(n, dtype=np.int32)
    nf = np.maximum(n, 1).astype(np.float32)
    large = 16 + (np.log(nf / np.float32(16)) / np.float32(np.log(2048 / 16)) * np.float32(16)).astype(np.int32)
    large = np.minimum(large, 31)
    return np.where(n < 16, n, large)


def _host_inputs(inp):
    f = lambda a: np.ascontiguousarray(np.asarray(a, dtype=np.float32))
    x = f(inp["x"])
    p = f(inp["p"])[0]
    rel_bias = f(inp["rel_bias"])
    w_in = f(inp["w_in"])[0]
    cols = list(range(512))
    for j in range(4):
        for kind in range(3):
            for g in range(3):
                h = 4 * g + j
                c = 512 + kind * 1536 + h * 128
                cols.extend(range(c, c + 128))
    w_in_p = np.ascontiguousarray(w_in[:, cols])
    gv = np.concatenate([
        f(inp["norm_mix_g"])[0].reshape(16, 128).T, f(inp["norm_ffn_g"])[0].reshape(16, 128).T,
        f(inp["norm_ple_g"])[0].reshape(16, 128).T, f(inp["final_norm_g"]).reshape(16, 128).T,
        f(inp["pool_scale"])[0].reshape(4, 128).T], axis=1)
    gv = np.ascontiguousarray(gv)
    poolw = np.ascontiguousarray(f(inp["pool_w"])[0].transpose(1, 0, 2))
    ki = np.arange(256)[:, None]
    q = np.arange(128)[None, :]
    off = q + 128 - ki
    valid = (off >= 0) & (off <= 128)
    biasT = np.zeros((128, 4, 3, 256), np.float32)
    for g in range(3):
        bk = _bucket(np.maximum(off, 0) * GD[g])
        for j in range(4):
            tb = rel_bias[bk, 4 * g + j]
            biasT[:, j, g, 0:128] = tb[0:128]
            biasT[:, j, g, 128:256] = tb[128:256]
    maskc = np.where(valid, 0.0, NEG).astype(np.float32)
    maskc = np.ascontiguousarray(np.concatenate([maskc[0:128], maskc[128:256]], axis=1))
    common = {
        "w_in": w_in_p, "w_out": f(inp["w_out"])[0], "w_gate": f(inp["w_gate"])[0], "w_up": f(inp["w_up"])[0],
        "w_down": f(inp["w_down"])[0], "w_pg": f(inp["w_ple_gate"])[0], "w_pp": f(inp["w_ple_proj"])[0],
        "poolw_in": poolw, "gvec_in": gv, "biasT": biasT, "maskc_in": maskc, "identf_in": np.eye(128, dtype=np.float32),
        "gfinb_in": np.ascontiguousarray(np.broadcast_to(f(inp["final_norm_g"])[None, :], (128, D))),
    }
    maps = []
    for c in range(8):
        b, qd = c // 4, c % 4
        s = qd * NT
        xh = np.zeros((2 * NT, D), np.float32)
        if qd > 0:
            xh[0:NT] = x[b, s - NT:s]
        xh[NT:] = x[b, s:s + NT]
        hneg = np.full((128, 1), 0.0 if qd > 0 else NEG, np.float32)
        rc = np.zeros((128, 4, 16), np.float32)
        for gi, w in enumerate((2, 4, 8, 16)):
            pos = s + np.arange(16) + 1
            rc[:, gi, :] = (1.0 / np.minimum(pos, w)).astype(np.float32)[None, :]
        m = dict(common)
        m.update({"xh": xh, "pc": np.ascontiguousarray(p[b, s:s + NT]), "hneg_in": hneg, "rcnt_in": rc})
        maps.append(m)
    return maps


_NC_CACHE = {}


def _get_nc():
    if "nc" not in _NC_CACHE:
        _, used = build(None)
        nc, _ = build(used)
        _NC_CACHE["nc"] = nc
    return _NC_CACHE["nc"]


def kernel(**inputs):
    maps = _host_inputs(inputs)
    nc = _get_nc()
    res = run_bass_kernel_spmd(nc, maps, core_ids=list(range(8)))
    out = np.zeros((2, 4 * NT, D), np.float32)
    for c in range(8):
        b, qd = c // 4, c % 4
        out[b, qd * NT:(qd + 1) * NT] = np.asarray(res.results[c]["out"], dtype=np.float32)
    return out
```

```python
import numpy as np
from contextlib import ExitStack
import concourse.bass as bass
import concourse.mybir as mybir
from concourse.bass_utils import run_bass_kernel_spmd

F32 = mybir.dt.float32
BF16 = mybir.dt.bfloat16
AF = mybir.ActivationFunctionType
ALU = mybir.AluOpType

COMPUTE = ("pe", "act", "dve", "pool")
EPOCH = 12000
NDMASEM = 20

NT = 2048
D = 2048
KC = 16
DFF = 5632
EPS = 1e-6
QSCALE = float(128 ** -0.5)
NEG = -30000.0
GD = (1, 4, 16)
GH = (128, 512, 2048)


class Tok:
    __slots__ = ("eng", "id", "sem", "val", "dma")

    def __init__(self, eng, id_, dma=False):
        self.eng = eng
        self.id = id_
        self.sem = None
        self.val = 0
        self.dma = dma


class Buf:
    __slots__ = ("name", "w", "r")

    def __init__(self, name=""):
        self.name = name
        self.w = None
        self.r = {}


def bufs(n):
    return [Buf() for _ in range(n)]


class K:
    def __init__(self, nc, stack, needed=None):
        self.nc = nc
        self.stack = stack
        self.needed = needed
        self.used = set()
        self.opid = 0
        self.engs = {"pe": nc.tensor, "act": nc.scalar, "dve": nc.vector,
                     "pool": nc.gpsimd, "sp": nc.sync}
        self.cur = {}
        self.nsem = 0
        for e in COMPUTE:
            self.cur[e] = [self._newsem(e), 0]
        self.waited = {e: {p: 0 for p in COMPUTE} for e in self.engs}
        self.waited_dma = {e: {} for e in self.engs}
        self.dsem = {}
        for q in ("sp", "pool"):
            self.dsem[q] = [[self._newsem(f"d{q}{i}"), 0] for i in range(NDMASEM)]
        self.dptr = {q: 0 for q in self.dsem}
        self.last = {e: None for e in COMPUTE}
        self.dma_toks = {}

    def _newsem(self, name):
        self.nsem += 1
        return self.stack.enter_context(self.nc.semaphore(f"s{self.nsem}_{name}"))

    def _wait(self, eng, t):
        if t is None:
            return
        if t.dma:
            key = id(t.sem)
            if self.waited_dma[eng].get(key, 0) >= t.val:
                return
            self.engs[eng].wait_ge(t.sem, t.val)
            self.waited_dma[eng][key] = t.val
            return
        if t.eng == eng and eng == "pe":
            return
        if self.waited[eng][t.eng] >= t.id:
            return
        self.used.add(t.id)
        self.waited[eng][t.eng] = t.id
        if t.sem is None:
            raise RuntimeError("dependency on an unsignalled op (two-pass mismatch)")
        self.engs[eng].wait_ge(t.sem, t.val)

    @staticmethod
    def _deps(reads, writes, extra):
        deps = []
        for b in reads:
            if b.w is not None:
                deps.append(b.w)
        for b in writes:
            if b.w is not None:
                deps.append(b.w)
            deps.extend(b.r.values())
        deps.extend(extra)
        return deps

    @staticmethod
    def _mark(tok, reads, writes, key):
        for b in reads:
            b.r[key] = tok
        for b in writes:
            b.w = tok
            b.r = {}

    def op(self, eng, fn, reads=(), writes=(), extra=()):
        for t in self._deps(reads, writes, extra):
            self._wait(eng, t)
        inst = fn()
        self.opid += 1
        tok = Tok(eng, self.opid)
        if self.needed is None or tok.id in self.needed:
            c = self.cur[eng]
            if c[1] >= EPOCH:
                c[0] = self._newsem(eng)
                c[1] = 0
            c[1] += 1
            inst.then_inc(c[0], 1)
            tok.sem = c[0]
            tok.val = c[1]
        self.last[eng] = tok
        self._mark(tok, reads, writes, eng)
        return tok

    def dma(self, q, out, in_, reads=(), writes=(), extra=(), **kw):
        for t in self._deps(reads, writes, extra):
            self._wait(q, t)
        ring = self.dsem[q]
        slot = ring[self.dptr[q] % NDMASEM]
        self.dptr[q] += 1
        sem, cnt = slot
        if cnt > 0:
            key = id(sem)
            if self.waited_dma[q].get(key, 0) < cnt:
                self.engs[q].wait_ge(sem, cnt)
                self.waited_dma[q][key] = cnt
        inst = self.engs[q].dma_start(out=out, in_=in_, **kw)
        slot[1] = cnt + 16
        inst.then_inc(sem, 16)
        self.opid += 1
        tok = Tok(q, self.opid, dma=True)
        tok.sem = sem
        tok.val = cnt + 16
        self._mark(tok, reads, writes, ("dma", tok.id))
        self.dma_toks[id(sem)] = tok
        return tok

    def all_tokens(self):
        return [t for t in self.last.values() if t is not None] + list(self.dma_toks.values())

    def barrier(self):
        toks = self.all_tokens()
        for e in ("pe", "act", "dve", "pool", "sp"):
            for t in toks:
                self._wait(e, t)


class Ring:
    def __init__(self, items):
        self.items = items
        self.i = 0

    def next(self):
        it = self.items[self.i % len(self.items)]
        self.i += 1
        return it


class WStream:
    def __init__(self, k, bufs_, specs, q="pool"):
        self.k = k
        self.q = q
        self.bufs = bufs_
        self.specs = specs
        self.loaded = 0
        self.i = 0
        self.first_depth = len(bufs_)

    def _load(self, j):
        flat, B = self.bufs[j % len(self.bufs)]
        for pi, (off, src, a, b) in enumerate(self.specs[j]):
            view = flat[:, off:off + a * b].rearrange("p (a b) -> p a b", a=a)
            self.k.dma(self.q, view, src, writes=[B[pi]])

    def prefetch(self):
        n = len(self.bufs)
        while self.loaded < min(len(self.specs), self.i + n):
            self._load(self.loaded)
            self.loaded += 1

    def get(self):
        n = len(self.bufs) if self.i > 0 else min(len(self.bufs), self.first_depth)
        while self.loaded < min(len(self.specs), self.i + n):
            self._load(self.loaded)
            self.loaded += 1
        flat, B = self.bufs[self.i % n]
        self.i += 1
        return flat, B


def v3(flat, off, a, b):
    return flat[:, off:off + a * b].rearrange("p (a b) -> p a b", a=a)


def build(needed=None, debug=False, stages=3):
    nc = bass.Bass("TRN2", target_bir_lowering=False)

    def din(name, shape):
        return nc.dram_tensor(name, shape, F32, kind="ExternalInput").ap()

    xh = din("xh", [2 * NT, D])
    pc = din("pc", [NT, 256])
    w_in = din("w_in", [D, 5120])
    w_out = din("w_out", [D, D])
    w_gate = din("w_gate", [D, DFF])
    w_up = din("w_up", [D, DFF])
    w_down = din("w_down", [DFF, D])
    w_pg = din("w_pg", [D, D])
    w_pp = din("w_pp", [256, D])
    poolw_d = din("poolw_in", [128, 4, 128])
    gvec_d = din("gvec_in", [128, 68])
    biasT_d = din("biasT", [128, 4, 3, 256])
    maskc_d = din("maskc_in", [128, 256])
    hneg_d = din("hneg_in", [128, 1])
    rcnt_d = din("rcnt_in", [128, 4, 16])
    ident_d = din("identf_in", [128, 128])
    gfinb_d = din("gfinb_in", [128, D])
    out_d = nc.dram_tensor("out", [NT, D], F32, kind="ExternalOutput").ap()
    skind = "ExternalOutput" if debug else "Internal"
    xT_d = nc.dram_tensor("xT_d", [KC, 128, NT], F32, kind=skind).ap()
    xnT_d = nc.dram_tensor("xnT_d", [KC, 128, 2 * NT], BF16, kind=skind).ap()
    mixT_d = nc.dram_tensor("mixT_d", [KC, 128, NT], BF16, kind=skind).ap()

    def dview(t3, c0, c1):
        return t3[:, :, c0:c1].rearrange("k p n -> p k n")

    with ExitStack() as st:
        k = K(nc, st, needed)

        def sb(stack, name, shape, dt):
            return stack.enter_context(nc.sbuf_tensor(name, shape, dt))

        identf = sb(st, "identf", [128, 128], F32)
        identb = sb(st, "identb", [128, 128], BF16)
        ones = sb(st, "ones", [128, 128], BF16)
        epst = sb(st, "epst", [128, 1], F32)
        gvec = sb(st, "gvec", [128, 68], F32)
        CB = Buf("const")
        k.dma("sp", identf[:], ident_d, writes=[CB])
        k.dma("sp", gvec[:], gvec_d, writes=[CB])
        k.dma("pool", identb[:], ident_d, writes=[CB])
        k.op("dve", lambda: nc.vector.memset(ones[:], 1.0), writes=[CB])
        k.op("dve", lambda: nc.vector.memset(epst[:], EPS), writes=[CB])

        banks = []
        for i in range(8):
            t = st.enter_context(nc.psum_tensor(f"bank{i}", [128, 512], F32))
            banks.append((t, Buf(f"bank{i}")))
        psum = Ring(banks)
        ev = [0]

        def evac_eng():
            ev[0] += 1
            return "act" if ev[0] % 2 else "dve"

        def copy_op(eng, out, in_, reads, writes, scale=None):
            if eng == "act":
                if scale is None:
                    return k.op("act", lambda: nc.scalar.copy(out, in_), reads=reads, writes=writes)
                return k.op("act", lambda: nc.scalar.mul(out, in_, scale), reads=reads, writes=writes)
            if scale is None:
                return k.op("dve", lambda: nc.vector.tensor_copy(out, in_), reads=reads, writes=writes)
            return k.op("dve", lambda: nc.vector.tensor_scalar(out, in_, scale, None, ALU.mult),
                        reads=reads, writes=writes)

        def norm_fm(src, srcB, gcol, dst, dstB, ncols, sq, sqB, rsr, cstart=0):
            for s in range(ncols // 512):
                c0 = cstart + s * 512
                k.op("act", lambda: nc.scalar.activation(sq, src[:, :, c0:c0 + 512], AF.Square),
                     reads=srcB, writes=sqB)
                bank, bB = psum.next()
                for kc in range(KC):
                    k.op("pe", lambda kc=kc: nc.tensor.matmul(bank[:, :], ones[:], sq[:, kc, :],
                                                             start=(kc == 0), stop=(kc == KC - 1)),
                         reads=sqB + [CB], writes=[bB])
                rs, rsB = rsr.next()
                k.op("act", lambda: nc.scalar.activation(rs[:], bank[:, :], AF.Sqrt, bias=epst[:], scale=1.0 / D),
                     reads=[bB, CB], writes=[rsB])
                k.op("dve", lambda: nc.vector.reciprocal(rs[:], rs[:]), reads=[rsB], writes=[rsB])
                for kc in range(KC):
                    k.op("dve", lambda kc=kc: nc.vector.scalar_tensor_tensor(
                        out=dst[:, kc, c0:c0 + 512], in0=src[:, kc, c0:c0 + 512],
                        scalar=gvec[:, gcol + kc:gcol + kc + 1], in1=rs[:], op0=ALU.mult, op1=ALU.mult),
                        reads=[srcB[kc], rsB, CB], writes=[dstB[kc]])

        def norm_lite(src, srcB, gcol, dst, dstB, ncols, sqs, rsr, skip_sq=False, skip_hg=False):
            for kc in range(0 if skip_hg else KC):
                if kc % 2 == 0:
                    k.op("dve", lambda kc=kc: nc.vector.tensor_scalar(
                        dst[:, kc, 0:ncols], src[:, kc, 0:ncols], gvec[:, gcol + kc:gcol + kc + 1], None, ALU.mult),
                        reads=[srcB[kc], CB], writes=[dstB[kc]])
                else:
                    k.op("act", lambda kc=kc: nc.scalar.activation(
                        dst[:, kc, 0:ncols], src[:, kc, 0:ncols], AF.Copy, scale=gvec[:, gcol + kc:gcol + kc + 1]),
                        reads=[srcB[kc], CB], writes=[dstB[kc]])
            return sumsq_chain(src, srcB, ncols, sqs, rsr, skip_sq)

        def sumsq_chain(src, srcB, ncols, sqs, rsr, skip_sq=False):
            for s_ in range(0 if skip_sq else ncols // 512):
                sq, sqB = sqs[s_]
                k.op("act", lambda s_=s_, sq=sq: nc.scalar.activation(sq, src[:, :, s_ * 512:(s_ + 1) * 512], AF.Square),
                     reads=srcB, writes=sqB)

            def finish():
                out = []
                for s_ in range(ncols // 512):
                    sq, sqB = sqs[s_]
                    bank, bB = psum.next()
                    for kc in range(KC):
                        k.op("pe", lambda kc=kc, sq=sq, bank=bank: nc.tensor.matmul(
                            bank[:, :], ones[:], sq[:, kc, :], start=(kc == 0), stop=(kc == KC - 1)),
                            reads=sqB + [CB], writes=[bB])
                    rs, rsB = rsr.next()
                    k.op("act", lambda rs=rs, bank=bank: nc.scalar.activation(
                        rs[:], bank[:, :], AF.Sqrt, bias=epst[:], scale=1.0 / D), reads=[bB, CB], writes=[rsB])
                    k.op("dve", lambda rs=rs: nc.vector.reciprocal(rs[:], rs[:]), reads=[rsB], writes=[rsB])
                    out.append((rs, rsB))
                return out
            return finish

        with ExitStack() as s1:
            xt = [(sb(s1, f"xt{i}", [128, D], F32), Buf()) for i in range(4)]
            xTt = [(sb(s1, f"xTt{i}", [128, KC, 512], F32), [bufs(4) for _ in range(4)]) for i in range(3)]
            sqs = [(sb(s1, f"sq1_{i}", [128, KC, 512], BF16), bufs(4)) for i in range(2)]
            xnt = [(sb(s1, f"xnt{i}", [128, KC, 512], BF16), bufs(KC)) for i in range(2)]
            rsr = Ring([(sb(s1, f"rs1_{i}", [128, 512], F32), Buf()) for i in range(2)])
            def s1_transpose(i):
                xT, xTB = xTt[i % 3]
                sq, sqB = sqs[i % 2]
                for b in range(4):
                    gb = i * 4 + b
                    xb, xbB = xt[gb % 4]
                    k.dma("sp", xb[:], xh[gb * 128:(gb + 1) * 128, :], writes=[xbB])
                    for j in range(4):
                        bank, bB = psum.next()
                        for q in range(4):
                            kc = 4 * j + q
                            k.op("pe", lambda kc=kc, q=q: nc.tensor.transpose(
                                bank[:, q * 128:(q + 1) * 128], xb[:, kc * 128:(kc + 1) * 128], identf[:]),
                                reads=[xbB, CB], writes=[bB])
                        copy_op(evac_eng(), xT[:, 4 * j:4 * j + 4, b * 128:(b + 1) * 128],
                                bank[:, :].rearrange("p (q n) -> p q n", q=4),
                                reads=[bB], writes=[xTB[j][b]])
                if i >= 4:
                    k.dma("pool", dview(xT_d, (i - 4) * 512, (i - 3) * 512), xT[:],
                          reads=[xTB[j][b] for j in range(4) for b in range(4)])

            def s1_squares(i):
                xT, xTB = xTt[i % 3]
                sq, sqB = sqs[i % 2]
                for b in range(4):
                    k.op("act", lambda b=b: nc.scalar.activation(sq[:, :, b * 128:(b + 1) * 128],
                                                               xT[:, :, b * 128:(b + 1) * 128], AF.Square),
                         reads=[xTB[j][b] for j in range(4)], writes=[sqB[b]])

            def s1_norm(i):
                xT, xTB = xTt[i % 3]
                sq, sqB = sqs[i % 2]
                xn, xnB = xnt[i % 2]
                bank, bB = psum.next()
                for kc in range(KC):
                    k.op("pe", lambda kc=kc: nc.tensor.matmul(bank[:, :], ones[:], sq[:, kc, :],
                                                             start=(kc == 0), stop=(kc == KC - 1)),
                         reads=sqB + [CB], writes=[bB])
                rs, rsB = rsr.next()
                k.op("act", lambda: nc.scalar.activation(rs[:], bank[:, :], AF.Sqrt, bias=epst[:], scale=1.0 / D),
                     reads=[bB, CB], writes=[rsB])
                k.op("dve", lambda: nc.vector.reciprocal(rs[:], rs[:]), reads=[rsB], writes=[rsB])
                for kc in range(KC):
                    k.op("dve", lambda kc=kc: nc.vector.scalar_tensor_tensor(
                        out=xn[:, kc, :], in0=xT[:, kc, :], scalar=gvec[:, kc:kc + 1], in1=rs[:],
                        op0=ALU.mult, op1=ALU.mult), reads=xTB[kc // 4] + [rsB, CB], writes=[xnB[kc]])
                k.dma("pool", dview(xnT_d, i * 512, (i + 1) * 512), xn[:], reads=xnB)

            s1_transpose(0)
            s1_transpose(1)
            s1_squares(0)
            for i in range(8):
                if i + 2 < 8:
                    s1_transpose(i + 2)
                if i + 1 < 8:
                    s1_squares(i + 1)
                s1_norm(i)
        k.barrier()

        if stages >= 2:
          with ExitStack() as s2:
            xnr = [(sb(s2, f"xn2_{i}", [128, KC, 512], BF16), Buf()) for i in range(3)]
            wsl = sb(s2, "wsl", [128, KC, 1152], BF16)
            wslB = bufs(9)
            poolw = sb(s2, "poolw", [128, 4, 128], BF16)
            rcnt = sb(s2, "rcnt", [128, 4, 16], F32)
            maskc = sb(s2, "maskc", [128, 256], F32)
            hneg = sb(s2, "hneg", [128, 1], F32)
            C2 = Buf("c2")
            k.dma("pool", poolw[:], poolw_d, writes=[C2])
            k.dma("sp", rcnt[:], rcnt_d, writes=[C2])
            k.dma("sp", maskc[:], maskc_d, writes=[C2])
            k.dma("sp", hneg[:], hneg_d, writes=[C2])
            xn_i = [0]

            def load_xn(i):
                t, B = xnr[xn_i[0] % 3]
                xn_i[0] += 1
                k.dma("sp", t[:], dview(xnT_d, i * 512, (i + 1) * 512), writes=[B])
                return t, B

            passes = [(j, gs) for j in range(4) for gs in ((0, 1), (2,))]
            tile_seq = []
            for (j, gs) in passes:
                for i in range(0 if 2 in gs else 3, 8):
                    tile_seq.append([(0, dview(xnT_d, i * 512, (i + 1) * 512), KC, 512)])
            xns = WStream(k, [(t[:].rearrange("p a b -> p (a b)"), [B]) for (t, B) in xnr], tile_seq, q="sp")

            def load_w(j, g):
                base = 512 + 1152 * j
                for kind in range(3):
                    c = kind * 384 + g * 128
                    k.dma("pool", wsl[:, :, c:c + 128],
                          w_in[:, base + c: base + c + 128].rearrange("(k p) n -> p k n", p=128),
                          writes=[wslB[kind * 3 + g]])

            with ExitStack() as sp_:
                uT = sb(sp_, "uT", [128, 4, 2064], F32)
                uTB = [bufs(5) for _ in range(4)]
                T = [(sb(sp_, f"pT{i}", [128, 2064], F32), Buf()) for i in range(2)]
                pooled = sb(sp_, "pooled", [128, 4, 2048], BF16)
                pooledB = [bufs(4) for _ in range(4)]
                t16 = sb(sp_, "t16", [128, 16], F32)
                t16B = Buf()
                mo = [(sb(sp_, f"mo{i}", [128, 2048], BF16), Buf()) for i in range(4)]
                k.dma("pool", wsl[:, :, 0:512], w_in[:, 0:512].rearrange("(k p) n -> p k n", p=128), writes=wslB[0:4])

                def pool_tile_post(kt):
                    lo, hi = 16 + 512 * kt, 16 + 512 * (kt + 1)
                    for c in range(4):
                        w = 2 << c
                        rd = [uTB[c][kt + 1]] + ([uTB[c][kt]] if True else [])
                        cur, curB = uT[:, c, :], rd
                        for stp in range(c + 1):
                            sh = 1 << stp
                            st_ = lo - w + 2 * sh
                            dT, dB = T[stp % 2]
                            k.op("dve", lambda cur=cur, st_=st_, sh=sh, dT=dT: nc.vector.tensor_tensor(
                                out=dT[:, st_:hi], in0=cur[:, st_:hi], in1=cur[:, st_ - sh:hi - sh], op=ALU.add),
                                reads=curB, writes=[dB])
                            cur, curB = dT[:, :], [dB]
                        k.op("dve", lambda cur=cur: nc.vector.scalar_tensor_tensor(
                            out=pooled[:, c, lo - 16:hi - 16], in0=cur[:, lo:hi], scalar=1.0 / w, in1=uT[:, c, lo:hi],
                            op0=ALU.mult, op1=ALU.subtract), reads=curB + rd, writes=[pooledB[c][kt]])
                        if kt == 0:
                            k.op("dve", lambda cur=cur: nc.vector.tensor_tensor(
                                out=t16[:], in0=cur[:, 16:32], in1=rcnt[:, c, :], op=ALU.mult),
                                reads=curB + [C2], writes=[t16B])
                            k.op("dve", lambda: nc.vector.tensor_tensor(
                                out=pooled[:, c, 0:16], in0=t16[:], in1=uT[:, c, 16:32], op=ALU.subtract),
                                reads=[t16B] + rd, writes=[pooledB[c][kt]])
                def pool_tile_mix(kt):
                    lo, hi = 16 + 512 * kt, 16 + 512 * (kt + 1)
                    for c in range(4):
                        m, mB = mo[c]
                        bank, bB = psum.next()
                        k.op("pe", lambda: nc.tensor.matmul(bank[:, :], poolw[:, c, :],
                                                           pooled[:, c, lo - 16:hi - 16], start=True, stop=True),
                             reads=[pooledB[c][kt], C2], writes=[bB])
                        k.op("act", lambda: nc.scalar.activation(
                            m[:, lo - 16:hi - 16], bank[:, :], AF.Copy, scale=gvec[:, 64 + c:65 + c]),
                            reads=[bB, CB], writes=[mB])

                for i in range(3, 8):
                    xn, xnB = load_xn(i)
                    c0, n = (496, 16) if i == 3 else (0, 512)
                    off = 0 if i == 3 else 16 + (i - 4) * 512
                    for c in range(4):
                        bank, bB = psum.next()
                        for kc in range(KC):
                            k.op("pe", lambda kc=kc: nc.tensor.matmul(
                                bank[:, 0:n], wsl[:, kc, c * 128:(c + 1) * 128], xn[:, kc, c0:c0 + n],
                                start=(kc == 0), stop=(kc == KC - 1)), reads=[xnB] + wslB[0:4], writes=[bB])
                        copy_op(evac_eng(), uT[:, c, off:off + n], bank[:, 0:n], reads=[bB], writes=[uTB[c][i - 3]])
                    if i >= 5:
                        pool_tile_mix(i - 5)
                    if i >= 4:
                        pool_tile_post(i - 4)
                pool_tile_mix(3)
                for c in range(4):
                    k.dma("pool", mixT_d[c], mo[c][0][:], reads=[mo[c][1]])
                for g in range(3):
                    load_w(0, g)
                xns.prefetch()
            k.barrier()

            LQ = [NT // d for d in GD]
            LK = [(NT + h) // d for d, h in zip(GD, GH)]
            NBK = [l // 128 for l in LK]
            QT = sb(s2, "QT", [128, 3, NT], BF16)
            KT = [sb(s2, f"KT{g}", [128, GD[g] * LK[g]], BF16) for g in range(3)]
            VT = [sb(s2, f"VT{g}", [128, GD[g] * LK[g]], BF16) for g in range(3)]
            V = [sb(s2, f"V{g}", [128, GD[g] * NBK[g], 128], BF16) for g in range(3)]
            num = sb(s2, "num", [128, 3, NT], F32)
            densum = sb(s2, "densum", [128, NT], F32)
            ybf = [(sb(s2, f"ybf{i}", [128, NT], BF16), Buf()) for i in range(3)]
            sbr = Ring([(sb(s2, f"sbs{i}", [128, 512], F32), Buf()) for i in range(2)])
            ptr = Ring([(sb(s2, f"pts{i}", [128, 512], BF16), Buf()) for i in range(2)])
            bm = sb(s2, "bm", [128, 3, 2, 256], F32)
            bmB = bufs(3)
            QTB = [bufs(8) for _ in range(3)]
            KTB = [bufs(8) for _ in range(3)]
            VTB = [bufs(8) for _ in range(3)]
            VB = [Buf() for _ in range(3)]
            numB = bufs(3)
            denB = Buf()
            yi = [0]
            ps_chunk = Ring(banks[0:4])
            ps_S = Ring(banks[4:6])
            ps_N, ps_D = banks[6], banks[7]

            def build_bm(j, g):
                k.dma("sp", bm[:, g, 0, :], biasT_d[:, j, g, :], writes=[bmB[g]])
                k.op("dve", lambda: nc.vector.tensor_tensor(out=bm[:, g, 0, :], in0=bm[:, g, 0, :], in1=maskc[:], op=ALU.add),
                     reads=[C2], writes=[bmB[g]])
                k.op("dve", lambda: nc.vector.tensor_copy(bm[:, g, 1, 128:256], bm[:, g, 0, 128:256]),
                     reads=[bmB[g]], writes=[bmB[g]])
                k.op("dve", lambda: nc.vector.tensor_scalar(bm[:, g, 1, 0:128], bm[:, g, 0, 0:128], hneg[:, 0:1], None, ALU.add),
                     reads=[bmB[g], C2], writes=[bmB[g]])

            def pass_chunks(j, gs):
                tasks = []
                for i in range(0 if 2 in gs else 3, 8):
                    holder = {}
                    kinds = (1, 2) if i < 4 else (0, 1, 2)
                    combos = [(g, kind) for g in gs for kind in kinds]
                    for ki, (g, kind) in enumerate(combos):
                        def chunk(i=i, kind=kind, g=g, holder=holder, first=(ki == 0)):
                            d, H = GD[g], GH[g]
                            if first:
                                flat, xB = xns.get()
                                holder["xn"] = (v3(flat, 0, KC, 512), xB)
                            xn, xnB = holder["xn"]
                            if kind == 0:
                                c0, n, e0 = 0, 512, (i - 4) * 512
                                dst3 = QT[:, g, :].rearrange("p (r l) -> p r l", r=d)
                                dB = QTB[g][i]
                            else:
                                e0 = i * 512 - NT + H
                                c0, n = (0, 512) if e0 >= 0 else (-e0, 512 + e0)
                                e0 = max(e0, 0)
                                tt_ = KT[g] if kind == 1 else VT[g]
                                dst3 = tt_[:, :].rearrange("p (r l) -> p r l", r=d)
                                dB = (KTB if kind == 1 else VTB)[g][i]
                            wc = kind * 384 + g * 128
                            bank, bB = ps_chunk.next()
                            for kc in range(KC):
                                k.op("pe", lambda kc=kc: nc.tensor.matmul(
                                    bank[:, 0:n], wsl[:, kc, wc:wc + 128], xn[:, kc, c0:c0 + n],
                                    start=(kc == 0), stop=(kc == KC - 1)),
                                    reads=xnB + [wslB[kind * 3 + g]], writes=[bB])
                            src3 = bank[:, 0:n].rearrange("p (l r) -> p r l", r=d)
                            dsl = dst3[:, :, e0 // d:(e0 + n) // d]
                            if kind == 0:
                                copy_op("act", dsl, src3, [bB], [dB], scale=QSCALE)
                            elif kind == 1:
                                copy_op("dve", dsl, src3, [bB], [dB])
                            else:
                                copy_op(evac_eng(), dsl, src3, [bB], [dB])
                        tasks.append(chunk)
                return tasks

            def attn_tasks(j, g):
                tasks = []
                d, nqb, nbk, lq, lk = GD[g], LQ[g] // 128, NBK[g], LQ[g], LK[g]
                nb = d * nbk

                def vtrans(b0):
                    nn = min(4, nb - b0)
                    bank, bB = ps_chunk.next()
                    bv = bank[:, :].bitcast(BF16)
                    for q in range(nn):
                        b = b0 + q
                        k.op("pe", lambda b=b, q=q: nc.tensor.transpose(
                            bv[:, q * 128:(q + 1) * 128], VT[g][:, b * 128:(b + 1) * 128], identb[:]),
                            reads=VTB[g] + [CB], writes=[bB])
                    copy_op(evac_eng(), V[g][:, b0:b0 + nn, :].rearrange("p a b -> p (a b)"),
                            bv[:, 0:nn * 128], [bB], [VB[g]])

                for b0 in range(0, nb, 4):
                    tasks.append(lambda b0=b0: vtrans(b0))
                blocks = [(r, qb) for r in range(d) for qb in range(nqb)]
                pairs = [blocks[p:p + 2] for p in range(0, 16, 2)]
                state = {}

                def emit_S(pi):
                    sbk, sB = ps_S.next()
                    for xi, (r, qb) in enumerate(pairs[pi]):
                        qv = QT[:, g, r * lq + qb * 128: r * lq + (qb + 1) * 128]
                        for hh in range(2):
                            kv = KT[g][:, r * lk + (qb + hh) * 128: r * lk + (qb + hh + 1) * 128]
                            co = xi * 256 + hh * 128
                            k.op("pe", lambda kv=kv, qv=qv, co=co: nc.tensor.matmul(
                                sbk[:, co:co + 128], kv, qv, start=True, stop=True),
                                reads=QTB[g] + KTB[g], writes=[sB])
                    sbt, sbB = sbr.next()
                    for xi, (r, qb) in enumerate(pairs[pi]):
                        var = 1 if qb == 0 else 0
                        k.op("dve", lambda xi=xi, var=var: nc.vector.tensor_tensor(
                            out=sbt[:, xi * 256:(xi + 1) * 256], in0=sbk[:, xi * 256:(xi + 1) * 256],
                            in1=bm[:, g, var, :], op=ALU.add), reads=[sB, bmB[g]], writes=[sbB])
                    pt, ptB = ptr.next()
                    k.op("act", lambda: nc.scalar.activation(pt[:], sbt[:], AF.Exp), reads=[sbB], writes=[ptB])
                    state[pi] = (pt, ptB)

                def emit_PV(pi):
                    pt, ptB = state.pop(pi)
                    (nb_, nbB), (db_, dbB) = ps_N, ps_D
                    p0 = (pi % 2) * 2
                    for xi, (r, qb) in enumerate(pairs[pi]):
                        co = (p0 + xi) * 128
                        for hh in range(2):
                            vv = V[g][:, r * nbk + qb + hh, :]
                            pv = pt[:, xi * 256 + hh * 128: xi * 256 + (hh + 1) * 128]
                            k.op("pe", lambda vv=vv, pv=pv, co=co, hh=hh: nc.tensor.matmul(
                                nb_[:, co:co + 128], vv, pv, start=(hh == 0), stop=(hh == 1)),
                                reads=[ptB, VB[g]], writes=[nbB])
                        for hh in range(2):
                            pv = pt[:, xi * 256 + hh * 128: xi * 256 + (hh + 1) * 128]
                            k.op("pe", lambda pv=pv, co=co, hh=hh: nc.tensor.matmul(
                                db_[:, co:co + 128], ones[:], pv, start=(hh == 0), stop=(hh == 1)),
                                reads=[ptB, CB], writes=[dbB])
                    if pi % 2 == 0:
                        return
                    q0 = (pi // 2) * 4
                    if g == 0:
                        u = q0 // 4
                        nv = num[:, g, u * 512:(u + 1) * 512]
                        dv = densum[:, u * 512:(u + 1) * 512]
                        sn, sd = nb_[:, :], db_[:, :]
                    elif g == 1:
                        r = q0 // 4
                        nv = num[:, g, :].rearrange("p (b i r) -> p r b i", b=4, r=4)[:, r, :, :]
                        dv = densum[:, :].rearrange("p (b i r) -> p r b i", b=4, r=4)[:, r, :, :]
                        sn = nb_[:, :].rearrange("p (b i) -> p b i", b=4)
                        sd = db_[:, :].rearrange("p (b i) -> p b i", b=4)
                    else:
                        nv = num[:, g, :].rearrange("p (i r) -> p r i", r=16)[:, q0:q0 + 4, :]
                        dv = densum[:, :].rearrange("p (i r) -> p r i", r=16)[:, q0:q0 + 4, :]
                        sn = nb_[:, :].rearrange("p (b i) -> p b i", b=4)
                        sd = db_[:, :].rearrange("p (b i) -> p b i", b=4)
                    k.op("act", lambda: nc.scalar.copy(nv, sn), reads=[nbB], writes=[numB[g]])
                    if g == 0:
                        k.op("dve", lambda: nc.vector.tensor_copy(dv, sd), reads=[dbB], writes=[denB])
                    else:
                        k.op("dve", lambda: nc.vector.tensor_tensor(out=dv, in0=sd, in1=dv, op=ALU.add),
                             reads=[dbB], writes=[denB])

                tasks.append(lambda: emit_S(0))
                for pi in range(8):
                    if pi + 1 < 8:
                        tasks.append(lambda pi=pi: emit_S(pi + 1))
                    tasks.append(lambda pi=pi: emit_PV(pi))
                if j + 1 < 4:
                    tasks.append(lambda: build_bm(j + 1, g))
                if g == 2:
                    for qq in range(4):
                        tasks.append(lambda qq=qq: finalize(j, qq))
                return tasks

            ycur = {}

            def finalize(j, qq):
                c0, c1 = qq * 512, (qq + 1) * 512
                k.op("dve", lambda: nc.vector.reciprocal(densum[:, c0:c1], densum[:, c0:c1]), reads=[denB], writes=[denB])
                for g in range(3):
                    if qq == 0:
                        ycur[g] = ybf[yi[0] % 3]
                        yi[0] += 1
                    y, yB = ycur[g]
                    k.op("dve", lambda g=g, y=y: nc.vector.tensor_tensor(
                        out=y[:, c0:c1], in0=num[:, g, c0:c1], in1=densum[:, c0:c1], op=ALU.mult),
                        reads=[numB[g], denB], writes=[yB])
                    if qq == 3:
                        k.dma("pool", mixT_d[4 + 4 * g + j], y[:], reads=[yB])

            for g in range(3):
                build_bm(0, g)
            pending = []
            for (j, gs) in passes:
                chunks = pass_chunks(j, gs)
                done = 0
                for ci, ch in enumerate(chunks):
                    ch()
                    want = (len(pending) + done) * (ci + 1) // len(chunks)
                    while done < want and pending:
                        pending.pop(0)()
                        done += 1
                while pending:
                    pending.pop(0)()
                pending = []
                for g in gs:
                    if j + 1 < 4:
                        load_w(j + 1, g)
                    pending += attn_tasks(j, g)
            while pending:
                pending.pop(0)()
          k.barrier()

        if stages >= 3:
          with ExitStack() as s3:
            TT = 1024
            hT = sb(s3, "hT", [128, KC, TT], F32)
            hB = bufs(KC)
            hn = sb(s3, "hn", [128, KC, TT], BF16)
            hnB = bufs(KC)
            R = sb(s3, "R", [128, 22 * 1024], BF16)
            RB = bufs(22)
            wr = [(sb(s3, f"wr{i}", [128, 8192], BF16)[:, :], bufs(2)) for i in range(3)]
            rsr = Ring([(sb(s3, f"rs3_{i}", [128, 512], F32), Buf()) for i in range(2)])
            sgr = Ring([(sb(s3, f"sg{i}", [128, 512], F32), Buf()) for i in range(2)])
            tmr = Ring([(sb(s3, f"tm{i}", [128, 512], F32), Buf()) for i in range(3)])
            t1r = tmr
            pblk = Ring([(sb(s3, f"pb{i}", [128, 256], F32), Buf()) for i in range(2)])
            sq = R[:, 0:8192].rearrange("p (k n) -> p k n", k=KC)
            sqB = RB[0:8]
            sqs2 = [(sq, sqB), (R[:, 10240:18432].rearrange("p (k n) -> p k n", k=KC), RB[10:18])]
            gfb = R[:, 18432:22528].bitcast(F32)
            gfbB = RB[18:22]
            rscol = sb(s3, "rscol", [128, 16], F32)
            rscolB = Buf()
            act = R[:, :].rearrange("p (k n) -> p k n", k=22)
            pTb = R[:, 8192:10240].rearrange("p (k n) -> p k n", k=2)
            pTB = RB[8:10]
            yblk = [(R[:, 10240 + i * 4096: 10240 + (i + 1) * 4096].bitcast(F32), RB[10 + 4 * i: 14 + 4 * i]) for i in range(2)]

            def kview(w, r0, nk, c0, nc_):
                return w[r0:r0 + nk * 128, c0:c0 + nc_].rearrange("(k p) n -> p k n", p=128)

            halves = [(0, 22), (22, 22)]
            specs = []
            for tt in range(NT // TT):
                specs += [[(0, kview(w_out, 0, KC, a * 512, 512), KC, 512)] for a in range(4)]
                for (f0, nf) in halves:
                    for t in range(nf // 2):
                        specs.append([(0, kview(w_gate, 0, KC, (f0 + 2 * t) * 128, 256), KC, 256),
                                      (4096, kview(w_up, 0, KC, (f0 + 2 * t) * 128, 256), KC, 256)])
                    for c in range(8):
                        specs.append([(0, kview(w_down, f0 * 128, nf, c * 256, 256), nf, 256)])
                for a in range(8):
                    specs.append([(0, kview(w_pg, 0, KC, a * 256, 256), KC, 256),
                                  (4096, kview(w_pp, 0, 2, a * 256, 256), 2, 256)])
            ws = WStream(k, wr, specs)
            ws.first_depth = 2
            for tt in range(NT // TT):
                t0 = tt * TT

                if tt == 0:
                    k.dma("sp", hn[:], dview(mixT_d, t0, t0 + TT), writes=hnB)
                def load_hT(gi, extra=()):
                    kc4 = 4 * gi
                    k.dma("sp", hT[:, kc4:kc4 + 4, :], xT_d[kc4:kc4 + 4, :, t0:t0 + TT].rearrange("k p n -> p k n"),
                          writes=hB[kc4:kc4 + 4], extra=extra)

                load_hT(0)
                load_hT(1)

                def mm_group(wv, wB, ocol, src, srcB, nk):
                    bb = [psum.next(), psum.next()]
                    for kk in range(nk):
                        for s_ in range(2):
                            k.op("pe", lambda kk=kk, s_=s_: nc.tensor.matmul(
                                bb[s_][0][:, :], wv[:, kk, ocol:ocol + 128], src[:, kk, s_ * 512:(s_ + 1) * 512],
                                start=(kk == 0), stop=(kk == nk - 1)),
                                reads=wB + srcB, writes=[bb[s_][1]])
                    return bb

                def sq_fly(oc, s_):
                    sqv, sqvB = sqs2[s_]
                    hv = hT[:, oc, s_ * 512:(s_ + 1) * 512]
                    k.op("act", lambda: nc.scalar.activation(sqv[:, oc, :], hv, AF.Square),
                         reads=[hB[oc]], writes=[sqvB[oc // 2]])

                def add_into_h(oc, bb, sq_on=False, hg_col=None):
                    for s_ in range(2):
                        hv = hT[:, oc, s_ * 512:(s_ + 1) * 512]
                        k.op("dve", lambda hv=hv, s_=s_: nc.vector.tensor_tensor(out=hv, in0=bb[s_][0][:, :], in1=hv, op=ALU.add),
                             reads=[bb[s_][1]], writes=[hB[oc]])
                        if sq_on:
                            sq_fly(oc, s_)
                        if hg_col is not None:
                            k.op("act", lambda hv=hv, s_=s_: nc.scalar.activation(
                                hn[:, oc, s_ * 512:(s_ + 1) * 512], hv, AF.Copy, scale=gvec[:, hg_col + oc:hg_col + oc + 1]),
                                reads=[hB[oc], CB], writes=[hnB[oc]])

                for a in range(4):
                    wf, wB = ws.get()
                    wv = v3(wf, 0, KC, 512)
                    if a + 2 < 4:
                        load_hT(a + 2, extra=[k.last["pe"]] if k.last["pe"] is not None else ())
                    for o in range(4):
                        bb = mm_group(wv, wB, o * 128, hn, hnB, KC)
                        add_into_h(4 * a + o, bb, sq_on=True)
                finF = norm_lite(hT, hB, 16, hn, hnB, TT, sqs2, rsr, skip_sq=True)
                rsF = None
                for (f0, nf) in halves:
                    for t in range(nf // 2):
                        wf, wB = ws.get()
                        wg, wu = v3(wf, 0, KC, 256), v3(wf, 4096, KC, 256)
                        for o in range(2):
                            fl = 2 * t + o
                            gb_ = mm_group(wg, wB, o * 128, hn, hnB, KC)
                            ub_ = mm_group(wu, wB, o * 128, hn, hnB, KC)
                            if rsF is None:
                                rsF = finF()
                            for s_ in range(2):
                                rs, rsB = rsF[s_]
                                t1, t1B = tmr.next()
                                sg, sgB = sgr.next()
                                t2, t2B = tmr.next()
                                k.op("dve", lambda s_=s_, t1=t1, rs=rs: nc.vector.tensor_tensor(
                                    out=t1[:], in0=gb_[s_][0][:, :], in1=rs[:], op=ALU.mult),
                                    reads=[gb_[s_][1], rsB], writes=[t1B])
                                k.op("act", lambda t1=t1, sg=sg: nc.scalar.activation(sg[:], t1[:], AF.Silu),
                                     reads=[t1B], writes=[sgB])
                                k.op("dve", lambda s_=s_, t2=t2, rs=rs: nc.vector.tensor_tensor(
                                    out=t2[:], in0=ub_[s_][0][:, :], in1=rs[:], op=ALU.mult),
                                    reads=[ub_[s_][1], rsB], writes=[t2B])
                                k.op("dve", lambda s_=s_, sg=sg, t2=t2: nc.vector.tensor_tensor(
                                    out=act[:, fl, s_ * 512:(s_ + 1) * 512], in0=sg[:], in1=t2[:], op=ALU.mult),
                                    reads=[sgB, t2B], writes=[RB[fl]])
                    for c in range(8):
                        wf, wB = ws.get()
                        wd = v3(wf, 0, nf, 256)
                        for o in range(2):
                            bb = mm_group(wd, wB, o * 128, act, RB[0:nf], nf)
                            add_into_h(2 * c + o, bb, hg_col=(32 if f0 > 0 else None))
                k.dma("sp", gfb, gfinb_d, writes=gfbB)
                for b in range(TT // 128):
                    pb, pbB = pblk.next()
                    k.dma("sp", pb[:], pc[t0 + b * 128: t0 + (b + 1) * 128, :], writes=[pbB])
                    bank, bB = psum.next()
                    for q in range(2):
                        k.op("pe", lambda q=q: nc.tensor.transpose(bank[:, q * 128:(q + 1) * 128], pb[:, q * 128:(q + 1) * 128], identf[:]),
                             reads=[pbB, CB], writes=[bB])
                    copy_op(evac_eng(), pTb[:, :, b * 128:(b + 1) * 128], bank[:, 0:256].rearrange("p (q n) -> p q n", q=2),
                            [bB], pTB)
                finP = norm_lite(hT, hB, 32, hn, hnB, TT, sqs2, rsr, skip_hg=True)
                rsP = None
                for a in range(8):
                    wf, wB = ws.get()
                    wv, wpp = v3(wf, 0, KC, 256), v3(wf, 4096, 2, 256)
                    for o in range(2):
                        oc = 2 * a + o
                        gb_ = mm_group(wv, wB, o * 128, hn, hnB, KC)
                        pb_ = mm_group(wpp, wB, o * 128, pTb, pTB, 2)
                        if rsP is None:
                            rsP = finP()
                        for s_ in range(2):
                            sg, sgB = sgr.next()
                            tm, tmB = tmr.next()
                            hv = hT[:, oc, s_ * 512:(s_ + 1) * 512]
                            rs, rsB = rsP[s_]
                            t1, t1B = t1r.next()
                            k.op("dve", lambda s_=s_, t1=t1, rs=rs: nc.vector.tensor_tensor(
                                out=t1[:], in0=gb_[s_][0][:, :], in1=rs[:], op=ALU.mult),
                                reads=[gb_[s_][1], rsB], writes=[t1B])
                            k.op("act", lambda t1=t1, sg=sg: nc.scalar.activation(sg[:], t1[:], AF.Sigmoid),
                                 reads=[t1B], writes=[sgB])
                            k.op("dve", lambda s_=s_, sg=sg, tm=tm: nc.vector.tensor_tensor(
                                out=tm[:], in0=sg[:], in1=pb_[s_][0][:, :], op=ALU.mult),
                                reads=[sgB, pb_[s_][1]], writes=[tmB])
                            k.op("dve", lambda hv=hv, tm=tm: nc.vector.tensor_tensor(out=hv, in0=tm[:], in1=hv, op=ALU.add),
                                 reads=[tmB], writes=[hB[oc]])
                            sq_fly(oc, s_)
                if tt + 1 < NT // TT:
                    k.dma("sp", hn[:], dview(mixT_d, t0 + TT, t0 + 2 * TT), writes=hnB)
                tbank, tB = psum.next()
                for b in range(TT // 128):
                    sqv, sqvB = sqs2[b // 4]
                    c0 = (b % 4) * 128
                    for kc in range(KC):
                        k.op("pe", lambda kc=kc, b=b, c0=c0, sqv=sqv: nc.tensor.matmul(
                            tbank[:, 2 * b:2 * b + 2], sqv[:, kc, c0:c0 + 128], ones[:, 0:2],
                            start=(kc == 0), stop=(kc == KC - 1)), reads=sqvB + [CB], writes=[tB])
                k.op("act", lambda: nc.scalar.activation(rscol[:], tbank[:, 0:16], AF.Sqrt, bias=epst[:], scale=1.0 / D),
                     reads=[tB, CB], writes=[rscolB])
                k.op("dve", lambda: nc.vector.reciprocal(rscol[:], rscol[:]), reads=[rscolB], writes=[rscolB])
                for b in range(TT // 128):
                    yb, ybB = yblk[b % 2]
                    for jq in range(4):
                        bank, bB = psum.next()
                        for q in range(4):
                            kc = 4 * jq + q
                            k.op("pe", lambda kc=kc, q=q: nc.tensor.transpose(
                                bank[:, q * 128:(q + 1) * 128], hT[:, kc, b * 128:(b + 1) * 128], identf[:]),
                                reads=[hB[kc], CB], writes=[bB])
                        k.op("dve", lambda jq=jq, b=b, bank=bank, yb=yb: nc.vector.scalar_tensor_tensor(
                            out=yb[:, jq * 512:(jq + 1) * 512], in0=bank[:, :], scalar=rscol[:, 2 * b:2 * b + 1],
                            in1=gfb[:, jq * 512:(jq + 1) * 512], op0=ALU.mult, op1=ALU.mult),
                            reads=[bB, rscolB] + gfbB, writes=ybB)
                    k.dma("sp", out_d[t0 + b * 128: t0 + (b + 1) * 128, :], yb, reads=ybB)
          k.barrier()

        for t in k.all_tokens():
            k._wait("sp", t)
        used = k.used
    return nc, used


def _bucket(n):
    n = np.asarray(n, dtype=np.int32)
    nf = np.maximum(n, 1).astype(np.float32)
    large = 16 + (np.log(nf / np.float32(16)) / np.float32(np.log(2048 / 16)) * np.float32(16)).astype(np.int32)
    large = np.minimum(large, 31)
    return np.where(n < 16, n, large)


def _host_inputs(inp):
    f = lambda a: np.ascontiguousarray(np.asarray(a, dtype=np.float32))
    x = f(inp["x"])
    p = f(inp["p"])[0]
    rel_bias = f(inp["rel_bias"])
    w_in = f(inp["w_in"])[0]
    cols = list(range(512))
    for j in range(4):
        for kind in range(3):
            for g in range(3):
                h = 4 * g + j
                c = 512 + kind * 1536 + h * 128
                cols.extend(range(c, c + 128))
    w_in_p = np.ascontiguousarray(w_in[:, cols])
    gv = np.concatenate([
        f(inp["norm_mix_g"])[0].reshape(16, 128).T, f(inp["norm_ffn_g"])[0].reshape(16, 128).T,
        f(inp["norm_ple_g"])[0].reshape(16, 128).T, f(inp["final_norm_g"]).reshape(16, 128).T,
        f(inp["pool_scale"])[0].reshape(4, 128).T], axis=1)
    gv = np.ascontiguousarray(gv)
    poolw = np.ascontiguousarray(f(inp["pool_w"])[0].transpose(1, 0, 2))
    ki = np.arange(256)[:, None]
    q = np.arange(128)[None, :]
    off = q + 128 - ki
    valid = (off >= 0) & (off <= 128)
    biasT = np.zeros((128, 4, 3, 256), np.float32)
    for g in range(3):
        bk = _bucket(np.maximum(off, 0) * GD[g])
        for j in range(4):
            tb = rel_bias[bk, 4 * g + j]
            biasT[:, j, g, 0:128] = tb[0:128]
            biasT[:, j, g, 128:256] = tb[128:256]
    maskc = np.where(valid, 0.0, NEG).astype(np.float32)
    maskc = np.ascontiguousarray(np.concatenate([maskc[0:128], maskc[128:256]], axis=1))
    common = {
        "w_in": w_in_p, "w_out": f(inp["w_out"])[0], "w_gate": f(inp["w_gate"])[0], "w_up": f(inp["w_up"])[0],
        "w_down": f(inp["w_down"])[0], "w_pg": f(inp["w_ple_gate"])[0], "w_pp": f(inp["w_ple_proj"])[0],
        "poolw_in": poolw, "gvec_in": gv, "biasT": biasT, "maskc_in": maskc, "identf_in": np.eye(128, dtype=np.float32),
        "gfinb_in": np.ascontiguousarray(np.broadcast_to(f(inp["final_norm_g"])[None, :], (128, D))),
    }
    maps = []
    for c in range(8):
        b, qd = c // 4, c % 4
        s = qd * NT
        xh = np.zeros((2 * NT, D), np.float32)
        if qd > 0:
            xh[0:NT] = x[b, s - NT:s]
        xh[NT:] = x[b, s:s + NT]
        hneg = np.full((128, 1), 0.0 if qd > 0 else NEG, np.float32)
        rc = np.zeros((128, 4, 16), np.float32)
        for gi, w in enumerate((2, 4, 8, 16)):
            pos = s + np.arange(16) + 1
            rc[:, gi, :] = (1.0 / np.minimum(pos, w)).astype(np.float32)[None, :]
        m = dict(common)
        m.update({"xh": xh, "pc": np.ascontiguousarray(p[b, s:s + NT]), "hneg_in": hneg, "rcnt_in": rc})
        maps.append(m)
    return maps


_NC_CACHE = {}


def _get_nc():
    if "nc" not in _NC_CACHE:
        _, used = build(None)
        nc, _ = build(used)
        _NC_CACHE["nc"] = nc
    return _NC_CACHE["nc"]


def kernel(**inputs):
    maps = _host_inputs(inputs)
    nc = _get_nc()
    res = run_bass_kernel_spmd(nc, maps, core_ids=list(range(8)))
    out = np.zeros((2, 4 * NT, D), np.float32)
    for c in range(8):
        b, qd = c // 4, c % 4
        out[b, qd * NT:(qd + 1) * NT] = np.asarray(res.results[c]["out"], dtype=np.float32)
    return out
```

```python
import numpy as np
from contextlib import ExitStack
import concourse.bass as bass
import concourse.mybir as mybir
from concourse.bass_utils import run_bass_kernel_spmd

F32 = mybir.dt.float32
BF16 = mybir.dt.bfloat16
AF = mybir.ActivationFunctionType
ALU = mybir.AluOpType

COMPUTE = ("pe", "act", "dve", "pool")
EPOCH = 12000
NDMASEM = 20

NT = 2048
D = 2048
KC = 16
DFF = 5632
EPS = 1e-6
QSCALE = float(128 ** -0.5)
NEG = -30000.0
GD = (1, 4, 16)
GH = (128, 512, 2048)


class Tok:
    __slots__ = ("eng", "id", "sem", "val", "dma")

    def __init__(self, eng, id_, dma=False):
        self.eng = eng
        self.id = id_
        self.sem = None
        self.val = 0
        self.dma = dma


class Buf:
    __slots__ = ("name", "w", "r")

    def __init__(self, name=""):
        self.name = name
        self.w = None
        self.r = {}


def bufs(n):
    return [Buf() for _ in range(n)]


class K:
    def __init__(self, nc, stack, needed=None):
        self.nc = nc
        self.stack = stack
        self.needed = needed
        self.used = set()
        self.opid = 0
        self.engs = {"pe": nc.tensor, "act": nc.scalar, "dve": nc.vector,
                     "pool": nc.gpsimd, "sp": nc.sync}
        self.cur = {}
        self.nsem = 0
        for e in COMPUTE:
            self.cur[e] = [self._newsem(e), 0]
        self.waited = {e: {p: 0 for p in COMPUTE} for e in self.engs}
        self.waited_dma = {e: {} for e in self.engs}
        self.dsem = {}
        for q in ("sp", "pool"):
            self.dsem[q] = [[self._newsem(f"d{q}{i}"), 0] for i in range(NDMASEM)]
        self.dptr = {q: 0 for q in self.dsem}
        self.last = {e: None for e in COMPUTE}
        self.dma_toks = {}

    def _newsem(self, name):
        self.nsem += 1
        return self.stack.enter_context(self.nc.semaphore(f"s{self.nsem}_{name}"))

    def _wait(self, eng, t):
        if t is None:
            return
        if t.dma:
            key = id(t.sem)
            if self.waited_dma[eng].get(key, 0) >= t.val:
                return
            self.engs[eng].wait_ge(t.sem, t.val)
            self.waited_dma[eng][key] = t.val
            return
        if t.eng == eng and eng == "pe":
            return
        if self.waited[eng][t.eng] >= t.id:
            return
        self.used.add(t.id)
        self.waited[eng][t.eng] = t.id
        if t.sem is None:
            raise RuntimeError("dependency on an unsignalled op (two-pass mismatch)")
        self.engs[eng].wait_ge(t.sem, t.val)

    @staticmethod
    def _deps(reads, writes, extra):
        deps = []
        for b in reads:
            if b.w is not None:
                deps.append(b.w)
        for b in writes:
            if b.w is not None:
                deps.append(b.w)
            deps.extend(b.r.values())
        deps.extend(extra)
        return deps

    @staticmethod
    def _mark(tok, reads, writes, key):
        for b in reads:
            b.r[key] = tok
        for b in writes:
            b.w = tok
            b.r = {}

    def op(self, eng, fn, reads=(), writes=(), extra=()):
        for t in self._deps(reads, writes, extra):
            self._wait(eng, t)
        inst = fn()
        self.opid += 1
        tok = Tok(eng, self.opid)
        if self.needed is None or tok.id in self.needed:
            c = self.cur[eng]
            if c[1] >= EPOCH:
                c[0] = self._newsem(eng)
                c[1] = 0
            c[1] += 1
            inst.then_inc(c[0], 1)
            tok.sem = c[0]
            tok.val = c[1]
        self.last[eng] = tok
        self._mark(tok, reads, writes, eng)
        return tok

    def dma(self, q, out, in_, reads=(), writes=(), extra=(), **kw):
        for t in self._deps(reads, writes, extra):
            self._wait(q, t)
        ring = self.dsem[q]
        slot = ring[self.dptr[q] % NDMASEM]
        self.dptr[q] += 1
        sem, cnt = slot
        if cnt > 0:
            key = id(sem)
            if self.waited_dma[q].get(key, 0) < cnt:
                self.engs[q].wait_ge(sem, cnt)
                self.waited_dma[q][key] = cnt
        inst = self.engs[q].dma_start(out=out, in_=in_, **kw)
        slot[1] = cnt + 16
        inst.then_inc(sem, 16)
        self.opid += 1
        tok = Tok(q, self.opid, dma=True)
        tok.sem = sem
        tok.val = cnt + 16
        self._mark(tok, reads, writes, ("dma", tok.id))
        self.dma_toks[id(sem)] = tok
        return tok

    def all_tokens(self):
        return [t for t in self.last.values() if t is not None] + list(self.dma_toks.values())

    def barrier(self):
        toks = self.all_tokens()
        for e in ("pe", "act", "dve", "pool", "sp"):
            for t in toks:
                self._wait(e, t)


class Ring:
    def __init__(self, items):
        self.items = items
        self.i = 0

    def next(self):
        it = self.items[self.i % len(self.items)]
        self.i += 1
        return it


class WStream:
    def __init__(self, k, bufs_, specs, q="pool"):
        self.k = k
        self.q = q
        self.bufs = bufs_
        self.specs = specs
        self.loaded = 0
        self.i = 0

    def _load(self, j):
        flat, B = self.bufs[j % len(self.bufs)]
        for pi, (off, src, a, b) in enumerate(self.specs[j]):
            view = flat[:, off:off + a * b].rearrange("p (a b) -> p a b", a=a)
            self.k.dma(self.q, view, src, writes=[B[pi]])

    def prefetch(self):
        n = len(self.bufs)
        while self.loaded < min(len(self.specs), self.i + n):
            self._load(self.loaded)
            self.loaded += 1

    def get(self):
        n = len(self.bufs)
        while self.loaded < min(len(self.specs), self.i + n):
            self._load(self.loaded)
            self.loaded += 1
        flat, B = self.bufs[self.i % n]
        self.i += 1
        return flat, B


def v3(flat, off, a, b):
    return flat[:, off:off + a * b].rearrange("p (a b) -> p a b", a=a)


def build(needed=None, debug=False, stages=3):
    nc = bass.Bass("TRN2", target_bir_lowering=False)

    def din(name, shape):
        return nc.dram_tensor(name, shape, F32, kind="ExternalInput").ap()

    xh = din("xh", [2 * NT, D])
    pc = din("pc", [NT, 256])
    w_in = din("w_in", [D, 5120])
    w_out = din("w_out", [D, D])
    w_gate = din("w_gate", [D, DFF])
    w_up = din("w_up", [D, DFF])
    w_down = din("w_down", [DFF, D])
    w_pg = din("w_pg", [D, D])
    w_pp = din("w_pp", [256, D])
    poolw_d = din("poolw_in", [128, 4, 128])
    gvec_d = din("gvec_in", [128, 68])
    biasT_d = din("biasT", [128, 4, 3, 256])
    maskc_d = din("maskc_in", [128, 256])
    hneg_d = din("hneg_in", [128, 1])
    rcnt_d = din("rcnt_in", [128, 4, 16])
    ident_d = din("identf_in", [128, 128])
    gfinb_d = din("gfinb_in", [128, D])
    out_d = nc.dram_tensor("out", [NT, D], F32, kind="ExternalOutput").ap()
    skind = "ExternalOutput" if debug else "Internal"
    xT_d = nc.dram_tensor("xT_d", [KC, 128, NT], F32, kind=skind).ap()
    xnT_d = nc.dram_tensor("xnT_d", [KC, 128, 2 * NT], BF16, kind=skind).ap()
    mixT_d = nc.dram_tensor("mixT_d", [KC, 128, NT], BF16, kind=skind).ap()

    def dview(t3, c0, c1):
        return t3[:, :, c0:c1].rearrange("k p n -> p k n")

    with ExitStack() as st:
        k = K(nc, st, needed)

        def sb(stack, name, shape, dt):
            return stack.enter_context(nc.sbuf_tensor(name, shape, dt))

        identf = sb(st, "identf", [128, 128], F32)
        identb = sb(st, "identb", [128, 128], BF16)
        ones = sb(st, "ones", [128, 128], BF16)
        epst = sb(st, "epst", [128, 1], F32)
        gvec = sb(st, "gvec", [128, 68], F32)
        CB = Buf("const")
        k.dma("sp", identf[:], ident_d, writes=[CB])
        k.dma("sp", gvec[:], gvec_d, writes=[CB])
        k.dma("pool", identb[:], ident_d, writes=[CB])
        k.op("dve", lambda: nc.vector.memset(ones[:], 1.0), writes=[CB])
        k.op("dve", lambda: nc.vector.memset(epst[:], EPS), writes=[CB])

        banks = []
        for i in range(8):
            t = st.enter_context(nc.psum_tensor(f"bank{i}", [128, 512], F32))
            banks.append((t, Buf(f"bank{i}")))
        psum = Ring(banks)
        ev = [0]

        def evac_eng():
            ev[0] += 1
            return "act" if ev[0] % 2 else "dve"

        def copy_op(eng, out, in_, reads, writes, scale=None):
            if eng == "act":
                if scale is None:
                    return k.op("act", lambda: nc.scalar.copy(out, in_), reads=reads, writes=writes)
                return k.op("act", lambda: nc.scalar.mul(out, in_, scale), reads=reads, writes=writes)
            if scale is None:
                return k.op("dve", lambda: nc.vector.tensor_copy(out, in_), reads=reads, writes=writes)
            return k.op("dve", lambda: nc.vector.tensor_scalar(out, in_, scale, None, ALU.mult),
                        reads=reads, writes=writes)

        def norm_fm(src, srcB, gcol, dst, dstB, ncols, sq, sqB, rsr, cstart=0):
            for s in range(ncols // 512):
                c0 = cstart + s * 512
                k.op("act", lambda: nc.scalar.activation(sq, src[:, :, c0:c0 + 512], AF.Square),
                     reads=srcB, writes=sqB)
                bank, bB = psum.next()
                for kc in range(KC):
                    k.op("pe", lambda kc=kc: nc.tensor.matmul(bank[:, :], ones[:], sq[:, kc, :],
                                                             start=(kc == 0), stop=(kc == KC - 1)),
                         reads=sqB + [CB], writes=[bB])
                rs, rsB = rsr.next()
                k.op("act", lambda: nc.scalar.activation(rs[:], bank[:, :], AF.Sqrt, bias=epst[:], scale=1.0 / D),
                     reads=[bB, CB], writes=[rsB])
                k.op("dve", lambda: nc.vector.reciprocal(rs[:], rs[:]), reads=[rsB], writes=[rsB])
                for kc in range(KC):
                    k.op("dve", lambda kc=kc: nc.vector.scalar_tensor_tensor(
                        out=dst[:, kc, c0:c0 + 512], in0=src[:, kc, c0:c0 + 512],
                        scalar=gvec[:, gcol + kc:gcol + kc + 1], in1=rs[:], op0=ALU.mult, op1=ALU.mult),
                        reads=[srcB[kc], rsB, CB], writes=[dstB[kc]])

        def norm_lite(src, srcB, gcol, dst, dstB, ncols, sqs, rsr, skip_sq=False, skip_hg=False):
            for kc in range(0 if skip_hg else KC):
                if kc % 2 == 0:
                    k.op("dve", lambda kc=kc: nc.vector.tensor_scalar(
                        dst[:, kc, 0:ncols], src[:, kc, 0:ncols], gvec[:, gcol + kc:gcol + kc + 1], None, ALU.mult),
                        reads=[srcB[kc], CB], writes=[dstB[kc]])
                else:
                    k.op("act", lambda kc=kc: nc.scalar.activation(
                        dst[:, kc, 0:ncols], src[:, kc, 0:ncols], AF.Copy, scale=gvec[:, gcol + kc:gcol + kc + 1]),
                        reads=[srcB[kc], CB], writes=[dstB[kc]])
            return sumsq_chain(src, srcB, ncols, sqs, rsr, skip_sq)

        def sumsq_chain(src, srcB, ncols, sqs, rsr, skip_sq=False):
            for s_ in range(0 if skip_sq else ncols // 512):
                sq, sqB = sqs[s_]
                k.op("act", lambda s_=s_, sq=sq: nc.scalar.activation(sq, src[:, :, s_ * 512:(s_ + 1) * 512], AF.Square),
                     reads=srcB, writes=sqB)

            def finish():
                out = []
                for s_ in range(ncols // 512):
                    sq, sqB = sqs[s_]
                    bank, bB = psum.next()
                    for kc in range(KC):
                        k.op("pe", lambda kc=kc, sq=sq, bank=bank: nc.tensor.matmul(
                            bank[:, :], ones[:], sq[:, kc, :], start=(kc == 0), stop=(kc == KC - 1)),
                            reads=sqB + [CB], writes=[bB])
                    rs, rsB = rsr.next()
                    k.op("act", lambda rs=rs, bank=bank: nc.scalar.activation(
                        rs[:], bank[:, :], AF.Sqrt, bias=epst[:], scale=1.0 / D), reads=[bB, CB], writes=[rsB])
                    k.op("dve", lambda rs=rs: nc.vector.reciprocal(rs[:], rs[:]), reads=[rsB], writes=[rsB])
                    out.append((rs, rsB))
                return out
            return finish

        with ExitStack() as s1:
            xt = [(sb(s1, f"xt{i}", [128, 2, D], F32), Buf()) for i in range(2)]
            xTt = [(sb(s1, f"xTt{i}", [128, KC, 512], F32), [bufs(4) for _ in range(4)]) for i in range(3)]
            sqs = [(sb(s1, f"sq1_{i}", [128, KC, 512], BF16), bufs(4)) for i in range(2)]
            xnt = [(sb(s1, f"xnt{i}", [128, KC, 512], BF16), bufs(KC)) for i in range(2)]
            rsr = Ring([(sb(s1, f"rs1_{i}", [128, 512], F32), Buf()) for i in range(2)])
            def s1_transpose(i):
                xT, xTB = xTt[i % 3]
                sq, sqB = sqs[i % 2]
                for b in range(4):
                    gb = i * 4 + b
                    xb2, xbB = xt[(gb // 2) % 2]
                    if b % 2 == 0:
                        k.dma("sp", xb2[:], xh[gb * 128:(gb + 2) * 128, :].rearrange("(b p) f -> p b f", p=128),
                              writes=[xbB])
                    xb = xb2[:, b % 2, :]
                    for j in range(4):
                        bank, bB = psum.next()
                        for q in range(4):
                            kc = 4 * j + q
                            k.op("pe", lambda kc=kc, q=q, xb=xb: nc.tensor.transpose(
                                bank[:, q * 128:(q + 1) * 128], xb[:, kc * 128:(kc + 1) * 128], identf[:]),
                                reads=[xbB, CB], writes=[bB])
                        copy_op(evac_eng(), xT[:, 4 * j:4 * j + 4, b * 128:(b + 1) * 128],
                                bank[:, :].rearrange("p (q n) -> p q n", q=4),
                                reads=[bB], writes=[xTB[j][b]])
                if i >= 4:
                    k.dma("pool", dview(xT_d, (i - 4) * 512, (i - 3) * 512), xT[:],
                          reads=[xTB[j][b] for j in range(4) for b in range(4)])

            def s1_squares(i):
                xT, xTB = xTt[i % 3]
                sq, sqB = sqs[i % 2]
                for b in range(4):
                    k.op("act", lambda b=b: nc.scalar.activation(sq[:, :, b * 128:(b + 1) * 128],
                                                               xT[:, :, b * 128:(b + 1) * 128], AF.Square),
                         reads=[xTB[j][b] for j in range(4)], writes=[sqB[b]])

            def s1_norm(i):
                xT, xTB = xTt[i % 3]
                sq, sqB = sqs[i % 2]
                xn, xnB = xnt[i % 2]
                bank, bB = psum.next()
                for kc in range(KC):
                    k.op("pe", lambda kc=kc: nc.tensor.matmul(bank[:, :], ones[:], sq[:, kc, :],
                                                             start=(kc == 0), stop=(kc == KC - 1)),
                         reads=sqB + [CB], writes=[bB])
                rs, rsB = rsr.next()
                k.op("act", lambda: nc.scalar.activation(rs[:], bank[:, :], AF.Sqrt, bias=epst[:], scale=1.0 / D),
                     reads=[bB, CB], writes=[rsB])
                k.op("dve", lambda: nc.vector.reciprocal(rs[:], rs[:]), reads=[rsB], writes=[rsB])
                for kc in range(KC):
                    k.op("dve", lambda kc=kc: nc.vector.scalar_tensor_tensor(
                        out=xn[:, kc, :], in0=xT[:, kc, :], scalar=gvec[:, kc:kc + 1], in1=rs[:],
                        op0=ALU.mult, op1=ALU.mult), reads=xTB[kc // 4] + [rsB, CB], writes=[xnB[kc]])
                k.dma("pool", dview(xnT_d, i * 512, (i + 1) * 512), xn[:], reads=xnB)

            s1_transpose(0)
            s1_transpose(1)
            s1_squares(0)
            for i in range(8):
                if i + 2 < 8:
                    s1_transpose(i + 2)
                if i + 1 < 8:
                    s1_squares(i + 1)
                s1_norm(i)
        k.barrier()

        if stages >= 2:
          with ExitStack() as s2:
            xnr = [(sb(s2, f"xn2_{i}", [128, KC, 512], BF16), Buf()) for i in range(3)]
            wsl = sb(s2, "wsl", [128, KC, 1152], BF16)
            wslB = bufs(9)
            poolw = sb(s2, "poolw", [128, 4, 128], BF16)
            rcnt = sb(s2, "rcnt", [128, 4, 16], F32)
            maskc = sb(s2, "maskc", [128, 256], F32)
            hneg = sb(s2, "hneg", [128, 1], F32)
            C2 = Buf("c2")
            k.dma("pool", poolw[:], poolw_d, writes=[C2])
            k.dma("sp", rcnt[:], rcnt_d, writes=[C2])
            k.dma("sp", maskc[:], maskc_d, writes=[C2])
            k.dma("sp", hneg[:], hneg_d, writes=[C2])
            xn_i = [0]

            def load_xn(i):
                t, B = xnr[xn_i[0] % 3]
                xn_i[0] += 1
                k.dma("sp", t[:], dview(xnT_d, i * 512, (i + 1) * 512), writes=[B])
                return t, B

            passes = [(j, gs) for j in range(4) for gs in ((0, 1), (2,))]
            tile_seq = []
            for (j, gs) in passes:
                for i in range(0 if 2 in gs else 3, 8):
                    tile_seq.append([(0, dview(xnT_d, i * 512, (i + 1) * 512), KC, 512)])
            xns = WStream(k, [(t[:].rearrange("p a b -> p (a b)"), [B]) for (t, B) in xnr], tile_seq, q="sp")

            def load_w(j, g):
                base = 512 + 1152 * j
                for kind in range(3):
                    c = kind * 384 + g * 128
                    k.dma("pool", wsl[:, :, c:c + 128],
                          w_in[:, base + c: base + c + 128].rearrange("(k p) n -> p k n", p=128),
                          writes=[wslB[kind * 3 + g]])

            with ExitStack() as sp_:
                uT = sb(sp_, "uT", [128, 4, 2064], F32)
                uTB = [bufs(5) for _ in range(4)]
                T = [(sb(sp_, f"pT{i}", [128, 2064], F32), Buf()) for i in range(2)]
                pooled = sb(sp_, "pooled", [128, 4, 2048], BF16)
                pooledB = [bufs(4) for _ in range(4)]
                t16 = sb(sp_, "t16", [128, 16], F32)
                t16B = Buf()
                mo = [(sb(sp_, f"mo{i}", [128, 2048], BF16), Buf()) for i in range(4)]
                k.dma("pool", wsl[:, :, 0:512], w_in[:, 0:512].rearrange("(k p) n -> p k n", p=128), writes=wslB[0:4])

                def pool_tile_post(kt):
                    lo, hi = 16 + 512 * kt, 16 + 512 * (kt + 1)
                    for c in range(4):
                        w = 2 << c
                        rd = [uTB[c][kt + 1]] + ([uTB[c][kt]] if True else [])
                        cur, curB = uT[:, c, :], rd
                        for stp in range(c + 1):
                            sh = 1 << stp
                            st_ = lo - w + 2 * sh
                            dT, dB = T[stp % 2]
                            k.op("dve", lambda cur=cur, st_=st_, sh=sh, dT=dT: nc.vector.tensor_tensor(
                                out=dT[:, st_:hi], in0=cur[:, st_:hi], in1=cur[:, st_ - sh:hi - sh], op=ALU.add),
                                reads=curB, writes=[dB])
                            cur, curB = dT[:, :], [dB]
                        k.op("dve", lambda cur=cur: nc.vector.scalar_tensor_tensor(
                            out=pooled[:, c, lo - 16:hi - 16], in0=cur[:, lo:hi], scalar=1.0 / w, in1=uT[:, c, lo:hi],
                            op0=ALU.mult, op1=ALU.subtract), reads=curB + rd, writes=[pooledB[c][kt]])
                        if kt == 0:
                            k.op("dve", lambda cur=cur: nc.vector.tensor_tensor(
                                out=t16[:], in0=cur[:, 16:32], in1=rcnt[:, c, :], op=ALU.mult),
                                reads=curB + [C2], writes=[t16B])
                            k.op("dve", lambda: nc.vector.tensor_tensor(
                                out=pooled[:, c, 0:16], in0=t16[:], in1=uT[:, c, 16:32], op=ALU.subtract),
                                reads=[t16B] + rd, writes=[pooledB[c][kt]])
                def pool_tile_mix(kt):
                    lo, hi = 16 + 512 * kt, 16 + 512 * (kt + 1)
                    for c in range(4):
                        m, mB = mo[c]
                        bank, bB = psum.next()
                        k.op("pe", lambda: nc.tensor.matmul(bank[:, :], poolw[:, c, :],
                                                           pooled[:, c, lo - 16:hi - 16], start=True, stop=True),
                             reads=[pooledB[c][kt], C2], writes=[bB])
                        k.op("act", lambda: nc.scalar.activation(
                            m[:, lo - 16:hi - 16], bank[:, :], AF.Copy, scale=gvec[:, 64 + c:65 + c]),
                            reads=[bB, CB], writes=[mB])

                for i in range(3, 8):
                    xn, xnB = load_xn(i)
                    c0, n = (496, 16) if i == 3 else (0, 512)
                    off = 0 if i == 3 else 16 + (i - 4) * 512
                    for c in range(4):
                        bank, bB = psum.next()
                        for kc in range(KC):
                            k.op("pe", lambda kc=kc: nc.tensor.matmul(
                                bank[:, 0:n], wsl[:, kc, c * 128:(c + 1) * 128], xn[:, kc, c0:c0 + n],
                                start=(kc == 0), stop=(kc == KC - 1)), reads=[xnB] + wslB[0:4], writes=[bB])
                        copy_op(evac_eng(), uT[:, c, off:off + n], bank[:, 0:n], reads=[bB], writes=[uTB[c][i - 3]])
                    if i >= 5:
                        pool_tile_mix(i - 5)
                    if i >= 4:
                        pool_tile_post(i - 4)
                pool_tile_mix(3)
                for c in range(4):
                    k.dma("pool", mixT_d[c], mo[c][0][:], reads=[mo[c][1]])
                for g in range(3):
                    load_w(0, g)
                xns.prefetch()
            k.barrier()

            LQ = [NT // d for d in GD]
            LK = [(NT + h) // d for d, h in zip(GD, GH)]
            NBK = [l // 128 for l in LK]
            QT = sb(s2, "QT", [128, 3, NT], BF16)
            KT = [sb(s2, f"KT{g}", [128, GD[g] * LK[g]], BF16) for g in range(3)]
            VT = [sb(s2, f"VT{g}", [128, GD[g] * LK[g]], BF16) for g in range(3)]
            V = [sb(s2, f"V{g}", [128, GD[g] * NBK[g], 128], BF16) for g in range(3)]
            num = sb(s2, "num", [128, 3, NT], F32)
            densum = sb(s2, "densum", [128, NT], F32)
            ybf = [(sb(s2, f"ybf{i}", [128, NT], BF16), Buf()) for i in range(3)]
            sbr = Ring([(sb(s2, f"sbs{i}", [128, 512], F32), Buf()) for i in range(2)])
            ptr = Ring([(sb(s2, f"pts{i}", [128, 512], BF16), Buf()) for i in range(2)])
            bm = sb(s2, "bm", [128, 3, 2, 256], F32)
            bmB = bufs(3)
            QTB = [bufs(8) for _ in range(3)]
            KTB = [bufs(8) for _ in range(3)]
            VTB = [bufs(8) for _ in range(3)]
            VB = [Buf() for _ in range(3)]
            numB = bufs(3)
            denB = Buf()
            yi = [0]
            ps_chunk = Ring(banks[0:4])
            ps_S = Ring(banks[4:6])
            ps_N, ps_D = banks[6], banks[7]

            def build_bm(j, g):
                k.dma("sp", bm[:, g, 0, :], biasT_d[:, j, g, :], writes=[bmB[g]])
                k.op("dve", lambda: nc.vector.tensor_tensor(out=bm[:, g, 0, :], in0=bm[:, g, 0, :], in1=maskc[:], op=ALU.add),
                     reads=[C2], writes=[bmB[g]])
                k.op("dve", lambda: nc.vector.tensor_copy(bm[:, g, 1, 128:256], bm[:, g, 0, 128:256]),
                     reads=[bmB[g]], writes=[bmB[g]])
                k.op("dve", lambda: nc.vector.tensor_scalar(bm[:, g, 1, 0:128], bm[:, g, 0, 0:128], hneg[:, 0:1], None, ALU.add),
                     reads=[bmB[g], C2], writes=[bmB[g]])

            def pass_chunks(j, gs):
                tasks = []
                for i in range(0 if 2 in gs else 3, 8):
                    holder = {}
                    kinds = (1, 2) if i < 4 else (0, 1, 2)
                    combos = [(g, kind) for g in gs for kind in kinds]
                    for ki, (g, kind) in enumerate(combos):
                        def chunk(i=i, kind=kind, g=g, holder=holder, first=(ki == 0)):
                            d, H = GD[g], GH[g]
                            if first:
                                flat, xB = xns.get()
                                holder["xn"] = (v3(flat, 0, KC, 512), xB)
                            xn, xnB = holder["xn"]
                            if kind == 0:
                                c0, n, e0 = 0, 512, (i - 4) * 512
                                dst3 = QT[:, g, :].rearrange("p (r l) -> p r l", r=d)
                                dB = QTB[g][i]
                            else:
                                e0 = i * 512 - NT + H
                                c0, n = (0, 512) if e0 >= 0 else (-e0, 512 + e0)
                                e0 = max(e0, 0)
                                tt_ = KT[g] if kind == 1 else VT[g]
                                dst3 = tt_[:, :].rearrange("p (r l) -> p r l", r=d)
                                dB = (KTB if kind == 1 else VTB)[g][i]
                            wc = kind * 384 + g * 128
                            bank, bB = ps_chunk.next()
                            for kc in range(KC):
                                k.op("pe", lambda kc=kc: nc.tensor.matmul(
                                    bank[:, 0:n], wsl[:, kc, wc:wc + 128], xn[:, kc, c0:c0 + n],
                                    start=(kc == 0), stop=(kc == KC - 1)),
                                    reads=xnB + [wslB[kind * 3 + g]], writes=[bB])
                            src3 = bank[:, 0:n].rearrange("p (l r) -> p r l", r=d)
                            dsl = dst3[:, :, e0 // d:(e0 + n) // d]
                            if kind == 0:
                                copy_op("act", dsl, src3, [bB], [dB], scale=QSCALE)
                            elif kind == 1:
                                copy_op("dve", dsl, src3, [bB], [dB])
                            else:
                                copy_op(evac_eng(), dsl, src3, [bB], [dB])
                        tasks.append(chunk)
                return tasks

            def attn_tasks(j, g):
                tasks = []
                d, nqb, nbk, lq, lk = GD[g], LQ[g] // 128, NBK[g], LQ[g], LK[g]
                nb = d * nbk

                def vtrans(b0):
                    nn = min(4, nb - b0)
                    bank, bB = ps_chunk.next()
                    bv = bank[:, :].bitcast(BF16)
                    for q in range(nn):
                        b = b0 + q
                        k.op("pe", lambda b=b, q=q: nc.tensor.transpose(
                            bv[:, q * 128:(q + 1) * 128], VT[g][:, b * 128:(b + 1) * 128], identb[:]),
                            reads=VTB[g] + [CB], writes=[bB])
                    copy_op(evac_eng(), V[g][:, b0:b0 + nn, :].rearrange("p a b -> p (a b)"),
                            bv[:, 0:nn * 128], [bB], [VB[g]])

                for b0 in range(0, nb, 4):
                    tasks.append(lambda b0=b0: vtrans(b0))
                blocks = [(r, qb) for r in range(d) for qb in range(nqb)]
                pairs = [blocks[p:p + 2] for p in range(0, 16, 2)]
                state = {}

                def emit_S(pi):
                    sbk, sB = ps_S.next()
                    for xi, (r, qb) in enumerate(pairs[pi]):
                        qv = QT[:, g, r * lq + qb * 128: r * lq + (qb + 1) * 128]
                        for hh in range(2):
                            kv = KT[g][:, r * lk + (qb + hh) * 128: r * lk + (qb + hh + 1) * 128]
                            co = xi * 256 + hh * 128
                            k.op("pe", lambda kv=kv, qv=qv, co=co: nc.tensor.matmul(
                                sbk[:, co:co + 128], kv, qv, start=True, stop=True),
                                reads=QTB[g] + KTB[g], writes=[sB])
                    sbt, sbB = sbr.next()
                    for xi, (r, qb) in enumerate(pairs[pi]):
                        var = 1 if qb == 0 else 0
                        k.op("dve", lambda xi=xi, var=var: nc.vector.tensor_tensor(
                            out=sbt[:, xi * 256:(xi + 1) * 256], in0=sbk[:, xi * 256:(xi + 1) * 256],
                            in1=bm[:, g, var, :], op=ALU.add), reads=[sB, bmB[g]], writes=[sbB])
                    pt, ptB = ptr.next()
                    k.op("act", lambda: nc.scalar.activation(pt[:], sbt[:], AF.Exp), reads=[sbB], writes=[ptB])
                    state[pi] = (pt, ptB)

                def emit_PV(pi):
                    pt, ptB = state.pop(pi)
                    (nb_, nbB), (db_, dbB) = ps_N, ps_D
                    p0 = (pi % 2) * 2
                    for xi, (r, qb) in enumerate(pairs[pi]):
                        co = (p0 + xi) * 128
                        for hh in range(2):
                            vv = V[g][:, r * nbk + qb + hh, :]
                            pv = pt[:, xi * 256 + hh * 128: xi * 256 + (hh + 1) * 128]
                            k.op("pe", lambda vv=vv, pv=pv, co=co, hh=hh: nc.tensor.matmul(
                                nb_[:, co:co + 128], vv, pv, start=(hh == 0), stop=(hh == 1)),
                                reads=[ptB, VB[g]], writes=[nbB])
                        for hh in range(2):
                            pv = pt[:, xi * 256 + hh * 128: xi * 256 + (hh + 1) * 128]
                            k.op("pe", lambda pv=pv, co=co, hh=hh: nc.tensor.matmul(
                                db_[:, co:co + 128], ones[:], pv, start=(hh == 0), stop=(hh == 1)),
                                reads=[ptB, CB], writes=[dbB])
                    if pi % 2 == 0:
                        return
                    q0 = (pi // 2) * 4
                    if g == 0:
                        u = q0 // 4
                        nv = num[:, g, u * 512:(u + 1) * 512]
                        dv = densum[:, u * 512:(u + 1) * 512]
                        sn, sd = nb_[:, :], db_[:, :]
                    elif g == 1:
                        r = q0 // 4
                        nv = num[:, g, :].rearrange("p (b i r) -> p r b i", b=4, r=4)[:, r, :, :]
                        dv = densum[:, :].rearrange("p (b i r) -> p r b i", b=4, r=4)[:, r, :, :]
                        sn = nb_[:, :].rearrange("p (b i) -> p b i", b=4)
                        sd = db_[:, :].rearrange("p (b i) -> p b i", b=4)
                    else:
                        nv = num[:, g, :].rearrange("p (i r) -> p r i", r=16)[:, q0:q0 + 4, :]
                        dv = densum[:, :].rearrange("p (i r) -> p r i", r=16)[:, q0:q0 + 4, :]
                        sn = nb_[:, :].rearrange("p (b i) -> p b i", b=4)
                        sd = db_[:, :].rearrange("p (b i) -> p b i", b=4)
                    k.op("act", lambda: nc.scalar.copy(nv, sn), reads=[nbB], writes=[numB[g]])
                    if g == 0:
                        k.op("dve", lambda: nc.vector.tensor_copy(dv, sd), reads=[dbB], writes=[denB])
                    else:
                        k.op("dve", lambda: nc.vector.tensor_tensor(out=dv, in0=sd, in1=dv, op=ALU.add),
                             reads=[dbB], writes=[denB])

                tasks.append(lambda: emit_S(0))
                for pi in range(8):
                    if pi + 1 < 8:
                        tasks.append(lambda pi=pi: emit_S(pi + 1))
                    tasks.append(lambda pi=pi: emit_PV(pi))
                if j + 1 < 4:
                    tasks.append(lambda: build_bm(j + 1, g))
                if g == 2:
                    for qq in range(4):
                        tasks.append(lambda qq=qq: finalize(j, qq))
                return tasks

            ycur = {}

            def finalize(j, qq):
                c0, c1 = qq * 512, (qq + 1) * 512
                k.op("dve", lambda: nc.vector.reciprocal(densum[:, c0:c1], densum[:, c0:c1]), reads=[denB], writes=[denB])
                for g in range(3):
                    if qq == 0:
                        ycur[g] = ybf[yi[0] % 3]
                        yi[0] += 1
                    y, yB = ycur[g]
                    k.op("dve", lambda g=g, y=y: nc.vector.tensor_tensor(
                        out=y[:, c0:c1], in0=num[:, g, c0:c1], in1=densum[:, c0:c1], op=ALU.mult),
                        reads=[numB[g], denB], writes=[yB])
                    if qq == 3:
                        k.dma("pool", mixT_d[4 + 4 * g + j], y[:], reads=[yB])

            for g in range(3):
                build_bm(0, g)
            pending = []
            for (j, gs) in passes:
                chunks = pass_chunks(j, gs)
                done = 0
                for ci, ch in enumerate(chunks):
                    ch()
                    want = (len(pending) + done) * (ci + 1) // len(chunks)
                    while done < want and pending:
                        pending.pop(0)()
                        done += 1
                while pending:
                    pending.pop(0)()
                pending = []
                for g in gs:
                    if j + 1 < 4:
                        load_w(j + 1, g)
                    pending += attn_tasks(j, g)
            while pending:
                pending.pop(0)()
          k.barrier()

        if stages >= 3:
          with ExitStack() as s3:
            TT = 1024
            hT = sb(s3, "hT", [128, KC, TT], F32)
            hB = bufs(KC)
            hn = sb(s3, "hn", [128, KC, TT], BF16)
            hnB = bufs(KC)
            R = sb(s3, "R", [128, 22 * 1024], BF16)
            RB = bufs(22)
            wr = [(sb(s3, f"wr{i}", [128, 8192], BF16)[:, :], bufs(2)) for i in range(3)]
            rsr = Ring([(sb(s3, f"rs3_{i}", [128, 512], F32), Buf()) for i in range(2)])
            sgr = Ring([(sb(s3, f"sg{i}", [128, 512], F32), Buf()) for i in range(2)])
            tmr = Ring([(sb(s3, f"tm{i}", [128, 512], F32), Buf()) for i in range(3)])
            t1r = tmr
            pblk = Ring([(sb(s3, f"pb{i}", [128, 256], F32), Buf()) for i in range(2)])
            sq = R[:, 0:8192].rearrange("p (k n) -> p k n", k=KC)
            sqB = RB[0:8]
            sqs2 = [(sq, sqB), (R[:, 10240:18432].rearrange("p (k n) -> p k n", k=KC), RB[10:18])]
            gfb = R[:, 18432:22528].bitcast(F32)
            gfbB = RB[18:22]
            rscol = sb(s3, "rscol", [128, 16], F32)
            rscolB = Buf()
            act = R[:, :].rearrange("p (k n) -> p k n", k=22)
            pTb = R[:, 8192:10240].rearrange("p (k n) -> p k n", k=2)
            pTB = RB[8:10]
            yblk = [(R[:, 10240 + i * 4096: 10240 + (i + 1) * 4096].bitcast(F32), RB[10 + 4 * i: 14 + 4 * i]) for i in range(2)]

            def kview(w, r0, nk, c0, nc_):
                return w[r0:r0 + nk * 128, c0:c0 + nc_].rearrange("(k p) n -> p k n", p=128)

            halves = [(0, 22), (22, 22)]
            specs = []
            for tt in range(NT // TT):
                specs += [[(0, kview(w_out, 0, KC, a * 512, 512), KC, 512)] for a in range(4)]
                for (f0, nf) in halves:
                    for t in range(nf // 2):
                        specs.append([(0, kview(w_gate, 0, KC, (f0 + 2 * t) * 128, 256), KC, 256),
                                      (4096, kview(w_up, 0, KC, (f0 + 2 * t) * 128, 256), KC, 256)])
                    for c in range(8):
                        specs.append([(0, kview(w_down, f0 * 128, nf, c * 256, 256), nf, 256)])
                for a in range(8):
                    specs.append([(0, kview(w_pg, 0, KC, a * 256, 256), KC, 256),
                                  (4096, kview(w_pp, 0, 2, a * 256, 256), 2, 256)])
            ws = WStream(k, wr, specs)
            for tt in range(NT // TT):
                t0 = tt * TT

                if tt == 0:
                    k.dma("sp", hn[:], dview(mixT_d, t0, t0 + TT), writes=hnB)
                for kc4 in range(0, KC, 4):
                    k.dma("sp", hT[:, kc4:kc4 + 4, :], xT_d[kc4:kc4 + 4, :, t0:t0 + TT].rearrange("k p n -> p k n"),
                          writes=hB[kc4:kc4 + 4])

                def mm_group(wv, wB, ocol, src, srcB, nk):
                    bb = [psum.next(), psum.next()]
                    for kk in range(nk):
                        for s_ in range(2):
                            k.op("pe", lambda kk=kk, s_=s_: nc.tensor.matmul(
                                bb[s_][0][:, :], wv[:, kk, ocol:ocol + 128], src[:, kk, s_ * 512:(s_ + 1) * 512],
                                start=(kk == 0), stop=(kk == nk - 1)),
                                reads=wB + srcB, writes=[bb[s_][1]])
                    return bb

                def sq_fly(oc, s_):
                    sqv, sqvB = sqs2[s_]
                    hv = hT[:, oc, s_ * 512:(s_ + 1) * 512]
                    k.op("act", lambda: nc.scalar.activation(sqv[:, oc, :], hv, AF.Square),
                         reads=[hB[oc]], writes=[sqvB[oc // 2]])

                def add_into_h(oc, bb, sq_on=False, hg_col=None):
                    for s_ in range(2):
                        hv = hT[:, oc, s_ * 512:(s_ + 1) * 512]
                        k.op("dve", lambda hv=hv, s_=s_: nc.vector.tensor_tensor(out=hv, in0=bb[s_][0][:, :], in1=hv, op=ALU.add),
                             reads=[bb[s_][1]], writes=[hB[oc]])
                        if sq_on:
                            sq_fly(oc, s_)
                        if hg_col is not None:
                            k.op("act", lambda hv=hv, s_=s_: nc.scalar.activation(
                                hn[:, oc, s_ * 512:(s_ + 1) * 512], hv, AF.Copy, scale=gvec[:, hg_col + oc:hg_col + oc + 1]),
                                reads=[hB[oc], CB], writes=[hnB[oc]])

                for a in range(4):
                    wf, wB = ws.get()
                    wv = v3(wf, 0, KC, 512)
                    for o in range(4):
                        bb = mm_group(wv, wB, o * 128, hn, hnB, KC)
                        add_into_h(4 * a + o, bb, sq_on=True)
                finF = norm_lite(hT, hB, 16, hn, hnB, TT, sqs2, rsr, skip_sq=True)
                rsF = None
                for (f0, nf) in halves:
                    for t in range(nf // 2):
                        wf, wB = ws.get()
                        wg, wu = v3(wf, 0, KC, 256), v3(wf, 4096, KC, 256)
                        for o in range(2):
                            fl = 2 * t + o
                            gb_ = mm_group(wg, wB, o * 128, hn, hnB, KC)
                            ub_ = mm_group(wu, wB, o * 128, hn, hnB, KC)
                            if rsF is None:
                                rsF = finF()
                            for s_ in range(2):
                                rs, rsB = rsF[s_]
                                t1, t1B = tmr.next()
                                sg, sgB = sgr.next()
                                t2, t2B = tmr.next()
                                k.op("dve", lambda s_=s_, t1=t1, rs=rs: nc.vector.tensor_tensor(
                                    out=t1[:], in0=gb_[s_][0][:, :], in1=rs[:], op=ALU.mult),
                                    reads=[gb_[s_][1], rsB], writes=[t1B])
                                k.op("act", lambda t1=t1, sg=sg: nc.scalar.activation(sg[:], t1[:], AF.Silu),
                                     reads=[t1B], writes=[sgB])
                                k.op("dve", lambda s_=s_, t2=t2, rs=rs: nc.vector.tensor_tensor(
                                    out=t2[:], in0=ub_[s_][0][:, :], in1=rs[:], op=ALU.mult),
                                    reads=[ub_[s_][1], rsB], writes=[t2B])
                                k.op("dve", lambda s_=s_, sg=sg, t2=t2: nc.vector.tensor_tensor(
                                    out=act[:, fl, s_ * 512:(s_ + 1) * 512], in0=sg[:], in1=t2[:], op=ALU.mult),
                                    reads=[sgB, t2B], writes=[RB[fl]])
                    for c in range(8):
                        wf, wB = ws.get()
                        wd = v3(wf, 0, nf, 256)
                        for o in range(2):
                            bb = mm_group(wd, wB, o * 128, act, RB[0:nf], nf)
                            add_into_h(2 * c + o, bb, hg_col=(32 if f0 > 0 else None))
                k.dma("sp", gfb, gfinb_d, writes=gfbB)
                for b in range(TT // 128):
                    pb, pbB = pblk.next()
                    k.dma("sp", pb[:], pc[t0 + b * 128: t0 + (b + 1) * 128, :], writes=[pbB])
                    bank, bB = psum.next()
                    for q in range(2):
                        k.op("pe", lambda q=q: nc.tensor.transpose(bank[:, q * 128:(q + 1) * 128], pb[:, q * 128:(q + 1) * 128], identf[:]),
                             reads=[pbB, CB], writes=[bB])
                    copy_op(evac_eng(), pTb[:, :, b * 128:(b + 1) * 128], bank[:, 0:256].rearrange("p (q n) -> p q n", q=2),
                            [bB], pTB)
                finP = norm_lite(hT, hB, 32, hn, hnB, TT, sqs2, rsr, skip_hg=True)
                rsP = None
                for a in range(8):
                    wf, wB = ws.get()
                    wv, wpp = v3(wf, 0, KC, 256), v3(wf, 4096, 2, 256)
                    for o in range(2):
                        oc = 2 * a + o
                        gb_ = mm_group(wv, wB, o * 128, hn, hnB, KC)
                        pb_ = mm_group(wpp, wB, o * 128, pTb, pTB, 2)
                        if rsP is None:
                            rsP = finP()
                        for s_ in range(2):
                            sg, sgB = sgr.next()
                            tm, tmB = tmr.next()
                            hv = hT[:, oc, s_ * 512:(s_ + 1) * 512]
                            rs, rsB = rsP[s_]
                            t1, t1B = t1r.next()
                            k.op("dve", lambda s_=s_, t1=t1, rs=rs: nc.vector.tensor_tensor(
                                out=t1[:], in0=gb_[s_][0][:, :], in1=rs[:], op=ALU.mult),
                                reads=[gb_[s_][1], rsB], writes=[t1B])
                            k.op("act", lambda t1=t1, sg=sg: nc.scalar.activation(sg[:], t1[:], AF.Sigmoid),
                                 reads=[t1B], writes=[sgB])
                            k.op("dve", lambda s_=s_, sg=sg, tm=tm: nc.vector.tensor_tensor(
                                out=tm[:], in0=sg[:], in1=pb_[s_][0][:, :], op=ALU.mult),
                                reads=[sgB, pb_[s_][1]], writes=[tmB])
                            k.op("dve", lambda hv=hv, tm=tm: nc.vector.tensor_tensor(out=hv, in0=tm[:], in1=hv, op=ALU.add),
                                 reads=[tmB], writes=[hB[oc]])
                            sq_fly(oc, s_)
                if tt + 1 < NT // TT:
                    k.dma("sp", hn[:], dview(mixT_d, t0 + TT, t0 + 2 * TT), writes=hnB)
                tbank, tB = psum.next()
                for b in range(TT // 128):
                    sqv, sqvB = sqs2[b // 4]
                    c0 = (b % 4) * 128
                    for kc in range(KC):
                        k.op("pe", lambda kc=kc, b=b, c0=c0, sqv=sqv: nc.tensor.matmul(
                            tbank[:, 2 * b:2 * b + 2], sqv[:, kc, c0:c0 + 128], ones[:, 0:2],
                            start=(kc == 0), stop=(kc == KC - 1)), reads=sqvB + [CB], writes=[tB])
                k.op("act", lambda: nc.scalar.activation(rscol[:], tbank[:, 0:16], AF.Sqrt, bias=epst[:], scale=1.0 / D),
                     reads=[tB, CB], writes=[rscolB])
                k.op("dve", lambda: nc.vector.reciprocal(rscol[:], rscol[:]), reads=[rscolB], writes=[rscolB])
                for b in range(TT // 128):
                    yb, ybB = yblk[b % 2]
                    for jq in range(4):
                        bank, bB = psum.next()
                        for q in range(4):
                            kc = 4 * jq + q
                            k.op("pe", lambda kc=kc, q=q: nc.tensor.transpose(
                                bank[:, q * 128:(q + 1) * 128], hT[:, kc, b * 128:(b + 1) * 128], identf[:]),
                                reads=[hB[kc], CB], writes=[bB])
                        k.op("dve", lambda jq=jq, b=b, bank=bank, yb=yb: nc.vector.scalar_tensor_tensor(
                            out=yb[:, jq * 512:(jq + 1) * 512], in0=bank[:, :], scalar=rscol[:, 2 * b:2 * b + 1],
                            in1=gfb[:, jq * 512:(jq + 1) * 512], op0=ALU.mult, op1=ALU.mult),
                            reads=[bB, rscolB] + gfbB, writes=ybB)
                    k.dma("sp", out_d[t0 + b * 128: t0 + (b + 1) * 128, :], yb, reads=ybB)
          k.barrier()

        for t in k.all_tokens():
            k._wait("sp", t)
        used = k.used
    return nc, used


def _bucket(n):
    n = np.asarray(n, dtype=np.int32)
    nf = np.maximum(n, 1).astype(np.float32)
    large = 16 + (np.log(nf / np.float32(16)) / np.float32(np.log(2048 / 16)) * np.float32(16)).astype(np.int32)
    large = np.minimum(large, 31)
    return np.where(n < 16, n, large)


def _host_inputs(inp):
    f = lambda a: np.ascontiguousarray(np.asarray(a, dtype=np.float32))
    x = f(inp["x"])
    p = f(inp["p"])[0]
    rel_bias = f(inp["rel_bias"])
    w_in = f(inp["w_in"])[0]
    cols = list(range(512))
    for j in range(4):
        for kind in range(3):
            for g in range(3):
                h = 4 * g + j
                c = 512 + kind * 1536 + h * 128
                cols.extend(range(c, c + 128))
    w_in_p = np.ascontiguousarray(w_in[:, cols])
    gv = np.concatenate([
        f(inp["norm_mix_g"])[0].reshape(16, 128).T, f(inp["norm_ffn_g"])[0].reshape(16, 128).T,
        f(inp["norm_ple_g"])[0].reshape(16, 128).T, f(inp["final_norm_g"]).reshape(16, 128).T,
        f(inp["pool_scale"])[0].reshape(4, 128).T], axis=1)
    gv = np.ascontiguousarray(gv)
    poolw = np.ascontiguousarray(f(inp["pool_w"])[0].transpose(1, 0, 2))
    ki = np.arange(256)[:, None]
    q = np.arange(128)[None, :]
    off = q + 128 - ki
    valid = (off >= 0) & (off <= 128)
    biasT = np.zeros((128, 4, 3, 256), np.float32)
    for g in range(3):
        bk = _bucket(np.maximum(off, 0) * GD[g])
        for j in range(4):
            tb = rel_bias[bk, 4 * g + j]
            biasT[:, j, g, 0:128] = tb[0:128]
            biasT[:, j, g, 128:256] = tb[128:256]
    maskc = np.where(valid, 0.0, NEG).astype(np.float32)
    maskc = np.ascontiguousarray(np.concatenate([maskc[0:128], maskc[128:256]], axis=1))
    common = {
        "w_in": w_in_p, "w_out": f(inp["w_out"])[0], "w_gate": f(inp["w_gate"])[0], "w_up": f(inp["w_up"])[0],
        "w_down": f(inp["w_down"])[0], "w_pg": f(inp["w_ple_gate"])[0], "w_pp": f(inp["w_ple_proj"])[0],
        "poolw_in": poolw, "gvec_in": gv, "biasT": biasT, "maskc_in": maskc, "identf_in": np.eye(128, dtype=np.float32),
        "gfinb_in": np.ascontiguousarray(np.broadcast_to(f(inp["final_norm_g"])[None, :], (128, D))),
    }
    maps = []
    for c in range(8):
        b, qd = c // 4, c % 4
        s = qd * NT
        xh = np.zeros((2 * NT, D), np.float32)
        if qd > 0:
            xh[0:NT] = x[b, s - NT:s]
        xh[NT:] = x[b, s:s + NT]
        hneg = np.full((128, 1), 0.0 if qd > 0 else NEG, np.float32)
        rc = np.zeros((128, 4, 16), np.float32)
        for gi, w in enumerate((2, 4, 8, 16)):
            pos = s + np.arange(16) + 1
            rc[:, gi, :] = (1.0 / np.minimum(pos, w)).astype(np.float32)[None, :]
        m = dict(common)
        m.update({"xh": xh, "pc": np.ascontiguousarray(p[b, s:s + NT]), "hneg_in": hneg, "rcnt_in": rc})
        maps.append(m)
    return maps


_NC_CACHE = {}


def _get_nc():
    if "nc" not in _NC_CACHE:
        _, used = build(None)
        nc, _ = build(used)
        _NC_CACHE["nc"] = nc
    return _NC_CACHE["nc"]


def kernel(**inputs):
    maps = _host_inputs(inputs)
    nc = _get_nc()
    res = run_bass_kernel_spmd(nc, maps, core_ids=list(range(8)))
    out = np.zeros((2, 4 * NT, D), np.float32)
    for c in range(8):
        b, qd = c // 4, c % 4
        out[b, qd * NT:(qd + 1) * NT] = np.asarray(res.results[c]["out"], dtype=np.float32)
    return out
```
